# Optimizing a Trainium2 kernel written in Bass

```python
import jax, jax.numpy as jnp
from jax import lax
import numpy as np

D_MODEL = 1024
BATCH = 4
SEQ = 4096
DEPTH = 1
DEC_BATCH = 8
DEC_SEQ = 64
PAST_LEN = 1024

CHUNK = 64
N_META = 16
D_RNN = 512
D_CONV = 512
N_RNN_HEADS = 8
RNN_HEAD_DIM = D_RNN // N_RNN_HEADS
RNN_CONV_W = 4
DW_CONV_W = 31
D_FF = 4 * D_MODEL
RGLRU_C = 8.0
EPS = 1e-6
D_IN = 2 * D_RNN + 2 * D_CONV

kernel_name = "hymba_griffin_conformer_stream_step"


def _rmsnorm(x, g):
    xf = x.astype(jnp.float32)
    y = xf * lax.rsqrt(jnp.mean(xf * xf, axis=-1, keepdims=True) + EPS)
    return (y * g.astype(jnp.float32)).astype(x.dtype)


def _layernorm(x, g, b):
    xf = x.astype(jnp.float32)
    mu = jnp.mean(xf, axis=-1, keepdims=True)
    xc = xf - mu
    y = xc * lax.rsqrt(jnp.mean(xc * xc, axis=-1, keepdims=True) + EPS)
    return (y * g.astype(jnp.float32) + b.astype(jnp.float32)).astype(x.dtype)


def _causal_dwconv(x, state, w, b):
    xpad = jnp.concatenate([state.astype(x.dtype), x], axis=1)
    out = lax.conv_general_dilated(
        xpad, w[:, None, :].astype(x.dtype), window_strides=(1,), padding="VALID",
        dimension_numbers=("NWC", "WIO", "NWC"), feature_group_count=x.shape[-1])
    return out + b.astype(x.dtype), xpad[:, -(w.shape[0] - 1):]


def _rglru(x, h0, w_r, b_r, w_i, b_i, lam):
    B, T, C = x.shape
    xf = x.astype(jnp.float32)
    xh = xf.reshape(B, T, N_RNN_HEADS, RNN_HEAD_DIM)
    r = jax.nn.sigmoid(jnp.einsum("bthd,hde->bthe", xh, w_r.astype(jnp.float32)).reshape(B, T, C)
                       + b_r.astype(jnp.float32))
    i = jax.nn.sigmoid(jnp.einsum("bthd,hde->bthe", xh, w_i.astype(jnp.float32)).reshape(B, T, C)
                       + b_i.astype(jnp.float32))
    log_a = -RGLRU_C * r * jax.nn.softplus(-lam.astype(jnp.float32))
    a = jnp.exp(log_a)
    mult = jnp.sqrt(jnp.maximum(-jnp.expm1(2.0 * log_a), 0.0))
    b_in = mult * i * xf

    def step(h, ab):
        a_t, b_t = ab
        h = a_t * h + b_t
        return h, h

    hT, hs = lax.scan(step, h0.astype(jnp.float32),
                      (jnp.swapaxes(a, 0, 1), jnp.swapaxes(b_in, 0, 1)))
    return jnp.swapaxes(hs, 0, 1).astype(x.dtype), hT.astype(h0.dtype)


def _layer(x, conv_state, h_state, dw_state, norm_mix, w_in, rnn_conv_w, rnn_conv_b,
           w_gate_r, b_gate_r, w_gate_i, b_gate_i, rglru_lambda, dw_w, dw_b, ln_conv_g,
           ln_conv_b, out_norm_rnn, out_norm_conv, w_out, norm_mlp, w_up, w_down):
    hn = _rmsnorm(x, norm_mix)
    z = hn @ w_in
    xr, gate, glu_v, glu_g = jnp.split(z, [D_RNN, 2 * D_RNN, 2 * D_RNN + D_CONV], axis=-1)
    xr_c, new_conv = _causal_dwconv(xr, conv_state, rnn_conv_w, rnn_conv_b)
    hr, new_h = _rglru(xr_c, h_state, w_gate_r, b_gate_r, w_gate_i, b_gate_i, rglru_lambda)
    y_rnn = hr * jax.nn.gelu(gate)
    v = glu_v * jax.nn.sigmoid(glu_g)
    vc, new_dw = _causal_dwconv(v, dw_state, dw_w, dw_b)
    y_conv = jax.nn.silu(_layernorm(vc, ln_conv_g, ln_conv_b))
    mix = jnp.concatenate([_rmsnorm(y_rnn, out_norm_rnn), _rmsnorm(y_conv, out_norm_conv)], axis=-1)
    x = x + mix @ w_out
    hm = _rmsnorm(x, norm_mlp)
    x = x + jnp.square(jax.nn.relu(hm @ w_up)) @ w_down
    return x, new_conv, new_h, new_dw


def setup_inputs(seed: int = 0) -> dict:
    key = jax.random.key(seed)
    ks = jax.random.split(key, 32)
    nrm = jax.random.normal
    f32 = jnp.float32
    u = jax.random.uniform(ks[10], (DEPTH, D_RNN), f32, 0.9, 0.999)
    a_base = u ** (1.0 / RGLRU_C)
    lam = jnp.log(a_base) - jnp.log1p(-a_base)
    return {
        "x_prompt": nrm(ks[0], (BATCH, SEQ, D_MODEL), f32),
        "x_sample": nrm(ks[1], (DEC_BATCH, DEC_SEQ, D_MODEL), f32),
        "state_rglru_conv": nrm(ks[2], (DEPTH, DEC_BATCH, RNN_CONV_W - 1, D_RNN), f32),
        "state_rglru_h": 0.5 * nrm(ks[3], (DEPTH, DEC_BATCH, D_RNN), f32),
        "state_dwconv": 0.5 * nrm(ks[4], (DEPTH, DEC_BATCH, DW_CONV_W - 1, D_CONV), f32),
        "meta_tokens": nrm(ks[5], (N_META, D_MODEL), f32),
        "norm_mix": 1.0 + 0.05 * nrm(ks[6], (DEPTH, D_MODEL), f32),
        "w_in": nrm(ks[7], (DEPTH, D_MODEL, D_IN), f32) * D_MODEL ** -0.5,
        "rnn_conv_w": nrm(ks[8], (DEPTH, RNN_CONV_W, D_RNN), f32) * RNN_CONV_W ** -0.5,
        "rnn_conv_b": 0.01 * nrm(ks[9], (DEPTH, D_RNN), f32),
        "w_gate_r": nrm(ks[11], (DEPTH, N_RNN_HEADS, RNN_HEAD_DIM, RNN_HEAD_DIM), f32) * RNN_HEAD_DIM ** -0.5,
        "b_gate_r": 0.01 * nrm(ks[12], (DEPTH, D_RNN), f32),
        "w_gate_i": nrm(ks[13], (DEPTH, N_RNN_HEADS, RNN_HEAD_DIM, RNN_HEAD_DIM), f32) * RNN_HEAD_DIM ** -0.5,
        "b_gate_i": 0.01 * nrm(ks[14], (DEPTH, D_RNN), f32),
        "rglru_lambda": lam,
        "dw_w": nrm(ks[15], (DEPTH, DW_CONV_W, D_CONV), f32) * DW_CONV_W ** -0.5,
        "dw_b": 0.01 * nrm(ks[16], (DEPTH, D_CONV), f32),
        "ln_conv_g": 1.0 + 0.05 * nrm(ks[17], (DEPTH, D_CONV), f32),
        "ln_conv_b": 0.01 * nrm(ks[18], (DEPTH, D_CONV), f32),
        "out_norm_rnn": 1.0 + 0.05 * nrm(ks[19], (DEPTH, D_RNN), f32),
        "out_norm_conv": 1.0 + 0.05 * nrm(ks[20], (DEPTH, D_CONV), f32),
        "w_out": nrm(ks[21], (DEPTH, D_RNN + D_CONV, D_MODEL), f32) * (D_RNN + D_CONV) ** -0.5,
        "norm_mlp": 1.0 + 0.05 * nrm(ks[22], (DEPTH, D_MODEL), f32),
        "w_up": nrm(ks[23], (DEPTH, D_MODEL, D_FF), f32) * D_MODEL ** -0.5,
        "w_down": nrm(ks[24], (DEPTH, D_FF, D_MODEL), f32) * D_FF ** -0.5,
        "norm_final": 1.0 + 0.05 * nrm(ks[25], (D_MODEL,), f32),
    }


def reference(x_prompt, x_sample, state_rglru_conv, state_rglru_h, state_dwconv, meta_tokens,
              norm_mix, w_in, rnn_conv_w, rnn_conv_b, w_gate_r, b_gate_r, w_gate_i, b_gate_i,
              rglru_lambda, dw_w, dw_b, ln_conv_g, ln_conv_b, out_norm_rnn, out_norm_conv,
              w_out, norm_mlp, w_up, w_down, norm_final):
    bp = x_prompt.shape[0]
    dt = x_prompt.dtype
    meta = jnp.broadcast_to(meta_tokens.astype(dt)[None], (bp, N_META, D_MODEL))
    xp = jnp.concatenate([meta, x_prompt], axis=1)
    xs = x_sample
    conv_p, h_p, dw_p, conv_s, h_s, dw_s = [], [], [], [], [], []
    for l in range(DEPTH):
        params = (norm_mix[l], w_in[l], rnn_conv_w[l], rnn_conv_b[l], w_gate_r[l], b_gate_r[l],
                  w_gate_i[l], b_gate_i[l], rglru_lambda[l], dw_w[l], dw_b[l], ln_conv_g[l],
                  ln_conv_b[l], out_norm_rnn[l], out_norm_conv[l], w_out[l], norm_mlp[l],
                  w_up[l], w_down[l])
        xp, c_p, hh_p, d_p = _layer(
            xp, jnp.zeros((bp, RNN_CONV_W - 1, D_RNN), dt), jnp.zeros((bp, D_RNN), dt),
            jnp.zeros((bp, DW_CONV_W - 1, D_CONV), dt), *params)
        xs, c_s, hh_s, d_s = _layer(xs, state_rglru_conv[l], state_rglru_h[l], state_dwconv[l], *params)
        conv_p.append(c_p); h_p.append(hh_p); dw_p.append(d_p)
        conv_s.append(c_s); h_s.append(hh_s); dw_s.append(d_s)
    y_prompt = _rmsnorm(xp, norm_final)[:, N_META:]
    y_sample = _rmsnorm(xs, norm_final)
    new_rglru_conv_prompt = jnp.stack(conv_p)
    new_rglru_h_prompt = jnp.stack(h_p)
    new_dwconv_prompt = jnp.stack(dw_p)
    new_rglru_conv_sample = jnp.stack(conv_s)
    new_rglru_h_sample = jnp.stack(h_s)
    new_dwconv_sample = jnp.stack(dw_s)
    return (y_prompt, y_sample, new_rglru_conv_prompt, new_rglru_h_prompt, new_dwconv_prompt,
            new_rglru_conv_sample, new_rglru_h_sample, new_dwconv_sample)
```

```python
import contextlib
import numpy as np
import concourse.bass as bass
import concourse.mybir as mybir
from concourse.bass_utils import run_bass_kernel_spmd

F32 = mybir.dt.float32
BF16 = mybir.dt.bfloat16
AF = mybir.ActivationFunctionType
ALU = mybir.AluOpType

D = 1024
DR = 512
DFF = 4096
NMETA = 16
SEQ = 4096
PRE = 2064
HALO = 48
MAIN = 2048
SMP = 64
C_PRE, C_HALO, C_MAIN, C_SMP = 0, PRE, PRE + HALO, PRE + HALO + MAIN
NTOK = PRE + HALO + MAIN + SMP
NOUT = MAIN + SMP
EPS = 1e-6

CV_GMIX, CV_GMLP, CV_GFIN = 0, 8, 16
CV_CB, CV_BR, CV_BI, CV_LAM, CV_DWB, CV_LNG, CV_LNB, CV_GR, CV_GC = 24, 28, 32, 36, 40, 44, 48, 52, 56
CV_RW, CV_DW = 60, 76
CV_FLAG, CV_H0, CV_EPS = 200, 201, 205
CV_C, CV_CH, CV_BRH, CV_BIH = 206, 210, 214, 218
CV_ONEP = 222
NV = 224


_SIM = {}
DEF_COST = {"pe": 2.0, "act": 0.6, "dve": 0.65, "pool": 0.9, "sp": 0.3}
HOP = 0.45
LIST_SCHED = True
PE_COLD = 1.0
FILL_WARM = True
FILL_LAST_SEG = 3
LNEXP_SQRT = True
PRENORM_BOOST = 1000.0
INPROJ_BOOST = 0.0
FILL_DUR = 0.25


def pe_c(nmm, n):
    return nmm * (0.045 + 0.205 * n / 512.0)


class _Rec:
    def __init__(self):
        self.calls = []

    def __getattr__(self, name):
        def f(*a, **k):
            self.calls.append((name, a, k))
            return self
        return f

    def then_inc(self, *a, **k):
        return self


def _fd(ap):
    try:
        return int(ap.free_size())
    except Exception:
        return 512


def _is_psum(ap):
    try:
        return "psum" in str(ap.space).lower() or "PSUM" in str(ap.space)
    except Exception:
        return False


_ASETS = {"Exp": frozenset([0, 6]), "Tanh": frozenset([0, 11, 18]), "Ln": frozenset([6]), "Sqrt": frozenset([3]),
          "Gelu_apprx_tanh": frozenset([11]), "Silu": frozenset([18])}
TBL_LOAD = 1.3
TBL_CHOICE = 1.0


def estimate_cost(eng, fn):
    r = _Rec()
    try:
        fn(r)
    except Exception:
        return None
    occ = 0.0
    lat_extra = 0.0
    aset = None
    for name, a, k in r.calls:
        out = k.get("out", a[0] if a else None)
        fd = _fd(out) if out is not None else 512
        if name == "matmul":
            rhs = k.get("rhs", a[2] if len(a) > 2 else None)
            nn = _fd(rhs) if rhs is not None else 512
            occ += max(0.055, 0.012 + 0.215 * nn / 512.0)
        elif name == "dma_start":
            try:
                nbytes = out.size() * (4 if out.dtype == F32 else 2)
            except Exception:
                nbytes = 1 << 20
            occ += 0.8 if eng == "pool" else 0.5
            lat_extra = 2.0 + nbytes / 200e3
        elif name == "activation":
            fname = str(k.get("func", "")).split(".")[-1]
            aset = _ASETS.get(fname)
            occ += 0.13 + 0.00082 * fd + (0.08 if not isinstance(k.get("scale", 1.0), (int, float)) else 0.0)
        elif eng == "dve":
            if name == "tensor_tensor_scan":
                occ += 0.2 + 0.0021 * fd
            elif name == "scalar_tensor_tensor":
                occ += 0.15 + 0.00105 * fd
            elif name == "tensor_tensor":
                occ += 0.1 + 0.001 * fd
            elif name == "tensor_scalar":
                occ += 0.12 + 0.00055 * fd
            elif name == "tensor_copy":
                src = k.get("in_", a[1] if len(a) > 1 else None)
                occ += 0.1 + (0.001 if (src is not None and _is_psum(src)) else 0.00065) * fd
            elif name == "reciprocal":
                occ += 0.1 + 0.0052 * fd
            else:
                occ += 0.06 + 0.0005 * fd
        elif eng == "pool":
            if name == "tensor_tensor":
                occ += 0.15 + 0.0017 * fd
            elif name == "tensor_scalar":
                occ += 0.45 + 0.0004 * fd
            elif name == "tensor_copy":
                occ += 0.12 + 0.0027 * fd
            else:
                occ += 0.06 + 0.0005 * fd
        else:
            occ += 0.3
    if occ == 0.0:
        return None
    return occ, (lat_extra if lat_extra else occ), aset


class VB:
    _n = 0

    def __init__(self, default_t):
        VB._n += 1
        self.id = VB._n
        self.key = f"VB#{self.id}"
        self.t = default_t

    def __getitem__(self, idx):
        return self.t[idx]

    def __getattr__(self, name):
        return getattr(self.t, name)


class Sched:
    def __init__(self, sem_names):
        self.sem_names = list(sem_names)
        self.ops = []
        self.events = []
        self.cnt = {k: 0 for k in sem_names}
        self.prog = {e: [] for e in ("pe", "act", "dve", "pool", "sp")}
        self.nbank = 0
        self.cur_boost = 0.0

    def op(self, eng, fn, reads=(), writes=(), sem=None, inc=None, c=None, lat=None, pin=False, boost=None):
        if sem is None:
            sem = eng
        if inc is None:
            inc = 16 if sem.startswith("d_") else 1
        if c is None:
            c = DEF_COST[eng]
        if lat is None:
            lat = 6.0 if sem.startswith("d_") else c
        self.ops.append(dict(eng=eng, fn=fn, reads=list(reads), writes=list(writes), sem=sem, inc=inc, c=c, lat=lat, pin=pin, boost=(self.cur_boost if boost is None else boost)))
        self.events.append(("op", len(self.ops) - 1))

    def fence(self, key):
        self.events.append(("fence", key))

    def settle(self, keys, sem):
        self.events.append(("settle", list(keys), sem))

    def finalize(self):
        ops = self.ops
        n = len(ops)
        for o in ops:
            est = estimate_cost(o["eng"], o["fn"])
            o["aset"] = None
            if est is not None:
                o["c"], o["lat"], o["aset"] = est
                if o["sem"].startswith("d_"):
                    o["lat"] = max(o["lat"], 2.0)
        if PE_COLD != 1.0:
            for ev in self.events:
                if ev[0] == "fence":
                    break
                if ev[0] == "op" and ops[ev[1]]["eng"] == "pe":
                    ops[ev[1]]["c"] *= PE_COLD
                    ops[ev[1]]["lat"] *= PE_COLD
        deps = [set() for _ in range(n)]
        last_w, readers, sem_ops = {}, {}, {}
        last_rec = {}
        pin_only = set()
        order = {e: [] for e in self.prog}
        finish = [0.0] * n
        eng_free = {e: 0.0 for e in self.prog}
        seg = []
        vb_alloc, vb_users = {}, {}
        for i, o in enumerate(ops):
            for k in set(o["reads"] + o["writes"]):
                if k.startswith("VB#"):
                    if k not in vb_alloc:
                        assert k in o["writes"], k
                        vb_alloc[k] = i
                        vb_users[k] = 0
                    else:
                        vb_users[k] += 1
        op_alloc = {i: k for k, i in vb_alloc.items()}
        phys = [dict(vb=None, ops=[], left=0) for _ in range(7)]
        vb_phys = {}
        seg_idx = [0]
        nfill = [0]
        self.fill_dep = None
        for i_, o_ in enumerate(ops):
            if "fill_src" in o_["writes"]:
                self.fill_dep = i_
        tbl = [None]
        nload = [0]

        def tbl_pen(o):
            a = o.get("aset")
            if a is None or tbl[0] is None:
                return 0.0 if a is None else TBL_LOAD
            return 0.0 if (a & tbl[0]) else TBL_LOAD

        def tbl_upd(o):
            a = o.get("aset")
            if a is None:
                return
            if tbl[0] is not None and (a & tbl[0]):
                tbl[0] = a & tbl[0]
            else:
                tbl[0] = a
                nload[0] += 1

        def sched_segment(seg):
            if not seg:
                return
            if not LIST_SCHED:
                for i in seg:
                    o = ops[i]
                    dr = max([finish[d] + HOP for d in deps[i]] + [0.0])
                    pen = tbl_pen(o) if o["eng"] == "act" else 0.0
                    st = max(eng_free[o["eng"]], dr) + pen
                    if o["eng"] == "act":
                        tbl_upd(o)
                    eng_free[o["eng"]] = st + o["c"]
                    finish[i] = st + o["lat"]
                    order[o["eng"]].append(i)
                return
            inseg = set(seg)
            succ = {i: [] for i in seg}
            indeg = {}
            for i in seg:
                k = 0
                for d in deps[i]:
                    if d in inseg:
                        succ[d].append(i)
                        k += 1
                indeg[i] = k
            prio = {}
            for i in reversed(seg):
                prio[i] = ops[i]["lat"] + max([prio[s_] + HOP for s_ in succ[i]] + [0.0])
            _SIM.setdefault("cp", []).append(round(max(prio.values()), 1))
            for i in seg:
                prio[i] += ops[i].get("boost", 0.0)
            dep_ready = {}
            ready = {e: [] for e in self.prog}

            def make_ready(i):
                dr = 0.0
                for d in deps[i]:
                    dr = max(dr, finish[d] + (HOP if ops[d]["eng"] != ops[i]["eng"] else 0.1))
                dep_ready[i] = dr
                ready[ops[i]["eng"]].append(i)
            for i in seg:
                if indeg[i] == 0:
                    make_ready(i)
            left = len(seg)
            while left:
                best = None
                for e, lst in ready.items():
                    if not lst:
                        continue
                    ef = eng_free[e]
                    for i in lst:
                        est = dep_ready[i] if dep_ready[i] > ef else ef
                        tp_ = 0.0
                        if e == "act":
                            tp_ = tbl_pen(ops[i])
                            est += tp_ * TBL_CHOICE
                        bsel = None
                        if i in op_alloc:
                            bt = None
                            for b, ph in enumerate(phys):
                                if ph["vb"] is None or ph["left"] == 0:
                                    t_ = max([finish[d] + HOP for d in ph["ops"]] + [0.0])
                                    if bt is None or t_ < bt:
                                        bt, bsel = t_, b
                            if bsel is None:
                                continue
                            if bt > est:
                                est = bt
                        key = (est, -prio[i], i)
                        if best is None or key < best[0]:
                            best = (key, e, i, bsel, est - tp_ * (TBL_CHOICE - 1.0))
                if best is None:
                    raise RuntimeError("scheduler: no PSUM bank available (deadlock)")
                key, e, i, bsel, st_real = best
                if bsel is not None:
                    ph = phys[bsel]
                    for d in ph["ops"]:
                        deps[i].add(d)
                    k_ = op_alloc[i]
                    self.vbs[k_].t = self.psb[bsel]
                    vb_phys[k_] = bsel
                    ph["vb"], ph["ops"], ph["left"] = k_, [i], vb_users[k_]
                for k_ in set(ops[i]["reads"] + ops[i]["writes"]):
                    if k_.startswith("VB#") and vb_alloc[k_] != i:
                        ph = phys[vb_phys[k_]]
                        ph["ops"].append(i)
                        ph["left"] -= 1
                ready[e].remove(i)
                o = ops[i]
                st = st_real
                if e == "pe" and FILL_WARM and seg_idx[0] <= FILL_LAST_SEG and self.fill_dep is not None and st - eng_free["pe"] > 2 * FILL_DUR and eng_free["pe"] > 0:
                    k_ = int((st - eng_free["pe"]) / FILL_DUR) - 1
                    t_ = eng_free["pe"]
                    for _ in range(k_):
                        ops.append(dict(eng="pe", fn=self.fill_fn, reads=[], writes=[], sem="pe", inc=1, c=FILL_DUR, lat=FILL_DUR, pin=False, aset=None))
                        deps.append(set([self.fill_dep]))
                        finish.append(t_ + FILL_DUR)
                        order["pe"].append(len(ops) - 1)
                        t_ += FILL_DUR
                        nfill[0] += 1
                if e == "act":
                    tbl_upd(o)
                eng_free[e] = st + o["c"]
                finish[i] = st + o["lat"]
                order[e].append(i)
                left -= 1
                for s_ in succ[i]:
                    indeg[s_] -= 1
                    if indeg[s_] == 0:
                        make_ready(s_)

        for ev in self.events:
            if ev[0] == "op":
                i = ev[1]
                o = ops[i]
                for k in o["reads"] + o["writes"]:
                    for d in last_w.get(k, ()):
                        deps[i].add(d)
                for k in o["writes"]:
                    for d in readers.get(k, ()):
                        deps[i].add(d)
                if o["pin"] and last_rec.get(o["eng"]) is not None:
                    if last_rec[o["eng"]] not in deps[i]:
                        pin_only.add((i, last_rec[o["eng"]]))
                    deps[i].add(last_rec[o["eng"]])
                last_rec[o["eng"]] = i
                deps[i].discard(i)
                for k in o["writes"]:
                    last_w[k] = [i]
                    readers[k] = []
                for k in o["reads"]:
                    readers.setdefault(k, []).append(i)
                sem_ops.setdefault(o["sem"], []).append(i)
                seg.append(i)
            elif ev[0] == "settle":
                for k in ev[1]:
                    last_w[k] = list(sem_ops.get(ev[2], []))
            elif ev[0] == "fence":
                sched_segment(seg)
                seg_idx[0] += 1
                _SIM.setdefault("segs", []).append((ev[1], max([finish[i] for i in seg] + [0.0]), {e: round(sum(ops[i]["c"] for i in seg if ops[i]["eng"] == e), 1) for e in self.prog}))
                seg = []
                last_w[ev[1]] = [order[e][-1] for e in ("pe", "act", "dve", "pool") if order[e]]
                readers[ev[1]] = []
        sched_segment(seg)
        self.sim_time = max(finish) if finish else 0.0
        _SIM["nload"] = nload[0]
        self.sim_finish = finish
        self.sim_order = order
        self.sim_deps = deps
        _SIM["S"] = self
        n = len(ops)
        tok = [None] * n
        cnt = {k: 0 for k in self.sem_names}
        _SIM["nfill"] = nfill[0]
        for e in self.prog:
            for i in order[e]:
                o = ops[i]
                cnt[o["sem"]] += o["inc"]
                tok[i] = (o["sem"], cnt[o["sem"]])
        self.cnt = cnt
        for e in self.prog:
            waited = {}
            for i in order[e]:
                o = ops[i]
                w = {}
                for d in deps[i]:
                    if (i, d) in pin_only:
                        continue
                    s_, v = tok[d]
                    if waited.get(s_, 0) < v:
                        w[s_] = max(w.get(s_, 0), v)
                for s_, v in w.items():
                    waited[s_] = v
                self.prog[e].append((list(w.items()), o["fn"], o["sem"], o["inc"]))

    def emit(self, block, sems, final_waits=()):
        self.finalize()
        _SIM["t"] = self.sim_time

        def run(engname, e):
            for (waits, fn, sem, inc) in self.prog[engname]:
                for (s, v) in waits:
                    e.wait_ge(sems[s], v)
                fn(e).then_inc(sems[sem], inc)

        @block.tensor
        def _(e):
            run("pe", e)

        @block.scalar
        def _(e):
            run("act", e)

        @block.vector
        def _(e):
            run("dve", e)

        @block.gpsimd
        def _(e):
            run("pool", e)

        @block.sync
        def _(e):
            run("sp", e)
            for s in final_waits:
                if self.cnt[s] > 0:
                    e.wait_ge(sems[s], self.cnt[s])


def build_program():
    nc = bass.Bass("TRN2", target_bir_lowering=False)
    dram = lambda name, shape, dt, kind="Internal": nc.dram_tensor(name, shape, dt, kind=kind).ap()
    xT = dram("xT", [D, NTOK], F32, "ExternalInput")
    cvec_d = dram("cvec", [128, NV], F32, "ExternalInput")
    ident_d = dram("ident", [128, 128], F32, "ExternalInput")
    wg_d = dram("wg", [128, 8 * 128], F32, "ExternalInput")
    xrst_d = dram("xrst", [128, 12], F32, "ExternalInput")
    dwst_d = dram("dwst", [128, 120], F32, "ExternalInput")
    wina_d = dram("w_in_a", [128, 8 * 512], F32, "ExternalInput")
    winb_d = dram("w_in_b", [128, 8 * 1536], F32, "ExternalInput")
    wout_d = dram("w_out", [128, 8 * 1024], F32, "ExternalInput")
    wup_d = dram("w_up", [8, 128, 8 * 512], F32, "ExternalInput")
    wdn_d = dram("w_dn", [8, 128, 32 * 128], F32, "ExternalInput")
    yT = dram("yT", [D, NOUT], F32, "ExternalOutput")
    st_d = dram("st", [128, 2 * 4 * 34], F32, "ExternalOutput")
    x1s = dram("x1s", [D, NOUT], F32)
    wup_b = dram("wup_b", [8, 128, 8 * 512], BF16)
    wdn_b = dram("wdn_b", [8, 128, 32 * 128], BF16)
    xT_v = xT.rearrange("(k p) t -> p k t", p=128)
    yT_v = yT.rearrange("(k p) t -> p k t", p=128)
    x1_v = x1s.rearrange("(k p) t -> p k t", p=128)

    with contextlib.ExitStack() as es:
        def sb(name, shape, dt):
            return es.enter_context(nc.sbuf_tensor(name, shape, dt))

        cvec = sb("cvec_s", [128, NV], F32)
        ident = sb("ident_s", [128, 128], F32)
        onesM = sb("onesM", [128, 128], BF16)
        onesC = sb("onesC", [128, 128], BF16)
        Dr = sb("Dr", [128, 16, 128], BF16)
        Ddw = sb("Ddw", [128, 124, 128], BF16)
        Wg = sb("Wg", [128, 8, 128], BF16)
        xrb = sb("xrb", [128, 4, 3 + 512], BF16)
        vb = sb("vb", [128, 4, 30 + 512], BF16)
        hcar = sb("hcar", [128, 4], F32)
        stt = sb("stt", [128, 2, 4, 34], F32)
        xrst = sb("xrst_s", [128, 4, 3], F32)
        dwst = sb("dwst_s", [128, 4, 30], F32)
        tiny = sb("tiny", [128, 8], F32)
        fill_src = sb("fill_src", [128, 512], BF16)
        xb = [sb(f"xb{i}", [128, 8, 512], F32) for i in range(2)]
        hn = sb("hn", [128, 8, 512], BF16)
        hn2 = sb("hn2", [128, 8, 512], BF16)
        rs = sb("rs", [128, 512], F32)
        R1_BYTES = 110592
        r1 = sb("r1", [128, R1_BYTES // 2], BF16)
        cur = [0]

        def carve(shape, dt, reset=None):
            if reset is not None:
                cur[0] = reset
            n = int(np.prod(shape))
            nb = n * (4 if dt == F32 else 2)
            off = cur[0]
            cur[0] += nb
            assert cur[0] <= R1_BYTES, cur[0]
            ap = r1[:, off // 2:(off + nb) // 2]
            if dt == F32:
                ap = ap.bitcast(F32)
            if len(shape) == 2:
                ap = ap.rearrange("p (a b) -> p a b", a=shape[0])
            elif len(shape) == 3:
                ap = ap.rearrange("p (a b c) -> p a b c", a=shape[0], b=shape[1])
            return ap

        def view(off, shape, dt):
            n = int(np.prod(shape))
            nb = n * (4 if dt == F32 else 2)
            assert off + nb <= R1_BYTES, (off, nb)
            ap = r1[:, off // 2:(off + nb) // 2]
            if dt == F32:
                ap = ap.bitcast(F32)
            if len(shape) == 2:
                ap = ap.rearrange("p (a b) -> p a b", a=shape[0])
            return ap

        w_in = view(0, [8, 2048], BF16)
        w_out = view(32768, [8, 1024], BF16)
        TP0 = 49152
        tp = [view(TP0 + 2048 * i, [512], F32) for i in range(20)]
        gg = view(TP0 + 2048 * 12, [4, 512], F32)
        vc = view(TP0 + 2048 * 16, [4, 512], F32)
        xcb = view(90112, [4, 512], BF16)
        ys = view(94208, [8, 512], BF16)
        mix = view(102400, [8, 512], BF16)
        NWU, NWD = 4, 3
        wu = [view(8192 * i, [8, 512], BF16) for i in range(NWU)]
        hT = view(32768, [32, 512], BF16)
        wd = [view(65536 + 8192 * i, [32, 128], BF16) for i in range(NWD)]
        t_relu = view(90112, [3, 512], F32)

        psb = [es.enter_context(nc.psum_tensor(f"ps{i}", [128, 512], F32)) for i in range(8)]

        sem_names = (["pe", "act", "dve", "pool", "sp", "d_c", "d_cast", "d_castb", "d_casto", "d_cast2", "d_x0", "d_x1", "d_s0", "d_s1", "d_o"]
                     + [f"d_wu{i}" for i in range(NWU)] + [f"d_wd{i}" for i in range(NWD)])
        sems = {n: es.enter_context(nc.semaphore(n)) for n in sem_names}
        block = es.enter_context(nc.Block())
        S = Sched(sem_names)
        S.psb = psb
        S.vbs = {}
        S.fill_fn = lambda e: e.matmul(psb[7][:, :], lhsT=onesM[:], rhs=fill_src[:], start=True, stop=True)

        def bank(pool="s"):
            vb = VB(psb[0])
            S.vbs[vb.key] = vb
            return vb, vb.key

        def cv(c, n=1):
            return cvec[:, c:c + n]

        def drain(gen):
            for _ in gen:
                pass

        def merge(primary, secondary, ratio=2):
            p_alive = s_alive = True
            while p_alive or s_alive:
                for _ in range(ratio):
                    if p_alive:
                        try:
                            next(primary)
                        except StopIteration:
                            p_alive = False
                if s_alive:
                    try:
                        next(secondary)
                    except StopIteration:
                        s_alive = False

        S.op("sp", lambda e: e.dma_start(out=cvec[:], in_=cvec_d), writes=["cvec"], sem="d_c", boost=5000.0)
        S.op("sp", lambda e: e.dma_start(out=ident[:], in_=ident_d), writes=["ident"], sem="d_c", boost=5000.0)
        S.op("sp", lambda e: e.dma_start(out=xrst[:].rearrange("p a b -> p (a b)"), in_=xrst_d), writes=["xrst"], sem="d_c", boost=5000.0)
        S.op("sp", lambda e: e.dma_start(out=dwst[:].rearrange("p a b -> p (a b)"), in_=dwst_d), writes=["dwst"], sem="d_c", boost=5000.0)
        S.settle(["cvec", "ident", "xrst", "dwst"], "d_c")
        S.op("pool", lambda e: e.dma_start(out=Wg[:].rearrange("p a b -> p (a b)"), in_=wg_d, max_dma_last_dim=4096), writes=["Wg"], sem="d_cast")
        S.op("pool", lambda e: e.dma_start(out=w_in[:, :, 0:512], in_=wina_d.rearrange("p (k c) -> p k c", k=8), max_dma_last_dim=4096),
             writes=["w_in_a"], sem="d_cast")
        S.settle(["Wg", "w_in_a"], "d_cast")
        def cast_wout():
            for h in range(4):
                S.op("pool", lambda e, h=h: e.dma_start(out=w_in[:, 2 * h:2 * h + 2, 512:2048],
                                                        in_=winb_d.rearrange("p (k c) -> p k c", k=8)[:, 2 * h:2 * h + 2, :], max_dma_last_dim=8192),
                     writes=[f"w_in_b.{h}"], sem="d_castb", pin=True, c=0.8)
            S.settle(["w_in_b"], "d_castb")
            for h in range(2):
                S.op("pool", lambda e, h=h: e.dma_start(out=w_out[:, 4 * h:4 * h + 4, :].rearrange("p a b -> p (a b)"),
                                                        in_=wout_d[:, h * 4096:(h + 1) * 4096], max_dma_last_dim=4096),
                     writes=[f"w_out.{h}"], sem="d_casto", pin=True, c=0.8)
            S.settle(["w_out"], "d_casto")

        d2d_list = [("u", g) for g in range(8)] + [("d", g) for g in range(8)]

        def cast_d2d(k):
            for _ in range(k):
                if not d2d_list:
                    return
                kind, g = d2d_list.pop(0)
                if kind == "u":
                    S.op("pool", lambda e, g=g: e.dma_start(out=wup_b[g], in_=wup_d[g], max_dma_last_dim=4096), sem="d_cast2", pin=True, c=0.8)
                else:
                    S.op("pool", lambda e, g=g: e.dma_start(out=wdn_b[g], in_=wdn_d[g], max_dma_last_dim=4096), sem="d_cast2", pin=True, c=0.8)

        ddw_next = [0]

        def build_ddw(k):
            return

        S.op("dve", lambda e: e.memset(onesM[:], 1.0 / D), writes=["onesM"])
        S.op("dve", lambda e: e.memset(fill_src[:], 0.5), writes=["fill_src"])
        S.op("dve", lambda e: e.memset(onesC[:], 1.0 / DR), writes=["onesC"])
        S.op("dve", lambda e: e.memset(xrb[:], 0.0), writes=[f"xrb.{j}" for j in range(4)])
        S.op("dve", lambda e: e.memset(vb[:], 0.0), writes=[f"vb.{j}" for j in range(4)])
        S.op("dve", lambda e: e.memset(hcar[:], 0.0), writes=["hcar"])
        S.op("dve", lambda e: e.memset(stt[:], 0.0), writes=["stt"])
        S.op("act", lambda e: e.activation(out=tiny[:, 0:4], in_=cv(CV_LAM, 4), func=AF.Exp, scale=-1.0), reads=["cvec"], writes=["tiny"])
        S.op("act", lambda e: e.activation(out=tiny[:, 4:8], in_=tiny[:, 0:4], func=AF.Ln, bias=1.0), reads=["tiny"], writes=["tiny2"])
        S.op("dve", lambda e: e.tensor_scalar(out=cv(CV_C, 4), in0=tiny[:, 4:8], scalar1=-8.0, scalar2=None, op0=ALU.mult), reads=["tiny2"], writes=["cvc"])
        S.op("dve", lambda e: e.tensor_scalar(out=cv(CV_CH, 4), in0=tiny[:, 4:8], scalar1=-4.0, scalar2=None, op0=ALU.mult), reads=["tiny2"], writes=["cvc"])
        S.op("dve", lambda e: e.tensor_scalar(out=cv(CV_BRH, 8), in0=cv(CV_BR, 8), scalar1=0.5, scalar2=None, op0=ALU.mult), reads=["cvec"], writes=["cvc"])
        S.op("dve", lambda e: e.tensor_tensor(
                 out=Dr[:, :, :],
                 in0=ident[:].unsqueeze(1).broadcast_to([128, 16, 128]),
                 in1=cvec[:, CV_RW:CV_RW + 16].unsqueeze(2).broadcast_to([128, 16, 128]),
                 op=ALU.mult),
             reads=["ident", "cvec"], writes=["Dr"])
        for j in range(4):
            S.op("dve", lambda e, j=j: e.scalar_tensor_tensor(
                     out=Ddw[:, j * 31:(j + 1) * 31, :],
                     in0=ident[:].unsqueeze(1).broadcast_to([128, 31, 128]), scalar=0.5,
                     in1=cvec[:, CV_DW + j * 31:CV_DW + (j + 1) * 31].unsqueeze(2).broadcast_to([128, 31, 128]),
                     op0=ALU.mult, op1=ALU.mult),
                 reads=["ident", "cvec"], writes=[f"Ddw.{j}"])

        def g_load(src_v, c0, n, slot, extra_reads=()):
            S.op("sp", lambda e: e.dma_start(out=xb[slot][:, :, :n], in_=src_v[:, :, c0:c0 + n]),
                 reads=list(extra_reads), writes=[f"xb{slot}"], sem=f"d_x{slot}", boost=PRENORM_BOOST)
            yield

        def g_rstd(ps, pk, n, out_t, out_k, boost=0.0):
            S.op("act", lambda e: e.activation(out=out_t[:, :n], in_=ps[:, :n], func=AF.Ln, bias=cv(CV_EPS)), reads=[pk, "cvec"], writes=[out_k], boost=boost)
            yield
            S.op("act", lambda e: e.activation(out=out_t[:, :n], in_=out_t[:, :n], func=AF.Exp, scale=-0.5), reads=[out_k], writes=[out_k], boost=boost)
            yield

        HBUF = [hn, hn2]
        HKEY = [[f"hn.{k}" for k in range(8)], [f"hn2.{k}" for k in range(8)]]
        HN = HKEY[0]

        def g_prenorm(slot, n, gcol, sqbuf, sqkeys, extra=(), hb=0):
            hn = HBUF[hb]
            if sqbuf is None:
                sqbuf, sqkeys = hn, HKEY[hb]
            x = xb[slot]
            xk = f"xb{slot}"
            S.op("act", lambda e: e.activation(out=sqbuf[:, :, :n], in_=x[:, :, :n], func=AF.Square), reads=[xk] + list(extra), writes=sqkeys, c=0.2 + 0.0008 * 8 * n, boost=PRENORM_BOOST)
            yield
            ps, pk = bank()

            def mm(e):
                for k in range(8):
                    ins = e.matmul(ps[:, :n], lhsT=onesM[:], rhs=sqbuf[:, k, :n], start=(k == 0), stop=(k == 7))
                return ins
            S.op("pe", mm, reads=sqkeys + ["onesM"], writes=[pk], c=pe_c(8, n), boost=PRENORM_BOOST)
            yield
            yield from g_rstd(ps, pk, n, rs, "rs", boost=PRENORM_BOOST)
            for k in range(8):
                S.op("dve", lambda e, k=k: e.scalar_tensor_tensor(out=hn[:, k, :n], in0=x[:, k, :n], scalar=cv(gcol + k), in1=rs[:, :n],
                                                                   op0=ALU.mult, op1=ALU.mult),
                     reads=[xk, "rs", "cvec"], writes=[HKEY[hb][k]], boost=PRENORM_BOOST)
                yield

        def inproj(m, n, pool="s", hb=0):
            hn = HBUF[hb]
            HN = HKEY[hb]
            ps, pk = bank(pool)

            def mm(e):
                for k in range(8):
                    ins = e.matmul(ps[:, :n], lhsT=w_in[:, k, m * 128:(m + 1) * 128], rhs=hn[:, k, :n], start=(k == 0), stop=(k == 7))
                return ins
            S.op("pe", mm, reads=HN + ["w_in_a" if m < 4 else "w_in_b"], writes=[pk], c=pe_c(8, n), boost=INPROJ_BOOST)
            return ps, pk

        def g_xr(n, seg, p_mode=False, hb=0):
            for j in range(4):
                ps, pk = inproj(j, n, hb=hb)
                yield
                if p_mode:
                    S.op("dve", lambda e, j=j, ps=ps: e.tensor_copy(out=xrb[:, j, 3:3 + n], in_=ps[:, :n]), reads=[pk], writes=[f"xrb.{j}"])
                else:
                    S.op("act", lambda e, j=j, ps=ps: e.activation(out=xrb[:, j, 3:3 + n], in_=ps[:, :n], func=AF.Copy), reads=[pk], writes=[f"xrb.{j}"])
                yield
                if seg is not None:
                    S.op("act", lambda e, j=j, ps=ps: e.activation(out=stt[:, seg, j, 0:3], in_=ps[:, n - 3:n], func=AF.Copy), reads=[pk], writes=["stt"])
                    yield

        def g_xr_issue(n, store, hb=0):
            for j in range(4):
                store[j] = inproj(j, n, pool="l", hb=hb)
                yield

        def g_xr_evac(n, store):
            for j in range(4):
                ps, pk = store[j]
                S.op("dve", lambda e, j=j, ps=ps: e.tensor_copy(out=xrb[:, j, 3:3 + n], in_=ps[:, :n]), reads=[pk], writes=[f"xrb.{j}"])
                yield

        def g_glu(n, seg, hb=0):
            for j in range(4):
                pg, pgk = inproj(12 + j, n, hb=hb)
                yield
                tg = tp[10 + j % 2]
                tgk = f"tp{10 + j % 2}"
                S.op("act", lambda e, pg=pg, tg=tg: e.activation(out=tg[:, :n], in_=pg[:, :n], func=AF.Tanh, scale=0.5), reads=[pgk], writes=[tgk])
                yield
                pv, pvk = inproj(8 + j, n, hb=hb)
                yield
                S.op("dve", lambda e, pv=pv, tg=tg, j=j: e.scalar_tensor_tensor(out=vb[:, j, 30:30 + n], in0=tg[:, :n], scalar=1.0, in1=pv[:, :n],
                                                                               op0=ALU.add, op1=ALU.mult),
                     reads=[pvk, tgk], writes=[f"vb.{j}"])
                yield
                if seg is not None:
                    S.op("dve", lambda e, tg=tg: e.tensor_scalar(out=tg[:, n - 30:n], in0=tg[:, n - 30:n], scalar1=0.5, scalar2=0.5, op0=ALU.mult, op1=ALU.add),
                         reads=[tgk], writes=[tgk])
                    S.op("dve", lambda e, pv=pv, tg=tg, j=j: e.tensor_tensor(out=stt[:, seg, j, 4:34], in0=tg[:, n - 30:n], in1=pv[:, n - 30:n], op=ALU.mult),
                         reads=[pvk, tgk], writes=["stt"])
                    yield

        def g_gate(n, hb=0):
            for j in range(4):
                pgt, pgtk = inproj(4 + j, n, hb=hb)
                yield
                S.op("act", lambda e, pgt=pgt, j=j: e.activation(out=gg[:, j, :n], in_=pgt[:, :n], func=AF.Gelu_apprx_tanh), reads=[pgtk, "PA"], writes=[f"gg.{j}"])
                yield

        def g_gate_issue(n, store, hb=0):
            for j in range(4):
                store[j] = inproj(4 + j, n, pool="l", hb=hb)
                yield

        def gate_evac(n, store):
            for j in range(4):
                pgt, pgtk = store[j]
                S.op("act", lambda e, pgt=pgt, j=j: e.activation(out=gg[:, j, :n], in_=pgt[:, :n], func=AF.Gelu_apprx_tanh), reads=[pgtk, "PA"], writes=[f"gg.{j}"])

        def g_chains(js, sets, n, seg=None, want_y=False, after_gates=None, after_conv=None, p_mode=False):
            J = list(zip(js, sets))
            T = lambda s, i: tp[5 * s + i]
            K = lambda s, i: f"tp{5 * s + i}"
            bk = {}
            for j, s in J:
                ps, pk = bank()
                bk[j] = (ps, pk)

                def mmc(e, j=j, ps=ps):
                    for k in range(4):
                        ins = e.matmul(ps[:, :n], lhsT=Dr[:, j * 4 + k, :], rhs=xrb[:, j, k:k + n], start=(k == 0), stop=(k == 3))
                    return ins
                S.op("pe", mmc, reads=[f"xrb.{j}", "Dr"], writes=[pk], c=pe_c(4, n))
                yield
            if after_conv is not None:
                after_conv()
            for j, s in J:
                ps, pk = bk[j]
                S.op("act", lambda e, j=j, s=s, ps=ps: e.activation(out=T(s, 0)[:, :n], in_=ps[:, :n], func=AF.Identity, bias=cv(CV_CB + j)),
                     reads=[pk, "cvec"], writes=[K(s, 0)])
                yield
                S.op("pool", lambda e, j=j: e.tensor_copy(out=xrb[:, j, 0:3], in_=xrb[:, j, n:n + 3]), reads=[f"xrb.{j}"], writes=[f"xrb.{j}"])
                yield
            for j, s in J:
                S.op("dve", lambda e, s=s: e.tensor_copy(out=xcb[:, s, :n], in_=T(s, 0)[:, :n]), reads=[K(s, 0)], writes=[f"xcb{s}"])
                yield
            gb = {}
            for j, s in J:
                pr, prk = bank()
                S.op("pe", lambda e, j=j, s=s, pr=pr: e.matmul(pr[:, :n], lhsT=Wg[:, j, :], rhs=xcb[:, s, :n], start=True, stop=True), reads=[f"xcb{s}", "Wg"], writes=[prk], c=pe_c(1, n))
                yield
                pi, pik = bank()
                S.op("pe", lambda e, j=j, s=s, pi=pi: e.matmul(pi[:, :n], lhsT=Wg[:, 4 + j, :], rhs=xcb[:, s, :n], start=True, stop=True), reads=[f"xcb{s}", "Wg"], writes=[pik], c=pe_c(1, n))
                yield
                gb[j] = (pr, prk, pi, pik)
                S.op("act", lambda e, j=j, s=s, pr=pr: e.activation(out=T(s, 1)[:, :n], in_=pr[:, :n], func=AF.Tanh, scale=0.5, bias=cv(CV_BRH + j)),
                     reads=[prk, "cvc"], writes=[K(s, 1)])
                yield
                S.op("act", lambda e, j=j, s=s, pi=pi: e.activation(out=T(s, 2)[:, :n], in_=pi[:, :n], func=AF.Tanh, scale=0.5, bias=cv(CV_BIH + j)),
                     reads=[pik, "cvc"], writes=[K(s, 2)])
                yield
            if after_gates is not None:
                after_gates()
            for j, s in J:
                if not p_mode:
                    S.op("act", lambda e, j=j, s=s: e.activation(out=T(s, 3)[:, :n], in_=T(s, 1)[:, :n], func=AF.Exp, scale=cv(CV_C + j), bias=cv(CV_C + j)),
                         reads=[K(s, 1), "cvc"], writes=[K(s, 3)])
                    yield
                S.op("act", lambda e, j=j, s=s: e.activation(out=T(s, 1)[:, :n], in_=T(s, 1)[:, :n], func=AF.Exp, scale=cv(CV_CH + j), bias=cv(CV_CH + j)),
                     reads=[K(s, 1), "cvc"], writes=[K(s, 1)])
                yield
                if p_mode:
                    S.op("dve", lambda e, s=s: e.tensor_tensor(out=T(s, 3)[:, :n], in0=T(s, 1)[:, :n], in1=T(s, 1)[:, :n], op=ALU.mult),
                         reads=[K(s, 1)], writes=[K(s, 3)])
                    yield
            for j, s in J:
                S.op("dve", lambda e, s=s: e.scalar_tensor_tensor(out=T(s, 2)[:, :n], in0=T(s, 2)[:, :n], scalar=1.0, in1=T(s, 0)[:, :n], op0=ALU.add, op1=ALU.mult),
                     reads=[K(s, 2), K(s, 0)], writes=[K(s, 2)])
                yield
            for j, s in J:
                if LNEXP_SQRT:
                    S.op("act", lambda e, s=s: e.activation(out=T(s, 3)[:, :n], in_=T(s, 3)[:, :n], func=AF.Ln, scale=-1.0, bias=1.0), reads=[K(s, 3)], writes=[K(s, 3)])
                    yield
                    S.op("act", lambda e, s=s: e.activation(out=T(s, 3)[:, :n], in_=T(s, 3)[:, :n], func=AF.Exp, scale=0.5), reads=[K(s, 3)], writes=[K(s, 3)])
                else:
                    S.op("act", lambda e, s=s: e.activation(out=T(s, 3)[:, :n], in_=T(s, 3)[:, :n], func=AF.Sqrt, scale=-1.0, bias=1.0), reads=[K(s, 3)], writes=[K(s, 3)])
                yield
            for j, s in J:
                S.op("dve", lambda e, s=s: e.scalar_tensor_tensor(out=T(s, 2)[:, :n], in0=T(s, 2)[:, :n], scalar=0.5, in1=T(s, 3)[:, :n], op0=ALU.mult, op1=ALU.mult),
                     reads=[K(s, 2), K(s, 3)], writes=[K(s, 2)])
                yield
                S.op("dve", lambda e, j=j, s=s: e.tensor_tensor_scan(out=T(s, 4)[:, :n], data0=T(s, 1)[:, :n], data1=T(s, 2)[:, :n], initial=hcar[:, j:j + 1],
                                                                     op0=ALU.mult, op1=ALU.add),
                     reads=[K(s, 1), K(s, 2), "hcar"], writes=[K(s, 4)], c=1.3)
                yield
                S.op("dve", lambda e, j=j, s=s: e.tensor_copy(out=hcar[:, j:j + 1], in_=T(s, 4)[:, n - 1:n]), reads=[K(s, 4)], writes=["hcar"])
                yield
                if seg is not None:
                    S.op("pool", lambda e, j=j, s=s: e.tensor_copy(out=stt[:, seg, j, 3:4], in_=T(s, 4)[:, n - 1:n]), reads=[K(s, 4)], writes=["stt"])
                if want_y:
                    S.op("dve", lambda e, j=j, s=s: e.tensor_tensor(out=gg[:, j, :n], in0=gg[:, j, :n], in1=T(s, 4)[:, :n], op=ALU.mult),
                         reads=[f"gg.{j}", K(s, 4)], writes=[f"gg.{j}"])
                    yield

        pre_chunks = [(0, 512), (512, 512), (1024, 512), (1536, 512), (2048, 16)]
        xslot = [0]

        def nslot():
            s = xslot[0]
            xslot[0] ^= 1
            return s

        def g_PHa(ci):
            c0, n = pre_chunks[ci]
            slot = nslot()
            yield from g_load(xT_v, C_PRE + c0, n, slot)
            yield from g_prenorm(slot, n, CV_GMIX, None, None, hb=ci % 2)

        xr_store = {}
        drain(g_PHa(0))
        drain(g_xr_issue(pre_chunks[0][1], xr_store, hb=0))
        for ci, (c0, n) in enumerate(pre_chunks):
            drain(g_xr_evac(n, xr_store))
            if ci == 0:
                cast_wout()
            if ci == len(pre_chunks) - 1:
                S.op("dve", lambda e: e.tensor_scalar(out=hcar[:], in0=hcar[:], scalar1=cv(CV_FLAG), scalar2=None, op0=ALU.mult),
                     reads=["hcar", "cvec"], writes=["hcar"])
            ch = g_chains([0, 1, 2, 3], [0, 1, 2, 3], n, p_mode=True)
            if ci + 1 < len(pre_chunks):
                xr_store = {}

                def sec(ci=ci, st=xr_store):
                    yield from g_PHa(ci + 1)
                    yield from g_xr_issue(pre_chunks[ci + 1][1], st, hb=(ci + 1) % 2)
                merge(ch, sec(), ratio=3)
            else:
                drain(ch)
            cast_d2d(2)
            build_ddw(31)
        S.fence("PA")

        B_chunks = [(0, 424), (424, 424), (848, 424), (1272, 420), (1692, 420)]
        B_slot = {}
        wu_i = [0]
        wd_i = [0]
        HT = [f"hT.{f}" for f in range(32)]

        def BHB(ci):
            return (ci + len(A_chunks)) % 2

        def g_BH(ci, first=False):
            oc0, n = B_chunks[ci]
            slot = nslot()
            B_slot[ci] = slot
            yield from g_load(x1_v, oc0, n, slot, extra_reads=["x1s"])
            if first:
                yield from g_prenorm(slot, n, CV_GMLP, None, None, hb=BHB(ci))
            else:
                yield from g_prenorm(slot, n, CV_GMLP, mix, ["mixB"], extra=["R1"], hb=BHB(ci))

        def load_wu(g):
            ws = wu_i[0] % NWU
            wu_i[0] += 1
            S.op("sp", lambda e: e.dma_start(out=wu[ws][:].rearrange("p a b -> p (a b)"), in_=wup_b[g]),
                 reads=["wup_b", "R1a"], writes=[f"wu{ws}"], sem=f"d_wu{ws}")
            return ws

        def load_wd(m):
            ws = wd_i[0] % NWD
            wd_i[0] += 1
            S.op("sp", lambda e: e.dma_start(out=wd[ws][:].rearrange("p a b -> p (a b)"), in_=wdn_b[m]),
                 reads=["wdn_b", "R1"], writes=[f"wd{ws}"], sem=f"d_wd{ws}")
            return ws

        def _one_up(ci, g, ws):
            oc0, n = B_chunks[ci]
            for mi in range(4):
                pu, puk = bank("all")

                hnb = HBUF[BHB(ci)]

                def mmu(e, mi=mi, pu=pu, hnb=hnb):
                    for k in range(8):
                        ins = e.matmul(pu[:, :n], lhsT=wu[ws][:, k, mi * 128:(mi + 1) * 128], rhs=hnb[:, k, :n], start=(k == 0), stop=(k == 7))
                    return ins
                S.op("pe", mmu, reads=HKEY[BHB(ci)] + [f"wu{ws}"], writes=[puk], c=pe_c(8, n))
                f = g * 4 + mi
                tb = f % 3
                S.op("act", lambda e, pu=pu, tb=tb: e.activation(out=t_relu[:, tb, :n], in_=pu[:, :n], func=AF.Relu), reads=[puk, "R1"], writes=[f"t_relu{tb}"])
                eng = "dve" if f % 4 != 3 else "pool"
                S.op(eng, lambda e, f=f, tb=tb: e.tensor_tensor(out=hT[:, f, :n], in0=t_relu[:, tb, :n], in1=t_relu[:, tb, :n], op=ALU.mult),
                     reads=[f"t_relu{tb}", "R1"], writes=[f"hT.{f}"])
                yield

        def _one_down(ci, m, ws):
            oc0, n = B_chunks[ci]
            slot = B_slot[ci]
            x = xb[slot]
            xk = f"xb{slot}"
            pd, pdk = bank("all")

            def mmd2(e):
                for k in range(32):
                    ins = e.matmul(pd[:, :n], lhsT=wd[ws][:, k, :], rhs=hT[:, k, :n], start=(k == 0), stop=(k == 31))
                return ins
            S.op("pe", mmd2, reads=HT + [f"wd{ws}"], writes=[pdk], c=pe_c(32, n))
            S.op("dve", lambda e: e.tensor_tensor(out=x[:, m, :n], in0=pd[:, :n], in1=x[:, m, :n], op=ALU.add),
                 reads=[pdk, xk], writes=[xk])
            yield

        def g_Bfinal(ci):
            oc0, n = B_chunks[ci]
            slot = B_slot[ci]
            x = xb[slot]
            xk = f"xb{slot}"
            S.op("act", lambda e: e.activation(out=mix[:, :, :n], in_=x[:, :, :n], func=AF.Square), reads=[xk, "R1"], writes=["mixB"], c=0.2 + 0.0008 * 8 * n)
            yield
            ps, pk = bank("all")

            def mmf(e):
                for k in range(8):
                    ins = e.matmul(ps[:, :n], lhsT=onesM[:], rhs=mix[:, k, :n], start=(k == 0), stop=(k == 7))
                return ins
            S.op("pe", mmf, reads=["mixB", "onesM"], writes=[pk], c=pe_c(8, n))
            yield
            yield from g_rstd(ps, pk, n, rs, "rs")
            for k in range(8):
                S.op("dve", lambda e, k=k: e.scalar_tensor_tensor(out=x[:, k, :n], in0=x[:, k, :n], scalar=cv(CV_GFIN + k), in1=rs[:, :n],
                                                                   op0=ALU.mult, op1=ALU.mult),
                     reads=[xk, "rs", "cvec"], writes=[xk])
                yield
            S.op("sp", lambda e: e.dma_start(out=yT_v[:, :, oc0:oc0 + n], in_=x[:, :, :n]), reads=[xk], sem=f"d_s{slot}")
            yield

        nB = len(B_chunks)
        wu_uses = [(c, g) for c in range(nB) for g in range(8)]
        wd_uses = [(c, m) for c in range(nB) for m in range(8)]
        wu_slot, wd_slot = {}, {}
        wu_ptr, wd_ptr = [0], [0]

        def ensure_wu(idx):
            while wu_ptr[0] <= min(idx, len(wu_uses) - 1):
                c, g = wu_uses[wu_ptr[0]]
                wu_slot[(c, g)] = load_wu(g)
                wu_ptr[0] += 1

        def ensure_wd(idx):
            while wd_ptr[0] <= min(idx, len(wd_uses) - 1):
                c, m = wd_uses[wd_ptr[0]]
                wd_slot[(c, m)] = load_wd(m)
                wd_ptr[0] += 1

        def step(gen, k):
            for _ in range(k):
                try:
                    next(gen)
                except StopIteration:
                    return

        A_chunks = [(C_SMP, SMP, "sample", 1, MAIN), (C_HALO, HALO, "halo", None, None)] \
            + [(C_MAIN + 512 * c, 512, "main", 0 if c == 3 else None, 512 * c) for c in range(4)]
        A_slot = {}

        def g_AH1a(ci):
            c0, n, mode, seg, oc0 = A_chunks[ci]
            if ci == 2:
                slot = A_slot[1]
            else:
                slot = nslot()
            A_slot[ci] = slot
            yield from g_load(xT_v, c0, n, slot)
            yield from g_prenorm(slot, n, CV_GMIX, None, None, hb=ci % 2)

        def g_AH1b(ci):
            c0, n, mode, seg, oc0 = A_chunks[ci]
            if mode == "sample":
                S.op("dve", lambda e: e.tensor_copy(out=tiny[:, 0:4], in_=hcar[:]), reads=["hcar"], writes=["tiny"])
                S.op("dve", lambda e: e.tensor_copy(out=xrb[:, :, 0:3], in_=xrst[:]), reads=["xrst"], writes=[f"xrb.{j}" for j in range(4)])
                S.op("dve", lambda e: e.tensor_scalar(out=vb[:, :, 0:30], in0=dwst[:], scalar1=2.0, scalar2=None, op0=ALU.mult), reads=["dwst"], writes=[f"vb.{j}" for j in range(4)])
                S.op("dve", lambda e: e.tensor_copy(out=hcar[:], in_=cv(CV_H0, 4)), reads=["cvec"], writes=["hcar"])
            yield from g_xr(n, seg, hb=ci % 2)
            yield from g_glu(n, seg, hb=ci % 2)

        def g_c31(j, n, store):
            pc, pck = bank("l")
            store[j] = (pc, pck)

            def mmd(e):
                for k in range(31):
                    ins = e.matmul(pc[:, :n], lhsT=Ddw[:, j * 31 + k, :], rhs=vb[:, j, k:k + n], start=(k == 0), stop=(k == 30))
                return ins
            S.op("pe", mmd, reads=[f"vb.{j}", f"Ddw.{j}"], writes=[pck], c=pe_c(31, n))
            S.op("pool", lambda e: e.tensor_copy(out=vb[:, j, 0:30], in_=vb[:, j, n:n + 30]), reads=[f"vb.{j}"], writes=[f"vb.{j}"])

        def A_M(ci, c31):
            c0, n, mode, seg, oc0 = A_chunks[ci]
            drain(g_chains([0, 1], [0, 1], n, seg=seg, want_y=True, after_conv=lambda: g_c31(0, n, c31), after_gates=lambda: g_c31(1, n, c31)))
            drain(g_chains([2, 3], [0, 1], n, seg=seg, want_y=True, after_conv=lambda: g_c31(2, n, c31), after_gates=lambda: g_c31(3, n, c31)))

        YS = [f"ys.{k}" for k in range(8)]

        def g_AT1(ci, c31):
            c0, n, mode, seg, oc0 = A_chunks[ci]
            mean_t, m2_t, rc_t, rr_t, rk_t = tp[0], tp[1], tp[2], tp[3], tp[4]
            for j in range(4):
                pc, pck = c31[j]
                S.op("act", lambda e, j=j, pc=pc: e.activation(out=vc[:, j, :n], in_=pc[:, :n], func=AF.Identity, bias=cv(CV_DWB + j)),
                     reads=[pck, "cvec", "PA"], writes=[f"vc.{j}"])
                yield
                S.op("act", lambda e, j=j, pc=pc: e.activation(out=ys[:, 4 + j, :n], in_=pc[:, :n], func=AF.Square, bias=cv(CV_DWB + j)),
                     reads=[pck, "cvec", "ys"], writes=[f"ys.{4 + j}"])
                yield
                S.op("dve", lambda e, j=j: e.tensor_copy(out=ys[:, j, :n], in_=vc[:, j, :n]), reads=[f"vc.{j}", "ys"], writes=[f"ys.{j}"])
                yield
            pm, pmk = bank()

            def mm_mean(e):
                for j in range(4):
                    ins = e.matmul(pm[:, :n], lhsT=onesC[:], rhs=ys[:, j, :n], start=(j == 0), stop=(j == 3))
                return ins
            S.op("pe", mm_mean, reads=YS[0:4] + ["onesC"], writes=[pmk], c=pe_c(4, n))
            yield
            pq, pqk = bank()

            def mm_msq(e):
                for j in range(4):
                    ins = e.matmul(pq[:, :n], lhsT=onesC[:], rhs=ys[:, 4 + j, :n], start=(j == 0), stop=(j == 3))
                return ins
            S.op("pe", mm_msq, reads=YS[4:8] + ["onesC"], writes=[pqk], c=pe_c(4, n))
            yield
            S.op("act", lambda e: e.activation(out=mean_t[:, :n], in_=pm[:, :n], func=AF.Copy), reads=[pmk], writes=["tp0"])
            yield
            S.op("pool", lambda e: e.tensor_tensor(out=m2_t[:, :n], in0=mean_t[:, :n], in1=mean_t[:, :n], op=ALU.mult), reads=["tp0"], writes=["tp1"])
            yield
            S.op("dve", lambda e: e.tensor_tensor(out=m2_t[:, :n], in0=pq[:, :n], in1=m2_t[:, :n], op=ALU.subtract), reads=[pqk, "tp1"], writes=["tp1"])
            yield
            S.op("dve", lambda e: e.tensor_scalar(out=m2_t[:, :n], in0=m2_t[:, :n], scalar1=0.0, scalar2=None, op0=ALU.max), reads=["tp1"], writes=["tp1"])
            yield
            S.op("act", lambda e: e.activation(out=rc_t[:, :n], in_=m2_t[:, :n], func=AF.Ln, bias=cv(CV_EPS)), reads=["tp1", "cvec"], writes=["tp2"])
            yield
            S.op("act", lambda e: e.activation(out=rc_t[:, :n], in_=rc_t[:, :n], func=AF.Exp, scale=-0.5), reads=["tp2"], writes=["tp2"])
            yield
            for j in range(4):
                S.op("pool", lambda e, j=j: e.tensor_tensor(out=vc[:, j, :n], in0=vc[:, j, :n], in1=mean_t[:, :n], op=ALU.subtract),
                     reads=[f"vc.{j}", "tp0"], writes=[f"vc.{j}"])
                yield
                S.op("dve", lambda e, j=j: e.tensor_tensor(out=vc[:, j, :n], in0=vc[:, j, :n], in1=rc_t[:, :n], op=ALU.mult),
                     reads=[f"vc.{j}", "tp2"], writes=[f"vc.{j}"])
                yield
            for j in range(4):
                S.op("act", lambda e, j=j: e.activation(out=vc[:, j, :n], in_=vc[:, j, :n], func=AF.Silu, scale=cv(CV_LNG + j), bias=cv(CV_LNB + j)),
                     reads=[f"vc.{j}", "cvec"], writes=[f"vc.{j}"])
                yield
            S.op("act", lambda e: e.activation(out=ys[:, 0:4, :n], in_=gg[:, :, :n], func=AF.Square), reads=[f"gg.{j}" for j in range(4)] + ["ys"], writes=YS[0:4], c=0.2 + 0.0008 * 4 * n)
            yield
            S.op("act", lambda e: e.activation(out=ys[:, 4:8, :n], in_=vc[:, :, :n], func=AF.Square), reads=[f"vc.{j}" for j in range(4)] + ["ys"], writes=YS[4:8], c=0.2 + 0.0008 * 4 * n)
            yield
            pr_, prk_ = bank()

            def mm_sr(e):
                for j in range(4):
                    ins = e.matmul(pr_[:, :n], lhsT=onesC[:], rhs=ys[:, j, :n], start=(j == 0), stop=(j == 3))
                return ins
            S.op("pe", mm_sr, reads=YS[0:4] + ["onesC"], writes=[prk_], c=pe_c(4, n))
            yield
            pc_, pck_ = bank()

            def mm_sc(e):
                for j in range(4):
                    ins = e.matmul(pc_[:, :n], lhsT=onesC[:], rhs=ys[:, 4 + j, :n], start=(j == 0), stop=(j == 3))
                return ins
            S.op("pe", mm_sc, reads=YS[4:8] + ["onesC"], writes=[pck_], c=pe_c(4, n))
            yield
            S.op("act", lambda e: e.activation(out=rr_t[:, :n], in_=pr_[:, :n], func=AF.Ln, bias=cv(CV_EPS)), reads=[prk_, "cvec"], writes=["tp3"])
            yield
            S.op("act", lambda e: e.activation(out=rk_t[:, :n], in_=pc_[:, :n], func=AF.Ln, bias=cv(CV_EPS)), reads=[pck_, "cvec"], writes=["tp4"])
            yield
            S.op("act", lambda e: e.activation(out=rr_t[:, :n], in_=rr_t[:, :n], func=AF.Exp, scale=-0.5), reads=["tp3"], writes=["tp3"])
            yield
            S.op("act", lambda e: e.activation(out=rk_t[:, :n], in_=rk_t[:, :n], func=AF.Exp, scale=-0.5), reads=["tp4"], writes=["tp4"])
            yield
            for j in range(4):
                S.op("dve", lambda e, j=j: e.scalar_tensor_tensor(out=mix[:, j, :n], in0=gg[:, j, :n], scalar=cv(CV_GR + j), in1=rr_t[:, :n], op0=ALU.mult, op1=ALU.mult),
                     reads=[f"gg.{j}", "tp3", "cvec"], writes=["mix"])
                yield
            for j in range(4):
                S.op("dve", lambda e, j=j: e.scalar_tensor_tensor(out=mix[:, 4 + j, :n], in0=vc[:, j, :n], scalar=cv(CV_GC + j), in1=rk_t[:, :n], op0=ALU.mult, op1=ALU.mult),
                     reads=[f"vc.{j}", "tp4", "cvec"], writes=["mix"])
                yield

        def g_AT2(ci):
            c0, n, mode, seg, oc0 = A_chunks[ci]
            slot = A_slot[ci]
            x = xb[slot]
            xk = f"xb{slot}"
            for m in range(8):
                po, pok = bank()

                def mmo(e, m=m, po=po):
                    for k in range(8):
                        ins = e.matmul(po[:, :n], lhsT=w_out[:, k, m * 128:(m + 1) * 128], rhs=mix[:, k, :n], start=(k == 0), stop=(k == 7))
                    return ins
                S.op("pe", mmo, reads=["mix", "w_out"], writes=[pok], c=pe_c(8, n))
                yield
                S.op("dve", lambda e, m=m, po=po: e.tensor_tensor(out=x[:, m, :n], in0=po[:, :n], in1=x[:, m, :n], op=ALU.add),
                     reads=[pok, xk], writes=[xk])
                yield
            S.op("sp", lambda e: e.dma_start(out=x1_v[:, :, oc0:oc0 + n], in_=x[:, :, :n]), reads=[xk], writes=["x1s"], sem=f"d_s{slot}")
            yield

        def halo_tails(n):
            for j in range(4):
                S.op("pool", lambda e, j=j: e.tensor_copy(out=xrb[:, j, 0:3], in_=xrb[:, j, n:n + 3]), reads=[f"xrb.{j}"], writes=[f"xrb.{j}"])
                S.op("pool", lambda e, j=j: e.tensor_copy(out=vb[:, j, 0:30], in_=vb[:, j, n:n + 30]), reads=[f"vb.{j}"], writes=[f"vb.{j}"])

        nA = len(A_chunks)
        drain(g_AH1a(0))
        drain(g_AH1b(0))
        drain(g_gate(A_chunks[0][1], hb=0))
        for ci in [0, 2, 3, 4, 5]:
            n = A_chunks[ci][1]
            c31 = {}
            A_M(ci, c31)
            if ci == 0:
                S.op("dve", lambda e: e.tensor_copy(out=hcar[:], in_=tiny[:, 0:4]), reads=["tiny", "hcar"], writes=["hcar"])
            cast_d2d(2)
            t1 = g_AT1(ci, c31)
            nxt = 2 if ci == 0 else ci + 1
            if nxt < nA:
                gstore = {}

                def head(ci=ci, nxt=nxt, gstore=gstore):
                    if ci == 0:
                        yield from g_AH1a(1)
                        yield from g_AH1b(1)
                        halo_tails(HALO)
                        yield
                    yield from g_AH1a(nxt)
                    yield from g_AH1b(nxt)
                    yield from g_gate_issue(A_chunks[nxt][1], gstore, hb=nxt % 2)
                merge(t1, head(), ratio=1)
                gate_evac(A_chunks[nxt][1], gstore)
                if nxt == nA - 1:
                    S.fence("R1a")
                    cast_d2d(16)
                    S.settle(["wup_b", "wdn_b"], "d_cast2")
                    ensure_wu(NWU - 1)
                drain(g_AT2(ci))
            else:
                b0 = g_BH(0, first=True)
                merge(t1, b0, ratio=2)
                drain(g_AT2(ci))
        S.op("sp", lambda e: e.dma_start(out=st_d, in_=stt[:].rearrange("p a b c -> p (a b c)")), reads=["stt"], sem="d_o")
        cast_d2d(16)
        S.settle(["wup_b", "wdn_b"], "d_cast2")

        S.fence("R1")
        for ci in range(nB):
            oc0, n = B_chunks[ci]
            ensure_wd(8 * ci + NWD - 1)
            nxt = g_BH(ci + 1) if ci + 1 < nB else iter(())
            for g in range(8):
                ensure_wu(8 * ci + g + NWU - 1)
                drain(_one_up(ci, g, wu_slot[(ci, g)]))
                if g == 1:
                    step(nxt, 2)
                if g == 4:
                    step(nxt, 3)
            drain(nxt)
            for m in range(8):
                ensure_wd(8 * ci + m + NWD - 1)
                if m == 2:
                    ensure_wu(8 * (ci + 1) + NWU - 1)
                drain(_one_down(ci, m, wd_slot[(ci, m)]))
            drain(g_Bfinal(ci))

        S.emit(block, sems, final_waits=["d_s0", "d_s1", "d_o"])
    return nc


_NC_CACHE = {}


def _host_inputs(inp):
    f32 = np.float32
    g = lambda k: np.asarray(inp[k], dtype=f32)
    x_prompt, x_sample, meta = g("x_prompt"), g("x_sample"), g("meta_tokens")
    fm8 = lambda v: np.ascontiguousarray(v.reshape(8, 128).T)
    fm4 = lambda v: np.ascontiguousarray(v.reshape(4, 128).T)
    cv = np.zeros((128, NV), f32)
    cv[:, CV_GMIX:CV_GMIX + 8] = fm8(g("norm_mix")[0])
    cv[:, CV_GMLP:CV_GMLP + 8] = fm8(g("norm_mlp")[0])
    cv[:, CV_GFIN:CV_GFIN + 8] = fm8(g("norm_final"))
    for col, key in ((CV_CB, "rnn_conv_b"), (CV_BR, "b_gate_r"), (CV_BI, "b_gate_i"), (CV_LAM, "rglru_lambda"), (CV_DWB, "dw_b"),
                     (CV_LNG, "ln_conv_g"), (CV_LNB, "ln_conv_b"), (CV_GR, "out_norm_rnn"), (CV_GC, "out_norm_conv")):
        cv[:, col:col + 4] = fm4(g(key)[0])
    rw = g("rnn_conv_w")[0]
    dw = g("dw_w")[0]
    for j in range(4):
        cv[:, CV_RW + 4 * j:CV_RW + 4 * j + 4] = rw[:, j * 128:(j + 1) * 128].T
        cv[:, CV_DW + 31 * j:CV_DW + 31 * j + 31] = dw[:, j * 128:(j + 1) * 128].T
    cv[:, CV_EPS] = EPS
    cv[:, CV_ONEP] = np.float32(1.0) + np.float32(1.1920929e-07)
    ident = np.eye(128, dtype=f32)
    wg = np.zeros((128, 8, 128), f32)
    for gi, key in enumerate(("w_gate_r", "w_gate_i")):
        w = g(key)[0]
        for j in range(4):
            for a in range(2):
                wg[64 * a:64 * a + 64, gi * 4 + j, 64 * a:64 * a + 64] = w[2 * j + a]
    wg = wg.reshape(128, 8 * 128)
    blk = lambda w, K, N: np.ascontiguousarray(w.reshape(K // 128, 128, N).transpose(1, 0, 2))
    w_in3 = blk(g("w_in")[0], D, 2048)
    w_in_a = np.ascontiguousarray(w_in3[:, :, 0:512]).reshape(128, -1)
    w_in_b = np.ascontiguousarray(w_in3[:, :, 512:2048]).reshape(128, -1)
    w_out = blk(g("w_out")[0], D, D).reshape(128, -1)
    wu = blk(g("w_up")[0], D, DFF)
    w_up = np.ascontiguousarray(wu.reshape(128, 8, 8, 512).transpose(2, 0, 1, 3)).reshape(8, 128, 8 * 512)
    wd = blk(g("w_down")[0], DFF, D)
    w_dn = np.ascontiguousarray(wd.reshape(128, 32, 8, 128).transpose(2, 0, 1, 3)).reshape(8, 128, 32 * 128)
    common = dict(ident=ident, wg=wg, w_in_a=w_in_a, w_in_b=w_in_b, w_out=w_out, w_up=w_up, w_dn=w_dn)
    st_conv, st_h, st_dw = g("state_rglru_conv")[0], g("state_rglru_h")[0], g("state_dwconv")[0]
    maps = []
    for c in range(8):
        j, p = divmod(c, 2)
        seq = np.concatenate([meta, x_prompt[j]], axis=0)
        xs = np.zeros((NTOK, D), f32)
        if p == 0:
            xs[C_PRE + 2048:C_PRE + 2064] = seq[0:16]
            xs[C_HALO + 32:C_HALO + 48] = seq[0:16]
            xs[C_MAIN:C_MAIN + MAIN] = seq[16:16 + MAIN]
        else:
            xs[C_PRE:C_PRE + PRE] = seq[0:PRE]
            xs[C_HALO:C_HALO + HALO] = seq[PRE - HALO:PRE]
            xs[C_MAIN:C_MAIN + MAIN] = seq[PRE:PRE + MAIN]
        xs[C_SMP:C_SMP + SMP] = x_sample[c]
        cvc = cv.copy()
        cvc[:, CV_FLAG] = float(p)
        cvc[:, CV_H0:CV_H0 + 4] = fm4(st_h[c])
        m = dict(common)
        m["xT"] = np.ascontiguousarray(xs.T)
        m["cvec"] = cvc
        m["xrst"] = np.ascontiguousarray(st_conv[c].T.reshape(4, 128, 3).transpose(1, 0, 2)).reshape(128, 12)
        m["dwst"] = np.ascontiguousarray(st_dw[c].T.reshape(4, 128, 30).transpose(1, 0, 2)).reshape(128, 120)
        maps.append(m)
    return maps


def kernel(**inputs):
    if "nc" not in _NC_CACHE:
        _NC_CACHE["nc"] = build_program()
    nc = _NC_CACHE["nc"]
    maps = _host_inputs(inputs)
    res = run_bass_kernel_spmd(nc, maps, core_ids=list(range(8)))
    outs = res.results
    f32 = np.float32
    y_prompt = np.zeros((4, SEQ, D), f32)
    y_sample = np.zeros((8, SMP, D), f32)
    conv_p = np.zeros((1, 4, 3, DR), f32)
    h_p = np.zeros((1, 4, DR), f32)
    dw_p = np.zeros((1, 4, 30, DR), f32)
    conv_s = np.zeros((1, 8, 3, DR), f32)
    h_s = np.zeros((1, 8, DR), f32)
    dw_s = np.zeros((1, 8, 30, DR), f32)
    for c in range(8):
        j, p = divmod(c, 2)
        yT = np.asarray(outs[c]["yT"], dtype=f32)
        y_prompt[j, p * MAIN:(p + 1) * MAIN] = yT[:, :MAIN].T
        y_sample[c] = yT[:, MAIN:].T
        st = np.asarray(outs[c]["st"], dtype=f32).reshape(128, 2, 4, 34)
        fm = lambda a: a.transpose(2, 1, 0).reshape(a.shape[2], 512)
        if p == 1:
            conv_p[0, j] = fm(st[:, 0, :, 0:3])
            h_p[0, j] = fm(st[:, 0, :, 3:4])[0]
            dw_p[0, j] = fm(st[:, 0, :, 4:34])
        conv_s[0, c] = fm(st[:, 1, :, 0:3])
        h_s[0, c] = fm(st[:, 1, :, 3:4])[0]
        dw_s[0, c] = fm(st[:, 1, :, 4:34])
    return (y_prompt, y_sample, conv_p, h_p, dw_p, conv_s, h_s, dw_s)
```

```python
import contextlib
import numpy as np
import concourse.bass as bass
import concourse.mybir as mybir
from concourse.bass_utils import run_bass_kernel_spmd

F32 = mybir.dt.float32
BF16 = mybir.dt.bfloat16
AF = mybir.ActivationFunctionType
ALU = mybir.AluOpType

D = 1024
DR = 512
DFF = 4096
NMETA = 16
SEQ = 4096
PRE = 2064
HALO = 48
MAIN = 2048
SMP = 64
C_PRE, C_HALO, C_MAIN, C_SMP = 0, PRE, PRE + HALO, PRE + HALO + MAIN
NTOK = PRE + HALO + MAIN + SMP
NOUT = MAIN + SMP
EPS = 1e-6

CV_GMIX, CV_GMLP, CV_GFIN = 0, 8, 16
CV_CB, CV_BR, CV_BI, CV_LAM, CV_DWB, CV_LNG, CV_LNB, CV_GR, CV_GC = 24, 28, 32, 36, 40, 44, 48, 52, 56
CV_RW, CV_DW = 60, 76
CV_FLAG, CV_H0, CV_EPS = 200, 201, 205
CV_C, CV_CH, CV_BRH, CV_BIH = 206, 210, 214, 218
CV_ONEP = 222
NV = 224


_SIM = {}
DEF_COST = {"pe": 2.0, "act": 0.6, "dve": 0.65, "pool": 0.9, "sp": 0.3}
HOP = 0.45
LIST_SCHED = True
PE_COLD = 1.0
FILL_WARM = True
FILL_LAST_SEG = 2
LNEXP_SQRT = True
PRENORM_BOOST = 1000.0
INPROJ_BOOST = 0.0
FILL_DUR = 0.25


def pe_c(nmm, n):
    return nmm * (0.045 + 0.205 * n / 512.0)


class _Rec:
    def __init__(self):
        self.calls = []

    def __getattr__(self, name):
        def f(*a, **k):
            self.calls.append((name, a, k))
            return self
        return f

    def then_inc(self, *a, **k):
        return self


def _fd(ap):
    try:
        return int(ap.free_size())
    except Exception:
        return 512


def _is_psum(ap):
    try:
        return "psum" in str(ap.space).lower() or "PSUM" in str(ap.space)
    except Exception:
        return False


_ASETS = {"Exp": frozenset([0, 6]), "Tanh": frozenset([0, 11, 18]), "Ln": frozenset([6]), "Sqrt": frozenset([3]),
          "Gelu_apprx_tanh": frozenset([11]), "Silu": frozenset([18])}
TBL_LOAD = 1.3
TBL_CHOICE = 1.0


def estimate_cost(eng, fn):
    r = _Rec()
    try:
        fn(r)
    except Exception:
        return None
    occ = 0.0
    lat_extra = 0.0
    aset = None
    for name, a, k in r.calls:
        out = k.get("out", a[0] if a else None)
        fd = _fd(out) if out is not None else 512
        if name == "matmul":
            rhs = k.get("rhs", a[2] if len(a) > 2 else None)
            nn = _fd(rhs) if rhs is not None else 512
            occ += max(0.055, 0.012 + 0.215 * nn / 512.0)
        elif name == "dma_start":
            try:
                nbytes = out.size() * (4 if out.dtype == F32 else 2)
            except Exception:
                nbytes = 1 << 20
            occ += 0.8 if eng == "pool" else 0.5
            lat_extra = 2.0 + nbytes / 200e3
        elif name == "activation":
            fname = str(k.get("func", "")).split(".")[-1]
            aset = _ASETS.get(fname)
            occ += 0.13 + 0.00082 * fd + (0.08 if not isinstance(k.get("scale", 1.0), (int, float)) else 0.0)
        elif eng == "dve":
            if name == "tensor_tensor_scan":
                occ += 0.2 + 0.0021 * fd
            elif name == "scalar_tensor_tensor":
                occ += 0.15 + 0.00105 * fd
            elif name == "tensor_tensor":
                occ += 0.1 + 0.001 * fd
            elif name == "tensor_scalar":
                occ += 0.12 + 0.00055 * fd
            elif name == "tensor_copy":
                src = k.get("in_", a[1] if len(a) > 1 else None)
                occ += 0.1 + (0.001 if (src is not None and _is_psum(src)) else 0.00065) * fd
            elif name == "reciprocal":
                occ += 0.1 + 0.0052 * fd
            else:
                occ += 0.06 + 0.0005 * fd
        elif eng == "pool":
            if name == "tensor_tensor":
                occ += 0.15 + 0.0017 * fd
            elif name == "tensor_scalar":
                occ += 0.45 + 0.0004 * fd
            elif name == "tensor_copy":
                occ += 0.12 + 0.0027 * fd
            else:
                occ += 0.06 + 0.0005 * fd
        else:
            occ += 0.3
    if occ == 0.0:
        return None
    return occ, (lat_extra if lat_extra else occ), aset


class VB:
    _n = 0

    def __init__(self, default_t):
        VB._n += 1
        self.id = VB._n
        self.key = f"VB#{self.id}"
        self.t = default_t

    def __getitem__(self, idx):
        return self.t[idx]

    def __getattr__(self, name):
        return getattr(self.t, name)


class Sched:
    def __init__(self, sem_names):
        self.sem_names = list(sem_names)
        self.ops = []
        self.events = []
        self.cnt = {k: 0 for k in sem_names}
        self.prog = {e: [] for e in ("pe", "act", "dve", "pool", "sp")}
        self.nbank = 0
        self.cur_boost = 0.0

    def op(self, eng, fn, reads=(), writes=(), sem=None, inc=None, c=None, lat=None, pin=False, boost=None):
        if sem is None:
            sem = eng
        if inc is None:
            inc = 16 if sem.startswith("d_") else 1
        if c is None:
            c = DEF_COST[eng]
        if lat is None:
            lat = 6.0 if sem.startswith("d_") else c
        self.ops.append(dict(eng=eng, fn=fn, reads=list(reads), writes=list(writes), sem=sem, inc=inc, c=c, lat=lat, pin=pin, boost=(self.cur_boost if boost is None else boost)))
        self.events.append(("op", len(self.ops) - 1))

    def fence(self, key):
        self.events.append(("fence", key))

    def settle(self, keys, sem):
        self.events.append(("settle", list(keys), sem))

    def finalize(self):
        ops = self.ops
        n = len(ops)
        for o in ops:
            est = estimate_cost(o["eng"], o["fn"])
            o["aset"] = None
            if est is not None:
                o["c"], o["lat"], o["aset"] = est
                if o["sem"].startswith("d_"):
                    o["lat"] = max(o["lat"], 2.0)
        if PE_COLD != 1.0:
            for ev in self.events:
                if ev[0] == "fence":
                    break
                if ev[0] == "op" and ops[ev[1]]["eng"] == "pe":
                    ops[ev[1]]["c"] *= PE_COLD
                    ops[ev[1]]["lat"] *= PE_COLD
        deps = [set() for _ in range(n)]
        last_w, readers, sem_ops = {}, {}, {}
        last_rec = {}
        pin_only = set()
        order = {e: [] for e in self.prog}
        finish = [0.0] * n
        eng_free = {e: 0.0 for e in self.prog}
        seg = []
        vb_alloc, vb_users = {}, {}
        for i, o in enumerate(ops):
            for k in set(o["reads"] + o["writes"]):
                if k.startswith("VB#"):
                    if k not in vb_alloc:
                        assert k in o["writes"], k
                        vb_alloc[k] = i
                        vb_users[k] = 0
                    else:
                        vb_users[k] += 1
        op_alloc = {i: k for k, i in vb_alloc.items()}
        phys = [dict(vb=None, ops=[], left=0) for _ in range(7)]
        vb_phys = {}
        seg_idx = [0]
        nfill = [0]
        self.fill_dep = None
        for i_, o_ in enumerate(ops):
            if "fill_src" in o_["writes"]:
                self.fill_dep = i_
        tbl = [None]
        nload = [0]

        def tbl_pen(o):
            a = o.get("aset")
            if a is None or tbl[0] is None:
                return 0.0 if a is None else TBL_LOAD
            return 0.0 if (a & tbl[0]) else TBL_LOAD

        def tbl_upd(o):
            a = o.get("aset")
            if a is None:
                return
            if tbl[0] is not None and (a & tbl[0]):
                tbl[0] = a & tbl[0]
            else:
                tbl[0] = a
                nload[0] += 1

        def sched_segment(seg):
            if not seg:
                return
            if not LIST_SCHED:
                for i in seg:
                    o = ops[i]
                    dr = max([finish[d] + HOP for d in deps[i]] + [0.0])
                    pen = tbl_pen(o) if o["eng"] == "act" else 0.0
                    st = max(eng_free[o["eng"]], dr) + pen
                    if o["eng"] == "act":
                        tbl_upd(o)
                    eng_free[o["eng"]] = st + o["c"]
                    finish[i] = st + o["lat"]
                    order[o["eng"]].append(i)
                return
            inseg = set(seg)
            succ = {i: [] for i in seg}
            indeg = {}
            for i in seg:
                k = 0
                for d in deps[i]:
                    if d in inseg:
                        succ[d].append(i)
                        k += 1
                indeg[i] = k
            prio = {}
            for i in reversed(seg):
                prio[i] = ops[i]["lat"] + max([prio[s_] + HOP for s_ in succ[i]] + [0.0])
            _SIM.setdefault("cp", []).append(round(max(prio.values()), 1))
            for i in seg:
                prio[i] += ops[i].get("boost", 0.0)
            dep_ready = {}
            ready = {e: [] for e in self.prog}

            def make_ready(i):
                dr = 0.0
                for d in deps[i]:
                    dr = max(dr, finish[d] + (HOP if ops[d]["eng"] != ops[i]["eng"] else 0.1))
                dep_ready[i] = dr
                ready[ops[i]["eng"]].append(i)
            for i in seg:
                if indeg[i] == 0:
                    make_ready(i)
            left = len(seg)
            while left:
                best = None
                for e, lst in ready.items():
                    if not lst:
                        continue
                    ef = eng_free[e]
                    for i in lst:
                        est = dep_ready[i] if dep_ready[i] > ef else ef
                        tp_ = 0.0
                        if e == "act":
                            tp_ = tbl_pen(ops[i])
                            est += tp_ * TBL_CHOICE
                        bsel = None
                        if i in op_alloc:
                            bt = None
                            for b, ph in enumerate(phys):
                                if ph["vb"] is None or ph["left"] == 0:
                                    t_ = max([finish[d] + HOP for d in ph["ops"]] + [0.0])
                                    if bt is None or t_ < bt:
                                        bt, bsel = t_, b
                            if bsel is None:
                                continue
                            if bt > est:
                                est = bt
                        key = (est, -prio[i], i)
                        if best is None or key < best[0]:
                            best = (key, e, i, bsel, est - tp_ * (TBL_CHOICE - 1.0))
                if best is None:
                    raise RuntimeError("scheduler: no PSUM bank available (deadlock)")
                key, e, i, bsel, st_real = best
                if bsel is not None:
                    ph = phys[bsel]
                    for d in ph["ops"]:
                        deps[i].add(d)
                    k_ = op_alloc[i]
                    self.vbs[k_].t = self.psb[bsel]
                    vb_phys[k_] = bsel
                    ph["vb"], ph["ops"], ph["left"] = k_, [i], vb_users[k_]
                for k_ in set(ops[i]["reads"] + ops[i]["writes"]):
                    if k_.startswith("VB#") and vb_alloc[k_] != i:
                        ph = phys[vb_phys[k_]]
                        ph["ops"].append(i)
                        ph["left"] -= 1
                ready[e].remove(i)
                o = ops[i]
                st = st_real
                if e == "pe" and FILL_WARM and seg_idx[0] <= FILL_LAST_SEG and self.fill_dep is not None and st - eng_free["pe"] > 2 * FILL_DUR and eng_free["pe"] > 0:
                    k_ = int((st - eng_free["pe"]) / FILL_DUR) - 1
                    t_ = eng_free["pe"]
                    for _ in range(k_):
                        ops.append(dict(eng="pe", fn=self.fill_fn, reads=[], writes=[], sem="pe", inc=1, c=FILL_DUR, lat=FILL_DUR, pin=False, aset=None))
                        deps.append(set([self.fill_dep]))
                        finish.append(t_ + FILL_DUR)
                        order["pe"].append(len(ops) - 1)
                        t_ += FILL_DUR
                        nfill[0] += 1
                if e == "act":
                    tbl_upd(o)
                eng_free[e] = st + o["c"]
                finish[i] = st + o["lat"]
                order[e].append(i)
                left -= 1
                for s_ in succ[i]:
                    indeg[s_] -= 1
                    if indeg[s_] == 0:
                        make_ready(s_)

        for ev in self.events:
            if ev[0] == "op":
                i = ev[1]
                o = ops[i]
                for k in o["reads"] + o["writes"]:
                    for d in last_w.get(k, ()):
                        deps[i].add(d)
                for k in o["writes"]:
                    for d in readers.get(k, ()):
                        deps[i].add(d)
                if o["pin"] and last_rec.get(o["eng"]) is not None:
                    if last_rec[o["eng"]] not in deps[i]:
                        pin_only.add((i, last_rec[o["eng"]]))
                    deps[i].add(last_rec[o["eng"]])
                last_rec[o["eng"]] = i
                deps[i].discard(i)
                for k in o["writes"]:
                    last_w[k] = [i]
                    readers[k] = []
                for k in o["reads"]:
                    readers.setdefault(k, []).append(i)
                sem_ops.setdefault(o["sem"], []).append(i)
                seg.append(i)
            elif ev[0] == "settle":
                for k in ev[1]:
                    last_w[k] = list(sem_ops.get(ev[2], []))
            elif ev[0] == "fence":
                sched_segment(seg)
                seg_idx[0] += 1
                _SIM.setdefault("segs", []).append((ev[1], max([finish[i] for i in seg] + [0.0]), {e: round(sum(ops[i]["c"] for i in seg if ops[i]["eng"] == e), 1) for e in self.prog}))
                seg = []
                last_w[ev[1]] = [order[e][-1] for e in ("pe", "act", "dve", "pool") if order[e]]
                readers[ev[1]] = []
        sched_segment(seg)
        self.sim_time = max(finish) if finish else 0.0
        _SIM["nload"] = nload[0]
        self.sim_finish = finish
        self.sim_order = order
        self.sim_deps = deps
        _SIM["S"] = self
        n = len(ops)
        tok = [None] * n
        cnt = {k: 0 for k in self.sem_names}
        _SIM["nfill"] = nfill[0]
        for e in self.prog:
            for i in order[e]:
                o = ops[i]
                cnt[o["sem"]] += o["inc"]
                tok[i] = (o["sem"], cnt[o["sem"]])
        self.cnt = cnt
        for e in self.prog:
            waited = {}
            for i in order[e]:
                o = ops[i]
                w = {}
                for d in deps[i]:
                    if (i, d) in pin_only:
                        continue
                    s_, v = tok[d]
                    if waited.get(s_, 0) < v:
                        w[s_] = max(w.get(s_, 0), v)
                for s_, v in w.items():
                    waited[s_] = v
                self.prog[e].append((list(w.items()), o["fn"], o["sem"], o["inc"]))

    def emit(self, block, sems, final_waits=()):
        self.finalize()
        _SIM["t"] = self.sim_time

        def run(engname, e):
            for (waits, fn, sem, inc) in self.prog[engname]:
                for (s, v) in waits:
                    e.wait_ge(sems[s], v)
                fn(e).then_inc(sems[sem], inc)

        @block.tensor
        def _(e):
            run("pe", e)

        @block.scalar
        def _(e):
            run("act", e)

        @block.vector
        def _(e):
            run("dve", e)

        @block.gpsimd
        def _(e):
            run("pool", e)

        @block.sync
        def _(e):
            run("sp", e)
            for s in final_waits:
                if self.cnt[s] > 0:
                    e.wait_ge(sems[s], self.cnt[s])


def build_program():
    nc = bass.Bass("TRN2", target_bir_lowering=False)
    dram = lambda name, shape, dt, kind="Internal": nc.dram_tensor(name, shape, dt, kind=kind).ap()
    xT = dram("xT", [D, NTOK], F32, "ExternalInput")
    cvec_d = dram("cvec", [128, NV], F32, "ExternalInput")
    ident_d = dram("ident", [128, 128], F32, "ExternalInput")
    wg_d = dram("wg", [128, 8 * 128], F32, "ExternalInput")
    xrst_d = dram("xrst", [128, 12], F32, "ExternalInput")
    dwst_d = dram("dwst", [128, 120], F32, "ExternalInput")
    wina_d = dram("w_in_a", [128, 8 * 512], F32, "ExternalInput")
    winb_d = dram("w_in_b", [128, 8 * 1536], F32, "ExternalInput")
    wout_d = dram("w_out", [128, 8 * 1024], F32, "ExternalInput")
    wup_d = dram("w_up", [8, 128, 8 * 512], F32, "ExternalInput")
    wdn_d = dram("w_dn", [8, 128, 32 * 128], F32, "ExternalInput")
    yT = dram("yT", [D, NOUT], F32, "ExternalOutput")
    st_d = dram("st", [128, 2 * 4 * 34], F32, "ExternalOutput")
    x1s = dram("x1s", [D, NOUT], F32)
    wup_b = dram("wup_b", [8, 128, 8 * 512], BF16)
    wdn_b = dram("wdn_b", [8, 128, 32 * 128], BF16)
    xT_v = xT.rearrange("(k p) t -> p k t", p=128)
    yT_v = yT.rearrange("(k p) t -> p k t", p=128)
    x1_v = x1s.rearrange("(k p) t -> p k t", p=128)

    with contextlib.ExitStack() as es:
        def sb(name, shape, dt):
            return es.enter_context(nc.sbuf_tensor(name, shape, dt))

        cvec = sb("cvec_s", [128, NV], F32)
        ident = sb("ident_s", [128, 128], F32)
        onesM = sb("onesM", [128, 128], BF16)
        onesC = sb("onesC", [128, 128], BF16)
        Dr = sb("Dr", [128, 16, 128], BF16)
        Ddw = sb("Ddw", [128, 124, 128], BF16)
        Wg = sb("Wg", [128, 8, 128], BF16)
        xrb = sb("xrb", [128, 4, 3 + 512], BF16)
        vb = sb("vb", [128, 4, 30 + 512], BF16)
        hcar = sb("hcar", [128, 4], F32)
        stt = sb("stt", [128, 2, 4, 34], F32)
        xrst = sb("xrst_s", [128, 4, 3], F32)
        dwst = sb("dwst_s", [128, 4, 30], F32)
        tiny = sb("tiny", [128, 8], F32)
        fill_src = sb("fill_src", [128, 512], BF16)
        xb = [sb(f"xb{i}", [128, 8, 512], F32) for i in range(2)]
        hn = sb("hn", [128, 8, 512], BF16)
        hn2 = sb("hn2", [128, 8, 512], BF16)
        rs = sb("rs", [128, 512], F32)
        R1_BYTES = 110592
        r1 = sb("r1", [128, R1_BYTES // 2], BF16)
        cur = [0]

        def carve(shape, dt, reset=None):
            if reset is not None:
                cur[0] = reset
            n = int(np.prod(shape))
            nb = n * (4 if dt == F32 else 2)
            off = cur[0]
            cur[0] += nb
            assert cur[0] <= R1_BYTES, cur[0]
            ap = r1[:, off // 2:(off + nb) // 2]
            if dt == F32:
                ap = ap.bitcast(F32)
            if len(shape) == 2:
                ap = ap.rearrange("p (a b) -> p a b", a=shape[0])
            elif len(shape) == 3:
                ap = ap.rearrange("p (a b c) -> p a b c", a=shape[0], b=shape[1])
            return ap

        def view(off, shape, dt):
            n = int(np.prod(shape))
            nb = n * (4 if dt == F32 else 2)
            assert off + nb <= R1_BYTES, (off, nb)
            ap = r1[:, off // 2:(off + nb) // 2]
            if dt == F32:
                ap = ap.bitcast(F32)
            if len(shape) == 2:
                ap = ap.rearrange("p (a b) -> p a b", a=shape[0])
            return ap

        w_in = view(0, [8, 2048], BF16)
        w_out = view(32768, [8, 1024], BF16)
        TP0 = 49152
        tp = [view(TP0 + 2048 * i, [512], F32) for i in range(20)]
        gg = view(TP0 + 2048 * 12, [4, 512], F32)
        vc = view(TP0 + 2048 * 16, [4, 512], F32)
        xcb = view(90112, [4, 512], BF16)
        ys = view(94208, [8, 512], BF16)
        mix = view(102400, [8, 512], BF16)
        NWU, NWD = 4, 3
        wu = [view(8192 * i, [8, 512], BF16) for i in range(NWU)]
        hT = view(32768, [32, 512], BF16)
        wd = [view(65536 + 8192 * i, [32, 128], BF16) for i in range(NWD)]
        t_relu = view(90112, [3, 512], F32)

        psb = [es.enter_context(nc.psum_tensor(f"ps{i}", [128, 512], F32)) for i in range(8)]

        sem_names = (["pe", "act", "dve", "pool", "sp", "d_c", "d_cast", "d_castb", "d_casto", "d_cast2", "d_x0", "d_x1", "d_s0", "d_s1", "d_o"]
                     + [f"d_wu{i}" for i in range(NWU)] + [f"d_wd{i}" for i in range(NWD)])
        sems = {n: es.enter_context(nc.semaphore(n)) for n in sem_names}
        block = es.enter_context(nc.Block())
        S = Sched(sem_names)
        S.psb = psb
        S.vbs = {}
        S.fill_fn = lambda e: e.matmul(psb[7][:, :], lhsT=onesM[:], rhs=fill_src[:], start=True, stop=True)

        def bank(pool="s"):
            vb = VB(psb[0])
            S.vbs[vb.key] = vb
            return vb, vb.key

        def cv(c, n=1):
            return cvec[:, c:c + n]

        def drain(gen):
            for _ in gen:
                pass

        def merge(primary, secondary, ratio=2):
            p_alive = s_alive = True
            while p_alive or s_alive:
                for _ in range(ratio):
                    if p_alive:
                        try:
                            next(primary)
                        except StopIteration:
                            p_alive = False
                if s_alive:
                    try:
                        next(secondary)
                    except StopIteration:
                        s_alive = False

        S.op("sp", lambda e: e.dma_start(out=cvec[:], in_=cvec_d), writes=["cvec"], sem="d_c", boost=5000.0)
        S.op("sp", lambda e: e.dma_start(out=ident[:], in_=ident_d), writes=["ident"], sem="d_c", boost=5000.0)
        S.op("sp", lambda e: e.dma_start(out=xrst[:].rearrange("p a b -> p (a b)"), in_=xrst_d), writes=["xrst"], sem="d_c", boost=5000.0)
        S.op("sp", lambda e: e.dma_start(out=dwst[:].rearrange("p a b -> p (a b)"), in_=dwst_d), writes=["dwst"], sem="d_c", boost=5000.0)
        S.settle(["cvec", "ident", "xrst", "dwst"], "d_c")
        S.op("pool", lambda e: e.dma_start(out=Wg[:].rearrange("p a b -> p (a b)"), in_=wg_d, max_dma_last_dim=4096), writes=["Wg"], sem="d_cast")
        S.op("pool", lambda e: e.dma_start(out=w_in[:, :, 0:512], in_=wina_d.rearrange("p (k c) -> p k c", k=8), max_dma_last_dim=4096),
             writes=["w_in_a"], sem="d_cast")
        S.settle(["Wg", "w_in_a"], "d_cast")
        def cast_wout():
            for h in range(4):
                S.op("pool", lambda e, h=h: e.dma_start(out=w_in[:, 2 * h:2 * h + 2, 512:2048],
                                                        in_=winb_d.rearrange("p (k c) -> p k c", k=8)[:, 2 * h:2 * h + 2, :], max_dma_last_dim=8192),
                     writes=[f"w_in_b.{h}"], sem="d_castb", pin=True, c=0.8)
            S.settle(["w_in_b"], "d_castb")
            for h in range(2):
                S.op("pool", lambda e, h=h: e.dma_start(out=w_out[:, 4 * h:4 * h + 4, :].rearrange("p a b -> p (a b)"),
                                                        in_=wout_d[:, h * 4096:(h + 1) * 4096], max_dma_last_dim=4096),
                     writes=[f"w_out.{h}"], sem="d_casto", pin=True, c=0.8)
            S.settle(["w_out"], "d_casto")

        d2d_list = [("u", g) for g in range(8)] + [("d", g) for g in range(8)]

        def cast_d2d(k):
            for _ in range(k):
                if not d2d_list:
                    return
                kind, g = d2d_list.pop(0)
                if kind == "u":
                    S.op("pool", lambda e, g=g: e.dma_start(out=wup_b[g], in_=wup_d[g], max_dma_last_dim=4096), sem="d_cast2", pin=True, c=0.8)
                else:
                    S.op("pool", lambda e, g=g: e.dma_start(out=wdn_b[g], in_=wdn_d[g], max_dma_last_dim=4096), sem="d_cast2", pin=True, c=0.8)

        ddw_next = [0]

        def build_ddw(k):
            return

        S.op("dve", lambda e: e.memset(onesM[:], 1.0 / D), writes=["onesM"])
        S.op("dve", lambda e: e.memset(fill_src[:], 0.5), writes=["fill_src"])
        S.op("dve", lambda e: e.memset(onesC[:], 1.0 / DR), writes=["onesC"])
        S.op("dve", lambda e: e.memset(xrb[:], 0.0), writes=[f"xrb.{j}" for j in range(4)])
        S.op("dve", lambda e: e.memset(vb[:], 0.0), writes=[f"vb.{j}" for j in range(4)])
        S.op("dve", lambda e: e.memset(hcar[:], 0.0), writes=["hcar"])
        S.op("dve", lambda e: e.memset(stt[:], 0.0), writes=["stt"])
        S.op("act", lambda e: e.activation(out=tiny[:, 0:4], in_=cv(CV_LAM, 4), func=AF.Exp, scale=-1.0), reads=["cvec"], writes=["tiny"])
        S.op("act", lambda e: e.activation(out=tiny[:, 4:8], in_=tiny[:, 0:4], func=AF.Ln, bias=1.0), reads=["tiny"], writes=["tiny2"])
        S.op("dve", lambda e: e.tensor_scalar(out=cv(CV_C, 4), in0=tiny[:, 4:8], scalar1=-8.0, scalar2=None, op0=ALU.mult), reads=["tiny2"], writes=["cvc"])
        S.op("dve", lambda e: e.tensor_scalar(out=cv(CV_CH, 4), in0=tiny[:, 4:8], scalar1=-4.0, scalar2=None, op0=ALU.mult), reads=["tiny2"], writes=["cvc"])
        S.op("dve", lambda e: e.tensor_scalar(out=cv(CV_BRH, 8), in0=cv(CV_BR, 8), scalar1=0.5, scalar2=None, op0=ALU.mult), reads=["cvec"], writes=["cvc"])
        S.op("dve", lambda e: e.tensor_tensor(
                 out=Dr[:, :, :],
                 in0=ident[:].unsqueeze(1).broadcast_to([128, 16, 128]),
                 in1=cvec[:, CV_RW:CV_RW + 16].unsqueeze(2).broadcast_to([128, 16, 128]),
                 op=ALU.mult),
             reads=["ident", "cvec"], writes=["Dr"])
        for j in range(4):
            S.op("dve", lambda e, j=j: e.scalar_tensor_tensor(
                     out=Ddw[:, j * 31:(j + 1) * 31, :],
                     in0=ident[:].unsqueeze(1).broadcast_to([128, 31, 128]), scalar=0.5,
                     in1=cvec[:, CV_DW + j * 31:CV_DW + (j + 1) * 31].unsqueeze(2).broadcast_to([128, 31, 128]),
                     op0=ALU.mult, op1=ALU.mult),
                 reads=["ident", "cvec"], writes=[f"Ddw.{j}"])

        def g_load(src_v, c0, n, slot, extra_reads=()):
            S.op("sp", lambda e: e.dma_start(out=xb[slot][:, :, :n], in_=src_v[:, :, c0:c0 + n]),
                 reads=list(extra_reads), writes=[f"xb{slot}"], sem=f"d_x{slot}", boost=PRENORM_BOOST)
            yield

        def g_rstd(ps, pk, n, out_t, out_k, boost=0.0):
            S.op("act", lambda e: e.activation(out=out_t[:, :n], in_=ps[:, :n], func=AF.Ln, bias=cv(CV_EPS)), reads=[pk, "cvec"], writes=[out_k], boost=boost)
            yield
            S.op("act", lambda e: e.activation(out=out_t[:, :n], in_=out_t[:, :n], func=AF.Exp, scale=-0.5), reads=[out_k], writes=[out_k], boost=boost)
            yield

        HBUF = [hn, hn2]
        HKEY = [[f"hn.{k}" for k in range(8)], [f"hn2.{k}" for k in range(8)]]
        HN = HKEY[0]

        def g_prenorm(slot, n, gcol, sqbuf, sqkeys, extra=(), hb=0):
            hn = HBUF[hb]
            if sqbuf is None:
                sqbuf, sqkeys = hn, HKEY[hb]
            x = xb[slot]
            xk = f"xb{slot}"
            S.op("act", lambda e: e.activation(out=sqbuf[:, :, :n], in_=x[:, :, :n], func=AF.Square), reads=[xk] + list(extra), writes=sqkeys, c=0.2 + 0.0008 * 8 * n, boost=PRENORM_BOOST)
            yield
            ps, pk = bank()

            def mm(e):
                for k in range(8):
                    ins = e.matmul(ps[:, :n], lhsT=onesM[:], rhs=sqbuf[:, k, :n], start=(k == 0), stop=(k == 7))
                return ins
            S.op("pe", mm, reads=sqkeys + ["onesM"], writes=[pk], c=pe_c(8, n), boost=PRENORM_BOOST)
            yield
            yield from g_rstd(ps, pk, n, rs, "rs", boost=PRENORM_BOOST)
            for k in range(8):
                S.op("dve", lambda e, k=k: e.scalar_tensor_tensor(out=hn[:, k, :n], in0=x[:, k, :n], scalar=cv(gcol + k), in1=rs[:, :n],
                                                                   op0=ALU.mult, op1=ALU.mult),
                     reads=[xk, "rs", "cvec"], writes=[HKEY[hb][k]], boost=PRENORM_BOOST)
                yield

        def inproj(m, n, pool="s", hb=0):
            hn = HBUF[hb]
            HN = HKEY[hb]
            ps, pk = bank(pool)

            def mm(e):
                for k in range(8):
                    ins = e.matmul(ps[:, :n], lhsT=w_in[:, k, m * 128:(m + 1) * 128], rhs=hn[:, k, :n], start=(k == 0), stop=(k == 7))
                return ins
            S.op("pe", mm, reads=HN + ["w_in_a" if m < 4 else "w_in_b"], writes=[pk], c=pe_c(8, n), boost=INPROJ_BOOST)
            return ps, pk

        def g_xr(n, seg, p_mode=False, hb=0):
            for j in range(4):
                ps, pk = inproj(j, n, hb=hb)
                yield
                if p_mode:
                    S.op("dve", lambda e, j=j, ps=ps: e.tensor_copy(out=xrb[:, j, 3:3 + n], in_=ps[:, :n]), reads=[pk], writes=[f"xrb.{j}"])
                else:
                    S.op("act", lambda e, j=j, ps=ps: e.activation(out=xrb[:, j, 3:3 + n], in_=ps[:, :n], func=AF.Copy), reads=[pk], writes=[f"xrb.{j}"])
                yield
                if seg is not None:
                    S.op("act", lambda e, j=j, ps=ps: e.activation(out=stt[:, seg, j, 0:3], in_=ps[:, n - 3:n], func=AF.Copy), reads=[pk], writes=["stt"])
                    yield

        def g_xr_issue(n, store, hb=0):
            for j in range(4):
                store[j] = inproj(j, n, pool="l", hb=hb)
                yield

        def g_xr_evac(n, store):
            for j in range(4):
                ps, pk = store[j]
                S.op("dve", lambda e, j=j, ps=ps: e.tensor_copy(out=xrb[:, j, 3:3 + n], in_=ps[:, :n]), reads=[pk], writes=[f"xrb.{j}"])
                yield

        def g_glu(n, seg, hb=0):
            for j in range(4):
                pg, pgk = inproj(12 + j, n, hb=hb)
                yield
                tg = tp[10 + j % 2]
                tgk = f"tp{10 + j % 2}"
                S.op("act", lambda e, pg=pg, tg=tg: e.activation(out=tg[:, :n], in_=pg[:, :n], func=AF.Tanh, scale=0.5), reads=[pgk], writes=[tgk])
                yield
                pv, pvk = inproj(8 + j, n, hb=hb)
                yield
                S.op("dve", lambda e, pv=pv, tg=tg, j=j: e.scalar_tensor_tensor(out=vb[:, j, 30:30 + n], in0=tg[:, :n], scalar=1.0, in1=pv[:, :n],
                                                                               op0=ALU.add, op1=ALU.mult),
                     reads=[pvk, tgk], writes=[f"vb.{j}"])
                yield
                if seg is not None:
                    S.op("dve", lambda e, tg=tg: e.tensor_scalar(out=tg[:, n - 30:n], in0=tg[:, n - 30:n], scalar1=0.5, scalar2=0.5, op0=ALU.mult, op1=ALU.add),
                         reads=[tgk], writes=[tgk])
                    S.op("dve", lambda e, pv=pv, tg=tg, j=j: e.tensor_tensor(out=stt[:, seg, j, 4:34], in0=tg[:, n - 30:n], in1=pv[:, n - 30:n], op=ALU.mult),
                         reads=[pvk, tgk], writes=["stt"])
                    yield

        def g_gate(n, hb=0):
            for j in range(4):
                pgt, pgtk = inproj(4 + j, n, hb=hb)
                yield
                S.op("act", lambda e, pgt=pgt, j=j: e.activation(out=gg[:, j, :n], in_=pgt[:, :n], func=AF.Gelu_apprx_tanh), reads=[pgtk, "PA"], writes=[f"tp{12 + j}"])
                yield

        def g_gate_issue(n, store, hb=0):
            for j in range(4):
                store[j] = inproj(4 + j, n, pool="l", hb=hb)
                yield

        def gate_evac(n, store):
            for j in range(4):
                pgt, pgtk = store[j]
                S.op("act", lambda e, pgt=pgt, j=j: e.activation(out=gg[:, j, :n], in_=pgt[:, :n], func=AF.Gelu_apprx_tanh), reads=[pgtk, "PA"], writes=[f"tp{12 + j}"])

        def g_chains(js, sets, n, seg=None, want_y=False, after_gates=None, after_conv=None, p_mode=False):
            J = list(zip(js, sets))
            T = lambda s, i: tp[5 * s + i]
            K = lambda s, i: f"tp{5 * s + i}"
            bk = {}
            for j, s in J:
                ps, pk = bank()
                bk[j] = (ps, pk)

                def mmc(e, j=j, ps=ps):
                    for k in range(4):
                        ins = e.matmul(ps[:, :n], lhsT=Dr[:, j * 4 + k, :], rhs=xrb[:, j, k:k + n], start=(k == 0), stop=(k == 3))
                    return ins
                S.op("pe", mmc, reads=[f"xrb.{j}", "Dr"], writes=[pk], c=pe_c(4, n))
                yield
            if after_conv is not None:
                after_conv()
            for j, s in J:
                ps, pk = bk[j]
                S.op("act", lambda e, j=j, s=s, ps=ps: e.activation(out=T(s, 0)[:, :n], in_=ps[:, :n], func=AF.Identity, bias=cv(CV_CB + j)),
                     reads=[pk, "cvec"], writes=[K(s, 0)])
                yield
                S.op("pool", lambda e, j=j: e.tensor_copy(out=xrb[:, j, 0:3], in_=xrb[:, j, n:n + 3]), reads=[f"xrb.{j}"], writes=[f"xrb.{j}"])
                yield
            for j, s in J:
                S.op("dve", lambda e, s=s: e.tensor_copy(out=xcb[:, s, :n], in_=T(s, 0)[:, :n]), reads=[K(s, 0)], writes=[f"xcb{s}"])
                yield
            gb = {}
            for j, s in J:
                pr, prk = bank()
                S.op("pe", lambda e, j=j, s=s, pr=pr: e.matmul(pr[:, :n], lhsT=Wg[:, j, :], rhs=xcb[:, s, :n], start=True, stop=True), reads=[f"xcb{s}", "Wg"], writes=[prk], c=pe_c(1, n))
                yield
                pi, pik = bank()
                S.op("pe", lambda e, j=j, s=s, pi=pi: e.matmul(pi[:, :n], lhsT=Wg[:, 4 + j, :], rhs=xcb[:, s, :n], start=True, stop=True), reads=[f"xcb{s}", "Wg"], writes=[pik], c=pe_c(1, n))
                yield
                gb[j] = (pr, prk, pi, pik)
                S.op("act", lambda e, j=j, s=s, pr=pr: e.activation(out=T(s, 1)[:, :n], in_=pr[:, :n], func=AF.Tanh, scale=0.5, bias=cv(CV_BRH + j)),
                     reads=[prk, "cvc"], writes=[K(s, 1)])
                yield
                S.op("act", lambda e, j=j, s=s, pi=pi: e.activation(out=T(s, 2)[:, :n], in_=pi[:, :n], func=AF.Tanh, scale=0.5, bias=cv(CV_BIH + j)),
                     reads=[pik, "cvc"], writes=[K(s, 2)])
                yield
            if after_gates is not None:
                after_gates()
            for j, s in J:
                if not p_mode:
                    S.op("act", lambda e, j=j, s=s: e.activation(out=T(s, 3)[:, :n], in_=T(s, 1)[:, :n], func=AF.Exp, scale=cv(CV_C + j), bias=cv(CV_C + j)),
                         reads=[K(s, 1), "cvc"], writes=[K(s, 3)])
                    yield
                S.op("act", lambda e, j=j, s=s: e.activation(out=T(s, 1)[:, :n], in_=T(s, 1)[:, :n], func=AF.Exp, scale=cv(CV_CH + j), bias=cv(CV_CH + j)),
                     reads=[K(s, 1), "cvc"], writes=[K(s, 1)])
                yield
                if p_mode:
                    S.op("dve", lambda e, s=s: e.tensor_tensor(out=T(s, 3)[:, :n], in0=T(s, 1)[:, :n], in1=T(s, 1)[:, :n], op=ALU.mult),
                         reads=[K(s, 1)], writes=[K(s, 3)])
                    yield
            for j, s in J:
                S.op("dve", lambda e, s=s: e.scalar_tensor_tensor(out=T(s, 2)[:, :n], in0=T(s, 2)[:, :n], scalar=1.0, in1=T(s, 0)[:, :n], op0=ALU.add, op1=ALU.mult),
                     reads=[K(s, 2), K(s, 0)], writes=[K(s, 2)])
                yield
            for j, s in J:
                if LNEXP_SQRT:
                    S.op("act", lambda e, s=s: e.activation(out=T(s, 3)[:, :n], in_=T(s, 3)[:, :n], func=AF.Ln, scale=-1.0, bias=1.0), reads=[K(s, 3)], writes=[K(s, 3)])
                    yield
                    S.op("act", lambda e, s=s: e.activation(out=T(s, 3)[:, :n], in_=T(s, 3)[:, :n], func=AF.Exp, scale=0.5), reads=[K(s, 3)], writes=[K(s, 3)])
                else:
                    S.op("act", lambda e, s=s: e.activation(out=T(s, 3)[:, :n], in_=T(s, 3)[:, :n], func=AF.Sqrt, scale=-1.0, bias=1.0), reads=[K(s, 3)], writes=[K(s, 3)])
                yield
            for j, s in J:
                S.op("dve", lambda e, s=s: e.scalar_tensor_tensor(out=T(s, 2)[:, :n], in0=T(s, 2)[:, :n], scalar=0.5, in1=T(s, 3)[:, :n], op0=ALU.mult, op1=ALU.mult),
                     reads=[K(s, 2), K(s, 3)], writes=[K(s, 2)])
                yield
                S.op("dve", lambda e, j=j, s=s: e.tensor_tensor_scan(out=T(s, 4)[:, :n], data0=T(s, 1)[:, :n], data1=T(s, 2)[:, :n], initial=hcar[:, j:j + 1],
                                                                     op0=ALU.mult, op1=ALU.add),
                     reads=[K(s, 1), K(s, 2), "hcar"], writes=[K(s, 4)], c=1.3)
                yield
                S.op("dve", lambda e, j=j, s=s: e.tensor_copy(out=hcar[:, j:j + 1], in_=T(s, 4)[:, n - 1:n]), reads=[K(s, 4)], writes=["hcar"])
                yield
                if seg is not None:
                    S.op("pool", lambda e, j=j, s=s: e.tensor_copy(out=stt[:, seg, j, 3:4], in_=T(s, 4)[:, n - 1:n]), reads=[K(s, 4)], writes=["stt"])
                if want_y:
                    S.op("dve", lambda e, j=j, s=s: e.tensor_tensor(out=gg[:, j, :n], in0=gg[:, j, :n], in1=T(s, 4)[:, :n], op=ALU.mult),
                         reads=[f"tp{12 + j}", K(s, 4)], writes=[f"tp{12 + j}"])
                    yield

        pre_chunks = [(0, 512), (512, 512), (1024, 512), (1536, 512), (2048, 16)]
        xslot = [0]

        def nslot():
            s = xslot[0]
            xslot[0] ^= 1
            return s

        def g_PHa(ci):
            c0, n = pre_chunks[ci]
            slot = nslot()
            yield from g_load(xT_v, C_PRE + c0, n, slot)
            yield from g_prenorm(slot, n, CV_GMIX, None, None, hb=ci % 2)

        xr_store = {}
        drain(g_PHa(0))
        drain(g_xr_issue(pre_chunks[0][1], xr_store, hb=0))
        for ci, (c0, n) in enumerate(pre_chunks):
            drain(g_xr_evac(n, xr_store))
            if ci == 0:
                cast_wout()
            if ci == len(pre_chunks) - 1:
                S.op("dve", lambda e: e.tensor_scalar(out=hcar[:], in0=hcar[:], scalar1=cv(CV_FLAG), scalar2=None, op0=ALU.mult),
                     reads=["hcar", "cvec"], writes=["hcar"])
            ch = g_chains([0, 1, 2, 3], [0, 1, 2, 3], n, p_mode=True)
            if ci + 1 < len(pre_chunks):
                xr_store = {}

                def sec(ci=ci, st=xr_store):
                    yield from g_PHa(ci + 1)
                    yield from g_xr_issue(pre_chunks[ci + 1][1], st, hb=(ci + 1) % 2)
                merge(ch, sec(), ratio=3)
            else:
                drain(ch)
            cast_d2d(2)
            build_ddw(31)

        B_chunks = [(0, 424), (424, 424), (848, 424), (1272, 420), (1692, 420)]
        B_slot = {}
        wu_i = [0]
        wd_i = [0]
        HT = [f"hT.{f}" for f in range(32)]

        def BHB(ci):
            return (ci + len(A_chunks)) % 2

        def g_BH(ci, first=False):
            oc0, n = B_chunks[ci]
            slot = nslot()
            B_slot[ci] = slot
            yield from g_load(x1_v, oc0, n, slot, extra_reads=["x1s"])
            if first:
                yield from g_prenorm(slot, n, CV_GMLP, None, None, hb=BHB(ci))
            else:
                yield from g_prenorm(slot, n, CV_GMLP, mix, ["mixB"], extra=["R1"], hb=BHB(ci))

        def load_wu(g):
            ws = wu_i[0] % NWU
            wu_i[0] += 1
            S.op("sp", lambda e: e.dma_start(out=wu[ws][:].rearrange("p a b -> p (a b)"), in_=wup_b[g]),
                 reads=["wup_b", "R1a"], writes=[f"wu{ws}"], sem=f"d_wu{ws}")
            return ws

        def load_wd(m):
            ws = wd_i[0] % NWD
            wd_i[0] += 1
            S.op("sp", lambda e: e.dma_start(out=wd[ws][:].rearrange("p a b -> p (a b)"), in_=wdn_b[m]),
                 reads=["wdn_b", "R1"], writes=[f"wd{ws}"], sem=f"d_wd{ws}")
            return ws

        def _one_up(ci, g, ws):
            oc0, n = B_chunks[ci]
            for mi in range(4):
                pu, puk = bank("all")

                hnb = HBUF[BHB(ci)]

                def mmu(e, mi=mi, pu=pu, hnb=hnb):
                    for k in range(8):
                        ins = e.matmul(pu[:, :n], lhsT=wu[ws][:, k, mi * 128:(mi + 1) * 128], rhs=hnb[:, k, :n], start=(k == 0), stop=(k == 7))
                    return ins
                S.op("pe", mmu, reads=HKEY[BHB(ci)] + [f"wu{ws}"], writes=[puk], c=pe_c(8, n))
                f = g * 4 + mi
                tb = f % 3
                S.op("act", lambda e, pu=pu, tb=tb: e.activation(out=t_relu[:, tb, :n], in_=pu[:, :n], func=AF.Relu), reads=[puk, "R1"], writes=[f"t_relu{tb}"])
                eng = "dve" if f % 4 != 3 else "pool"
                S.op(eng, lambda e, f=f, tb=tb: e.tensor_tensor(out=hT[:, f, :n], in0=t_relu[:, tb, :n], in1=t_relu[:, tb, :n], op=ALU.mult),
                     reads=[f"t_relu{tb}", "R1"], writes=[f"hT.{f}"])
                yield

        def _one_down(ci, m, ws):
            oc0, n = B_chunks[ci]
            slot = B_slot[ci]
            x = xb[slot]
            xk = f"xb{slot}"
            pd, pdk = bank("all")

            def mmd2(e):
                for k in range(32):
                    ins = e.matmul(pd[:, :n], lhsT=wd[ws][:, k, :], rhs=hT[:, k, :n], start=(k == 0), stop=(k == 31))
                return ins
            S.op("pe", mmd2, reads=HT + [f"wd{ws}"], writes=[pdk], c=pe_c(32, n))
            S.op("dve", lambda e: e.tensor_tensor(out=x[:, m, :n], in0=pd[:, :n], in1=x[:, m, :n], op=ALU.add),
                 reads=[pdk, xk], writes=[xk])
            yield

        def g_Bfinal(ci):
            oc0, n = B_chunks[ci]
            slot = B_slot[ci]
            x = xb[slot]
            xk = f"xb{slot}"
            S.op("act", lambda e: e.activation(out=mix[:, :, :n], in_=x[:, :, :n], func=AF.Square), reads=[xk, "R1"], writes=["mixB"], c=0.2 + 0.0008 * 8 * n)
            yield
            ps, pk = bank("all")

            def mmf(e):
                for k in range(8):
                    ins = e.matmul(ps[:, :n], lhsT=onesM[:], rhs=mix[:, k, :n], start=(k == 0), stop=(k == 7))
                return ins
            S.op("pe", mmf, reads=["mixB", "onesM"], writes=[pk], c=pe_c(8, n))
            yield
            yield from g_rstd(ps, pk, n, rs, "rs")
            for k in range(8):
                S.op("dve", lambda e, k=k: e.scalar_tensor_tensor(out=x[:, k, :n], in0=x[:, k, :n], scalar=cv(CV_GFIN + k), in1=rs[:, :n],
                                                                   op0=ALU.mult, op1=ALU.mult),
                     reads=[xk, "rs", "cvec"], writes=[xk])
                yield
            S.op("sp", lambda e: e.dma_start(out=yT_v[:, :, oc0:oc0 + n], in_=x[:, :, :n]), reads=[xk], sem=f"d_s{slot}")
            yield

        nB = len(B_chunks)
        wu_uses = [(c, g) for c in range(nB) for g in range(8)]
        wd_uses = [(c, m) for c in range(nB) for m in range(8)]
        wu_slot, wd_slot = {}, {}
        wu_ptr, wd_ptr = [0], [0]

        def ensure_wu(idx):
            while wu_ptr[0] <= min(idx, len(wu_uses) - 1):
                c, g = wu_uses[wu_ptr[0]]
                wu_slot[(c, g)] = load_wu(g)
                wu_ptr[0] += 1

        def ensure_wd(idx):
            while wd_ptr[0] <= min(idx, len(wd_uses) - 1):
                c, m = wd_uses[wd_ptr[0]]
                wd_slot[(c, m)] = load_wd(m)
                wd_ptr[0] += 1

        def step(gen, k):
            for _ in range(k):
                try:
                    next(gen)
                except StopIteration:
                    return

        A_chunks = [(C_SMP, SMP, "sample", 1, MAIN), (C_HALO, HALO, "halo", None, None)] \
            + [(C_MAIN + 512 * c, 512, "main", 0 if c == 3 else None, 512 * c) for c in range(4)]
        A_slot = {}

        def g_AH1a(ci):
            c0, n, mode, seg, oc0 = A_chunks[ci]
            if ci == 2:
                slot = A_slot[1]
            else:
                slot = nslot()
            A_slot[ci] = slot
            yield from g_load(xT_v, c0, n, slot)
            yield from g_prenorm(slot, n, CV_GMIX, None, None, hb=ci % 2)

        def g_AH1b(ci):
            c0, n, mode, seg, oc0 = A_chunks[ci]
            if mode == "sample":
                S.op("dve", lambda e: e.tensor_copy(out=tiny[:, 0:4], in_=hcar[:]), reads=["hcar"], writes=["tiny"])
                S.op("dve", lambda e: e.tensor_copy(out=xrb[:, :, 0:3], in_=xrst[:]), reads=["xrst"], writes=[f"xrb.{j}" for j in range(4)])
                S.op("dve", lambda e: e.tensor_scalar(out=vb[:, :, 0:30], in0=dwst[:], scalar1=2.0, scalar2=None, op0=ALU.mult), reads=["dwst"], writes=[f"vb.{j}" for j in range(4)])
                S.op("dve", lambda e: e.tensor_copy(out=hcar[:], in_=cv(CV_H0, 4)), reads=["cvec"], writes=["hcar"])
            yield from g_xr(n, seg, hb=ci % 2)
            yield from g_glu(n, seg, hb=ci % 2)

        def g_c31(j, n, store):
            pc, pck = bank("l")
            store[j] = (pc, pck)

            def mmd(e):
                for k in range(31):
                    ins = e.matmul(pc[:, :n], lhsT=Ddw[:, j * 31 + k, :], rhs=vb[:, j, k:k + n], start=(k == 0), stop=(k == 30))
                return ins
            S.op("pe", mmd, reads=[f"vb.{j}", f"Ddw.{j}"], writes=[pck], c=pe_c(31, n))
            S.op("pool", lambda e: e.tensor_copy(out=vb[:, j, 0:30], in_=vb[:, j, n:n + 30]), reads=[f"vb.{j}"], writes=[f"vb.{j}"])

        def A_M(ci, c31):
            c0, n, mode, seg, oc0 = A_chunks[ci]
            drain(g_chains([0, 1], [0, 1], n, seg=seg, want_y=True, after_conv=lambda: g_c31(0, n, c31), after_gates=lambda: g_c31(1, n, c31)))
            drain(g_chains([2, 3], [0, 1], n, seg=seg, want_y=True, after_conv=lambda: g_c31(2, n, c31), after_gates=lambda: g_c31(3, n, c31)))

        YS = [f"ys.{k}" for k in range(8)]

        def g_AT1(ci, c31):
            c0, n, mode, seg, oc0 = A_chunks[ci]
            mean_t, m2_t, rc_t, rr_t, rk_t = tp[0], tp[1], tp[2], tp[3], tp[4]
            for j in range(4):
                pc, pck = c31[j]
                S.op("act", lambda e, j=j, pc=pc: e.activation(out=vc[:, j, :n], in_=pc[:, :n], func=AF.Identity, bias=cv(CV_DWB + j)),
                     reads=[pck, "cvec", "PA"], writes=[f"tp{16 + j}"])
                yield
                S.op("act", lambda e, j=j, pc=pc: e.activation(out=ys[:, 4 + j, :n], in_=pc[:, :n], func=AF.Square, bias=cv(CV_DWB + j)),
                     reads=[pck, "cvec", "ys"], writes=[f"ys.{4 + j}"])
                yield
                S.op("dve", lambda e, j=j: e.tensor_copy(out=ys[:, j, :n], in_=vc[:, j, :n]), reads=[f"tp{16 + j}", "ys"], writes=[f"ys.{j}"])
                yield
            pm, pmk = bank()

            def mm_mean(e):
                for j in range(4):
                    ins = e.matmul(pm[:, :n], lhsT=onesC[:], rhs=ys[:, j, :n], start=(j == 0), stop=(j == 3))
                return ins
            S.op("pe", mm_mean, reads=YS[0:4] + ["onesC"], writes=[pmk], c=pe_c(4, n))
            yield
            pq, pqk = bank()

            def mm_msq(e):
                for j in range(4):
                    ins = e.matmul(pq[:, :n], lhsT=onesC[:], rhs=ys[:, 4 + j, :n], start=(j == 0), stop=(j == 3))
                return ins
            S.op("pe", mm_msq, reads=YS[4:8] + ["onesC"], writes=[pqk], c=pe_c(4, n))
            yield
            S.op("act", lambda e: e.activation(out=mean_t[:, :n], in_=pm[:, :n], func=AF.Copy), reads=[pmk], writes=["tp0"])
            yield
            S.op("pool", lambda e: e.tensor_tensor(out=m2_t[:, :n], in0=mean_t[:, :n], in1=mean_t[:, :n], op=ALU.mult), reads=["tp0"], writes=["tp1"])
            yield
            S.op("dve", lambda e: e.tensor_tensor(out=m2_t[:, :n], in0=pq[:, :n], in1=m2_t[:, :n], op=ALU.subtract), reads=[pqk, "tp1"], writes=["tp1"])
            yield
            S.op("dve", lambda e: e.tensor_scalar(out=m2_t[:, :n], in0=m2_t[:, :n], scalar1=0.0, scalar2=None, op0=ALU.max), reads=["tp1"], writes=["tp1"])
            yield
            S.op("act", lambda e: e.activation(out=rc_t[:, :n], in_=m2_t[:, :n], func=AF.Ln, bias=cv(CV_EPS)), reads=["tp1", "cvec"], writes=["tp2"])
            yield
            S.op("act", lambda e: e.activation(out=rc_t[:, :n], in_=rc_t[:, :n], func=AF.Exp, scale=-0.5), reads=["tp2"], writes=["tp2"])
            yield
            for j in range(4):
                S.op("pool", lambda e, j=j: e.tensor_tensor(out=vc[:, j, :n], in0=vc[:, j, :n], in1=mean_t[:, :n], op=ALU.subtract),
                     reads=[f"tp{16 + j}", "tp0"], writes=[f"tp{16 + j}"])
                yield
                S.op("dve", lambda e, j=j: e.tensor_tensor(out=vc[:, j, :n], in0=vc[:, j, :n], in1=rc_t[:, :n], op=ALU.mult),
                     reads=[f"tp{16 + j}", "tp2"], writes=[f"tp{16 + j}"])
                yield
            for j in range(4):
                S.op("act", lambda e, j=j: e.activation(out=vc[:, j, :n], in_=vc[:, j, :n], func=AF.Silu, scale=cv(CV_LNG + j), bias=cv(CV_LNB + j)),
                     reads=[f"tp{16 + j}", "cvec"], writes=[f"tp{16 + j}"])
                yield
            S.op("act", lambda e: e.activation(out=ys[:, 0:4, :n], in_=gg[:, :, :n], func=AF.Square), reads=[f"tp{12 + j}" for j in range(4)] + ["ys"], writes=YS[0:4], c=0.2 + 0.0008 * 4 * n)
            yield
            S.op("act", lambda e: e.activation(out=ys[:, 4:8, :n], in_=vc[:, :, :n], func=AF.Square), reads=[f"tp{16 + j}" for j in range(4)] + ["ys"], writes=YS[4:8], c=0.2 + 0.0008 * 4 * n)
            yield
            pr_, prk_ = bank()

            def mm_sr(e):
                for j in range(4):
                    ins = e.matmul(pr_[:, :n], lhsT=onesC[:], rhs=ys[:, j, :n], start=(j == 0), stop=(j == 3))
                return ins
            S.op("pe", mm_sr, reads=YS[0:4] + ["onesC"], writes=[prk_], c=pe_c(4, n))
            yield
            pc_, pck_ = bank()

            def mm_sc(e):
                for j in range(4):
                    ins = e.matmul(pc_[:, :n], lhsT=onesC[:], rhs=ys[:, 4 + j, :n], start=(j == 0), stop=(j == 3))
                return ins
            S.op("pe", mm_sc, reads=YS[4:8] + ["onesC"], writes=[pck_], c=pe_c(4, n))
            yield
            S.op("act", lambda e: e.activation(out=rr_t[:, :n], in_=pr_[:, :n], func=AF.Ln, bias=cv(CV_EPS)), reads=[prk_, "cvec"], writes=["tp3"])
            yield
            S.op("act", lambda e: e.activation(out=rk_t[:, :n], in_=pc_[:, :n], func=AF.Ln, bias=cv(CV_EPS)), reads=[pck_, "cvec"], writes=["tp4"])
            yield
            S.op("act", lambda e: e.activation(out=rr_t[:, :n], in_=rr_t[:, :n], func=AF.Exp, scale=-0.5), reads=["tp3"], writes=["tp3"])
            yield
            S.op("act", lambda e: e.activation(out=rk_t[:, :n], in_=rk_t[:, :n], func=AF.Exp, scale=-0.5), reads=["tp4"], writes=["tp4"])
            yield
            for j in range(4):
                S.op("dve", lambda e, j=j: e.scalar_tensor_tensor(out=mix[:, j, :n], in0=gg[:, j, :n], scalar=cv(CV_GR + j), in1=rr_t[:, :n], op0=ALU.mult, op1=ALU.mult),
                     reads=[f"tp{12 + j}", "tp3", "cvec"], writes=["mix"])
                yield
            for j in range(4):
                S.op("dve", lambda e, j=j: e.scalar_tensor_tensor(out=mix[:, 4 + j, :n], in0=vc[:, j, :n], scalar=cv(CV_GC + j), in1=rk_t[:, :n], op0=ALU.mult, op1=ALU.mult),
                     reads=[f"tp{16 + j}", "tp4", "cvec"], writes=["mix"])
                yield

        def g_AT2(ci):
            c0, n, mode, seg, oc0 = A_chunks[ci]
            slot = A_slot[ci]
            x = xb[slot]
            xk = f"xb{slot}"
            for m in range(8):
                po, pok = bank()

                def mmo(e, m=m, po=po):
                    for k in range(8):
                        ins = e.matmul(po[:, :n], lhsT=w_out[:, k, m * 128:(m + 1) * 128], rhs=mix[:, k, :n], start=(k == 0), stop=(k == 7))
                    return ins
                S.op("pe", mmo, reads=["mix", "w_out"], writes=[pok], c=pe_c(8, n))
                yield
                S.op("dve", lambda e, m=m, po=po: e.tensor_tensor(out=x[:, m, :n], in0=po[:, :n], in1=x[:, m, :n], op=ALU.add),
                     reads=[pok, xk], writes=[xk])
                yield
            S.op("sp", lambda e: e.dma_start(out=x1_v[:, :, oc0:oc0 + n], in_=x[:, :, :n]), reads=[xk], writes=["x1s"], sem=f"d_s{slot}")
            yield

        def halo_tails(n):
            for j in range(4):
                S.op("pool", lambda e, j=j: e.tensor_copy(out=xrb[:, j, 0:3], in_=xrb[:, j, n:n + 3]), reads=[f"xrb.{j}"], writes=[f"xrb.{j}"])
                S.op("pool", lambda e, j=j: e.tensor_copy(out=vb[:, j, 0:30], in_=vb[:, j, n:n + 30]), reads=[f"vb.{j}"], writes=[f"vb.{j}"])

        nA = len(A_chunks)
        drain(g_AH1a(0))
        drain(g_AH1b(0))
        drain(g_gate(A_chunks[0][1], hb=0))
        for ci in [0, 2, 3, 4, 5]:
            n = A_chunks[ci][1]
            c31 = {}
            A_M(ci, c31)
            if ci == 0:
                S.op("dve", lambda e: e.tensor_copy(out=hcar[:], in_=tiny[:, 0:4]), reads=["tiny", "hcar"], writes=["hcar"])
            cast_d2d(2)
            t1 = g_AT1(ci, c31)
            nxt = 2 if ci == 0 else ci + 1
            if nxt < nA:
                gstore = {}

                def head(ci=ci, nxt=nxt, gstore=gstore):
                    if ci == 0:
                        yield from g_AH1a(1)
                        yield from g_AH1b(1)
                        halo_tails(HALO)
                        yield
                    yield from g_AH1a(nxt)
                    yield from g_AH1b(nxt)
                    yield from g_gate_issue(A_chunks[nxt][1], gstore, hb=nxt % 2)
                merge(t1, head(), ratio=1)
                gate_evac(A_chunks[nxt][1], gstore)
                if nxt == nA - 1:
                    S.fence("R1a")
                    cast_d2d(16)
                    S.settle(["wup_b", "wdn_b"], "d_cast2")
                    ensure_wu(NWU - 1)
                drain(g_AT2(ci))
            else:
                b0 = g_BH(0, first=True)
                merge(t1, b0, ratio=2)
                drain(g_AT2(ci))
        S.op("sp", lambda e: e.dma_start(out=st_d, in_=stt[:].rearrange("p a b c -> p (a b c)")), reads=["stt"], sem="d_o")
        cast_d2d(16)
        S.settle(["wup_b", "wdn_b"], "d_cast2")

        S.fence("R1")
        for ci in range(nB):
            oc0, n = B_chunks[ci]
            ensure_wd(8 * ci + NWD - 1)
            nxt = g_BH(ci + 1) if ci + 1 < nB else iter(())
            for g in range(8):
                ensure_wu(8 * ci + g + NWU - 1)
                drain(_one_up(ci, g, wu_slot[(ci, g)]))
                if g == 1:
                    step(nxt, 2)
                if g == 4:
                    step(nxt, 3)
            drain(nxt)
            for m in range(8):
                ensure_wd(8 * ci + m + NWD - 1)
                if m == 2:
                    ensure_wu(8 * (ci + 1) + NWU - 1)
                drain(_one_down(ci, m, wd_slot[(ci, m)]))
            drain(g_Bfinal(ci))

        S.emit(block, sems, final_waits=["d_s0", "d_s1", "d_o"])
    return nc


_NC_CACHE = {}


def _host_inputs(inp):
    f32 = np.float32
    g = lambda k: np.asarray(inp[k], dtype=f32)
    x_prompt, x_sample, meta = g("x_prompt"), g("x_sample"), g("meta_tokens")
    fm8 = lambda v: np.ascontiguousarray(v.reshape(8, 128).T)
    fm4 = lambda v: np.ascontiguousarray(v.reshape(4, 128).T)
    cv = np.zeros((128, NV), f32)
    cv[:, CV_GMIX:CV_GMIX + 8] = fm8(g("norm_mix")[0])
    cv[:, CV_GMLP:CV_GMLP + 8] = fm8(g("norm_mlp")[0])
    cv[:, CV_GFIN:CV_GFIN + 8] = fm8(g("norm_final"))
    for col, key in ((CV_CB, "rnn_conv_b"), (CV_BR, "b_gate_r"), (CV_BI, "b_gate_i"), (CV_LAM, "rglru_lambda"), (CV_DWB, "dw_b"),
                     (CV_LNG, "ln_conv_g"), (CV_LNB, "ln_conv_b"), (CV_GR, "out_norm_rnn"), (CV_GC, "out_norm_conv")):
        cv[:, col:col + 4] = fm4(g(key)[0])
    rw = g("rnn_conv_w")[0]
    dw = g("dw_w")[0]
    for j in range(4):
        cv[:, CV_RW + 4 * j:CV_RW + 4 * j + 4] = rw[:, j * 128:(j + 1) * 128].T
        cv[:, CV_DW + 31 * j:CV_DW + 31 * j + 31] = dw[:, j * 128:(j + 1) * 128].T
    cv[:, CV_EPS] = EPS
    cv[:, CV_ONEP] = np.float32(1.0) + np.float32(1.1920929e-07)
    ident = np.eye(128, dtype=f32)
    wg = np.zeros((128, 8, 128), f32)
    for gi, key in enumerate(("w_gate_r", "w_gate_i")):
        w = g(key)[0]
        for j in range(4):
            for a in range(2):
                wg[64 * a:64 * a + 64, gi * 4 + j, 64 * a:64 * a + 64] = w[2 * j + a]
    wg = wg.reshape(128, 8 * 128)
    blk = lambda w, K, N: np.ascontiguousarray(w.reshape(K // 128, 128, N).transpose(1, 0, 2))
    w_in3 = blk(g("w_in")[0], D, 2048)
    w_in_a = np.ascontiguousarray(w_in3[:, :, 0:512]).reshape(128, -1)
    w_in_b = np.ascontiguousarray(w_in3[:, :, 512:2048]).reshape(128, -1)
    w_out = blk(g("w_out")[0], D, D).reshape(128, -1)
    wu = blk(g("w_up")[0], D, DFF)
    w_up = np.ascontiguousarray(wu.reshape(128, 8, 8, 512).transpose(2, 0, 1, 3)).reshape(8, 128, 8 * 512)
    wd = blk(g("w_down")[0], DFF, D)
    w_dn = np.ascontiguousarray(wd.reshape(128, 32, 8, 128).transpose(2, 0, 1, 3)).reshape(8, 128, 32 * 128)
    common = dict(ident=ident, wg=wg, w_in_a=w_in_a, w_in_b=w_in_b, w_out=w_out, w_up=w_up, w_dn=w_dn)
    st_conv, st_h, st_dw = g("state_rglru_conv")[0], g("state_rglru_h")[0], g("state_dwconv")[0]
    maps = []
    for c in range(8):
        j, p = divmod(c, 2)
        seq = np.concatenate([meta, x_prompt[j]], axis=0)
        xs = np.zeros((NTOK, D), f32)
        if p == 0:
            xs[C_PRE + 2048:C_PRE + 2064] = seq[0:16]
            xs[C_HALO + 32:C_HALO + 48] = seq[0:16]
            xs[C_MAIN:C_MAIN + MAIN] = seq[16:16 + MAIN]
        else:
            xs[C_PRE:C_PRE + PRE] = seq[0:PRE]
            xs[C_HALO:C_HALO + HALO] = seq[PRE - HALO:PRE]
            xs[C_MAIN:C_MAIN + MAIN] = seq[PRE:PRE + MAIN]
        xs[C_SMP:C_SMP + SMP] = x_sample[c]
        cvc = cv.copy()
        cvc[:, CV_FLAG] = float(p)
        cvc[:, CV_H0:CV_H0 + 4] = fm4(st_h[c])
        m = dict(common)
        m["xT"] = np.ascontiguousarray(xs.T)
        m["cvec"] = cvc
        m["xrst"] = np.ascontiguousarray(st_conv[c].T.reshape(4, 128, 3).transpose(1, 0, 2)).reshape(128, 12)
        m["dwst"] = np.ascontiguousarray(st_dw[c].T.reshape(4, 128, 30).transpose(1, 0, 2)).reshape(128, 120)
        maps.append(m)
    return maps


def kernel(**inputs):
    if "nc" not in _NC_CACHE:
        _NC_CACHE["nc"] = build_program()
    nc = _NC_CACHE["nc"]
    maps = _host_inputs(inputs)
    res = run_bass_kernel_spmd(nc, maps, core_ids=list(range(8)))
    outs = res.results
    f32 = np.float32
    y_prompt = np.zeros((4, SEQ, D), f32)
    y_sample = np.zeros((8, SMP, D), f32)
    conv_p = np.zeros((1, 4, 3, DR), f32)
    h_p = np.zeros((1, 4, DR), f32)
    dw_p = np.zeros((1, 4, 30, DR), f32)
    conv_s = np.zeros((1, 8, 3, DR), f32)
    h_s = np.zeros((1, 8, DR), f32)
    dw_s = np.zeros((1, 8, 30, DR), f32)
    for c in range(8):
        j, p = divmod(c, 2)
        yT = np.asarray(outs[c]["yT"], dtype=f32)
        y_prompt[j, p * MAIN:(p + 1) * MAIN] = yT[:, :MAIN].T
        y_sample[c] = yT[:, MAIN:].T
        st = np.asarray(outs[c]["st"], dtype=f32).reshape(128, 2, 4, 34)
        fm = lambda a: a.transpose(2, 1, 0).reshape(a.shape[2], 512)
        if p == 1:
            conv_p[0, j] = fm(st[:, 0, :, 0:3])
            h_p[0, j] = fm(st[:, 0, :, 3:4])[0]
            dw_p[0, j] = fm(st[:, 0, :, 4:34])
        conv_s[0, c] = fm(st[:, 1, :, 0:3])
        h_s[0, c] = fm(st[:, 1, :, 3:4])[0]
        dw_s[0, c] = fm(st[:, 1, :, 4:34])
    return (y_prompt, y_sample, conv_p, h_p, dw_p, conv_s, h_s, dw_s)
```

```python
import contextlib
import numpy as np
import concourse.bass as bass
import concourse.mybir as mybir
from concourse.bass_utils import run_bass_kernel_spmd

F32 = mybir.dt.float32
BF16 = mybir.dt.bfloat16
AF = mybir.ActivationFunctionType
ALU = mybir.AluOpType

D = 1024
DR = 512
DFF = 4096
NMETA = 16
SEQ = 4096
PRE = 2064
HALO = 48
MAIN = 2048
SMP = 64
C_PRE, C_HALO, C_MAIN, C_SMP = 0, PRE, PRE + HALO, PRE + HALO + MAIN
NTOK = PRE + HALO + MAIN + SMP
NOUT = MAIN + SMP
EPS = 1e-6

CV_GMIX, CV_GMLP, CV_GFIN = 0, 8, 16
CV_CB, CV_BR, CV_BI, CV_LAM, CV_DWB, CV_LNG, CV_LNB, CV_GR, CV_GC = 24, 28, 32, 36, 40, 44, 48, 52, 56
CV_RW, CV_DW = 60, 76
CV_FLAG, CV_H0, CV_EPS = 200, 201, 205
CV_C, CV_CH, CV_BRH, CV_BIH = 206, 210, 214, 218
CV_ONEP = 222
NV = 224


_SIM = {}
DEF_COST = {"pe": 2.0, "act": 0.6, "dve": 0.65, "pool": 0.9, "sp": 0.3}
HOP = 0.45
LIST_SCHED = True
PE_COLD = 1.0
FILL_WARM = True
FILL_LAST_SEG = 2
LNEXP_SQRT = True
PRENORM_BOOST = 1000.0
INPROJ_BOOST = 0.0
FILL_DUR = 0.25


def pe_c(nmm, n):
    return nmm * (0.045 + 0.205 * n / 512.0)


class _Rec:
    def __init__(self):
        self.calls = []

    def __getattr__(self, name):
        def f(*a, **k):
            self.calls.append((name, a, k))
            return self
        return f

    def then_inc(self, *a, **k):
        return self


def _fd(ap):
    try:
        return int(ap.free_size())
    except Exception:
        return 512


def _is_psum(ap):
    try:
        return "psum" in str(ap.space).lower() or "PSUM" in str(ap.space)
    except Exception:
        return False


_ASETS = {"Exp": frozenset([0, 6]), "Tanh": frozenset([0, 11, 18]), "Ln": frozenset([6]), "Sqrt": frozenset([3]),
          "Gelu_apprx_tanh": frozenset([11]), "Silu": frozenset([18])}
TBL_LOAD = 1.3
TBL_CHOICE = 1.0


def estimate_cost(eng, fn):
    r = _Rec()
    try:
        fn(r)
    except Exception:
        return None
    occ = 0.0
    lat_extra = 0.0
    aset = None
    for name, a, k in r.calls:
        out = k.get("out", a[0] if a else None)
        fd = _fd(out) if out is not None else 512
        if name == "matmul":
            rhs = k.get("rhs", a[2] if len(a) > 2 else None)
            nn = _fd(rhs) if rhs is not None else 512
            occ += max(0.055, 0.012 + 0.215 * nn / 512.0)
        elif name == "dma_start":
            try:
                nbytes = out.size() * (4 if out.dtype == F32 else 2)
            except Exception:
                nbytes = 1 << 20
            occ += 0.8 if eng == "pool" else 0.5
            lat_extra = 2.0 + nbytes / 200e3
        elif name == "activation":
            fname = str(k.get("func", "")).split(".")[-1]
            aset = _ASETS.get(fname)
            occ += 0.13 + 0.00082 * fd + (0.08 if not isinstance(k.get("scale", 1.0), (int, float)) else 0.0)
        elif eng == "dve":
            if name == "tensor_tensor_scan":
                occ += 0.2 + 0.0021 * fd
            elif name == "scalar_tensor_tensor":
                occ += 0.15 + 0.00105 * fd
            elif name == "tensor_tensor":
                occ += 0.1 + 0.001 * fd
            elif name == "tensor_scalar":
                occ += 0.12 + 0.00055 * fd
            elif name == "tensor_copy":
                src = k.get("in_", a[1] if len(a) > 1 else None)
                occ += 0.1 + (0.001 if (src is not None and _is_psum(src)) else 0.00065) * fd
            elif name == "reciprocal":
                occ += 0.1 + 0.0052 * fd
            else:
                occ += 0.06 + 0.0005 * fd
        elif eng == "pool":
            if name == "tensor_tensor":
                occ += 0.15 + 0.0017 * fd
            elif name == "tensor_scalar":
                occ += 0.45 + 0.0004 * fd
            elif name == "tensor_copy":
                occ += 0.12 + 0.0027 * fd
            else:
                occ += 0.06 + 0.0005 * fd
        else:
            occ += 0.3
    if occ == 0.0:
        return None
    return occ, (lat_extra if lat_extra else occ), aset


class VB:
    _n = 0

    def __init__(self, default_t):
        VB._n += 1
        self.id = VB._n
        self.key = f"VB#{self.id}"
        self.t = default_t

    def __getitem__(self, idx):
        return self.t[idx]

    def __getattr__(self, name):
        return getattr(self.t, name)


class Sched:
    def __init__(self, sem_names):
        self.sem_names = list(sem_names)
        self.ops = []
        self.events = []
        self.cnt = {k: 0 for k in sem_names}
        self.prog = {e: [] for e in ("pe", "act", "dve", "pool", "sp")}
        self.nbank = 0
        self.cur_boost = 0.0

    def op(self, eng, fn, reads=(), writes=(), sem=None, inc=None, c=None, lat=None, pin=False, boost=None):
        if sem is None:
            sem = eng
        if inc is None:
            inc = 16 if sem.startswith("d_") else 1
        if c is None:
            c = DEF_COST[eng]
        if lat is None:
            lat = 6.0 if sem.startswith("d_") else c
        self.ops.append(dict(eng=eng, fn=fn, reads=list(reads), writes=list(writes), sem=sem, inc=inc, c=c, lat=lat, pin=pin, boost=(self.cur_boost if boost is None else boost)))
        self.events.append(("op", len(self.ops) - 1))

    def fence(self, key):
        self.events.append(("fence", key))

    def settle(self, keys, sem):
        self.events.append(("settle", list(keys), sem))

    def finalize(self):
        ops = self.ops
        n = len(ops)
        for o in ops:
            est = estimate_cost(o["eng"], o["fn"])
            o["aset"] = None
            if est is not None:
                o["c"], o["lat"], o["aset"] = est
                if o["sem"].startswith("d_"):
                    o["lat"] = max(o["lat"], 2.0)
        if PE_COLD != 1.0:
            for ev in self.events:
                if ev[0] == "fence":
                    break
                if ev[0] == "op" and ops[ev[1]]["eng"] == "pe":
                    ops[ev[1]]["c"] *= PE_COLD
                    ops[ev[1]]["lat"] *= PE_COLD
        deps = [set() for _ in range(n)]
        last_w, readers, sem_ops = {}, {}, {}
        last_rec = {}
        pin_only = set()
        order = {e: [] for e in self.prog}
        finish = [0.0] * n
        eng_free = {e: 0.0 for e in self.prog}
        seg = []
        vb_alloc, vb_users = {}, {}
        for i, o in enumerate(ops):
            for k in set(o["reads"] + o["writes"]):
                if k.startswith("VB#"):
                    if k not in vb_alloc:
                        assert k in o["writes"], k
                        vb_alloc[k] = i
                        vb_users[k] = 0
                    else:
                        vb_users[k] += 1
        op_alloc = {i: k for k, i in vb_alloc.items()}
        phys = [dict(vb=None, ops=[], left=0) for _ in range(7)]
        vb_phys = {}
        seg_idx = [0]
        nfill = [0]
        self.fill_dep = None
        for i_, o_ in enumerate(ops):
            if "fill_src" in o_["writes"]:
                self.fill_dep = i_
        tbl = [None]
        nload = [0]

        def tbl_pen(o):
            a = o.get("aset")
            if a is None or tbl[0] is None:
                return 0.0 if a is None else TBL_LOAD
            return 0.0 if (a & tbl[0]) else TBL_LOAD

        def tbl_upd(o):
            a = o.get("aset")
            if a is None:
                return
            if tbl[0] is not None and (a & tbl[0]):
                tbl[0] = a & tbl[0]
            else:
                tbl[0] = a
                nload[0] += 1

        def sched_segment(seg):
            if not seg:
                return
            if not LIST_SCHED:
                for i in seg:
                    o = ops[i]
                    dr = max([finish[d] + HOP for d in deps[i]] + [0.0])
                    pen = tbl_pen(o) if o["eng"] == "act" else 0.0
                    st = max(eng_free[o["eng"]], dr) + pen
                    if o["eng"] == "act":
                        tbl_upd(o)
                    eng_free[o["eng"]] = st + o["c"]
                    finish[i] = st + o["lat"]
                    order[o["eng"]].append(i)
                return
            inseg = set(seg)
            succ = {i: [] for i in seg}
            indeg = {}
            for i in seg:
                k = 0
                for d in deps[i]:
                    if d in inseg:
                        succ[d].append(i)
                        k += 1
                indeg[i] = k
            prio = {}
            for i in reversed(seg):
                prio[i] = ops[i]["lat"] + max([prio[s_] + HOP for s_ in succ[i]] + [0.0])
            _SIM.setdefault("cp", []).append(round(max(prio.values()), 1))
            for i in seg:
                prio[i] += ops[i].get("boost", 0.0)
            dep_ready = {}
            ready = {e: [] for e in self.prog}

            def make_ready(i):
                dr = 0.0
                for d in deps[i]:
                    dr = max(dr, finish[d] + (HOP if ops[d]["eng"] != ops[i]["eng"] else 0.1))
                dep_ready[i] = dr
                ready[ops[i]["eng"]].append(i)
            for i in seg:
                if indeg[i] == 0:
                    make_ready(i)
            left = len(seg)
            while left:
                best = None
                for e, lst in ready.items():
                    if not lst:
                        continue
                    ef = eng_free[e]
                    for i in lst:
                        est = dep_ready[i] if dep_ready[i] > ef else ef
                        tp_ = 0.0
                        if e == "act":
                            tp_ = tbl_pen(ops[i])
                            est += tp_ * TBL_CHOICE
                        bsel = None
                        if i in op_alloc:
                            bt = None
                            for b, ph in enumerate(phys):
                                if ph["vb"] is None or ph["left"] == 0:
                                    t_ = max([finish[d] + HOP for d in ph["ops"]] + [0.0])
                                    if bt is None or t_ < bt:
                                        bt, bsel = t_, b
                            if bsel is None:
                                continue
                            if bt > est:
                                est = bt
                        key = (est, -prio[i], i)
                        if best is None or key < best[0]:
                            best = (key, e, i, bsel, est - tp_ * (TBL_CHOICE - 1.0))
                if best is None:
                    raise RuntimeError("scheduler: no PSUM bank available (deadlock)")
                key, e, i, bsel, st_real = best
                if bsel is not None:
                    ph = phys[bsel]
                    for d in ph["ops"]:
                        deps[i].add(d)
                    k_ = op_alloc[i]
                    self.vbs[k_].t = self.psb[bsel]
                    vb_phys[k_] = bsel
                    ph["vb"], ph["ops"], ph["left"] = k_, [i], vb_users[k_]
                for k_ in set(ops[i]["reads"] + ops[i]["writes"]):
                    if k_.startswith("VB#") and vb_alloc[k_] != i:
                        ph = phys[vb_phys[k_]]
                        ph["ops"].append(i)
                        ph["left"] -= 1
                ready[e].remove(i)
                o = ops[i]
                st = st_real
                if e == "pe" and FILL_WARM and seg_idx[0] <= FILL_LAST_SEG and self.fill_dep is not None and st - eng_free["pe"] > 2 * FILL_DUR and eng_free["pe"] > 0:
                    k_ = int((st - eng_free["pe"]) / FILL_DUR) - 1
                    t_ = eng_free["pe"]
                    for _ in range(k_):
                        ops.append(dict(eng="pe", fn=self.fill_fn, reads=[], writes=[], sem="pe", inc=1, c=FILL_DUR, lat=FILL_DUR, pin=False, aset=None))
                        deps.append(set([self.fill_dep]))
                        finish.append(t_ + FILL_DUR)
                        order["pe"].append(len(ops) - 1)
                        t_ += FILL_DUR
                        nfill[0] += 1
                if e == "act":
                    tbl_upd(o)
                eng_free[e] = st + o["c"]
                finish[i] = st + o["lat"]
                order[e].append(i)
                left -= 1
                for s_ in succ[i]:
                    indeg[s_] -= 1
                    if indeg[s_] == 0:
                        make_ready(s_)

        for ev in self.events:
            if ev[0] == "op":
                i = ev[1]
                o = ops[i]
                for k in o["reads"] + o["writes"]:
                    for d in last_w.get(k, ()):
                        deps[i].add(d)
                for k in o["writes"]:
                    for d in readers.get(k, ()):
                        deps[i].add(d)
                if o["pin"] and last_rec.get(o["eng"]) is not None:
                    if last_rec[o["eng"]] not in deps[i]:
                        pin_only.add((i, last_rec[o["eng"]]))
                    deps[i].add(last_rec[o["eng"]])
                last_rec[o["eng"]] = i
                deps[i].discard(i)
                for k in o["writes"]:
                    last_w[k] = [i]
                    readers[k] = []
                for k in o["reads"]:
                    readers.setdefault(k, []).append(i)
                sem_ops.setdefault(o["sem"], []).append(i)
                seg.append(i)
            elif ev[0] == "settle":
                for k in ev[1]:
                    last_w[k] = list(sem_ops.get(ev[2], []))
            elif ev[0] == "fence":
                sched_segment(seg)
                seg_idx[0] += 1
                _SIM.setdefault("segs", []).append((ev[1], max([finish[i] for i in seg] + [0.0]), {e: round(sum(ops[i]["c"] for i in seg if ops[i]["eng"] == e), 1) for e in self.prog}))
                seg = []
                last_w[ev[1]] = [order[e][-1] for e in ("pe", "act", "dve", "pool") if order[e]]
                readers[ev[1]] = []
        sched_segment(seg)
        self.sim_time = max(finish) if finish else 0.0
        _SIM["nload"] = nload[0]
        self.sim_finish = finish
        self.sim_order = order
        self.sim_deps = deps
        _SIM["S"] = self
        n = len(ops)
        tok = [None] * n
        cnt = {k: 0 for k in self.sem_names}
        _SIM["nfill"] = nfill[0]
        for e in self.prog:
            for i in order[e]:
                o = ops[i]
                cnt[o["sem"]] += o["inc"]
                tok[i] = (o["sem"], cnt[o["sem"]])
        self.cnt = cnt
        for e in self.prog:
            waited = {}
            for i in order[e]:
                o = ops[i]
                w = {}
                for d in deps[i]:
                    if (i, d) in pin_only:
                        continue
                    s_, v = tok[d]
                    if waited.get(s_, 0) < v:
                        w[s_] = max(w.get(s_, 0), v)
                for s_, v in w.items():
                    waited[s_] = v
                self.prog[e].append((list(w.items()), o["fn"], o["sem"], o["inc"]))

    def emit(self, block, sems, final_waits=()):
        self.finalize()
        _SIM["t"] = self.sim_time

        def run(engname, e):
            for (waits, fn, sem, inc) in self.prog[engname]:
                for (s, v) in waits:
                    e.wait_ge(sems[s], v)
                fn(e).then_inc(sems[sem], inc)

        @block.tensor
        def _(e):
            run("pe", e)

        @block.scalar
        def _(e):
            run("act", e)

        @block.vector
        def _(e):
            run("dve", e)

        @block.gpsimd
        def _(e):
            run("pool", e)

        @block.sync
        def _(e):
            run("sp", e)
            for s in final_waits:
                if self.cnt[s] > 0:
                    e.wait_ge(sems[s], self.cnt[s])


def build_program():
    nc = bass.Bass("TRN2", target_bir_lowering=False)
    dram = lambda name, shape, dt, kind="Internal": nc.dram_tensor(name, shape, dt, kind=kind).ap()
    xT = dram("xT", [D, NTOK], F32, "ExternalInput")
    cvec_d = dram("cvec", [128, NV], F32, "ExternalInput")
    ident_d = dram("ident", [128, 128], F32, "ExternalInput")
    wg_d = dram("wg", [128, 8 * 128], F32, "ExternalInput")
    xrst_d = dram("xrst", [128, 12], F32, "ExternalInput")
    dwst_d = dram("dwst", [128, 120], F32, "ExternalInput")
    wina_d = dram("w_in_a", [128, 8 * 512], F32, "ExternalInput")
    winb_d = dram("w_in_b", [128, 8 * 1536], F32, "ExternalInput")
    wout_d = dram("w_out", [128, 8 * 1024], F32, "ExternalInput")
    wup_d = dram("w_up", [8, 128, 8 * 512], F32, "ExternalInput")
    wdn_d = dram("w_dn", [8, 128, 32 * 128], F32, "ExternalInput")
    yT = dram("yT", [D, NOUT], F32, "ExternalOutput")
    st_d = dram("st", [128, 2 * 4 * 34], F32, "ExternalOutput")
    x1s = dram("x1s", [D, NOUT], F32)
    wup_b = dram("wup_b", [8, 128, 8 * 512], BF16)
    wdn_b = dram("wdn_b", [8, 128, 32 * 128], BF16)
    xT_v = xT.rearrange("(k p) t -> p k t", p=128)
    yT_v = yT.rearrange("(k p) t -> p k t", p=128)
    x1_v = x1s.rearrange("(k p) t -> p k t", p=128)

    with contextlib.ExitStack() as es:
        def sb(name, shape, dt):
            return es.enter_context(nc.sbuf_tensor(name, shape, dt))

        cvec = sb("cvec_s", [128, NV], F32)
        ident = sb("ident_s", [128, 128], F32)
        onesM = sb("onesM", [128, 128], BF16)
        onesC = sb("onesC", [128, 128], BF16)
        Dr = sb("Dr", [128, 16, 128], BF16)
        Ddw = sb("Ddw", [128, 124, 128], BF16)
        Wg = sb("Wg", [128, 8, 128], BF16)
        xrb = sb("xrb", [128, 4, 3 + 512], BF16)
        vb = sb("vb", [128, 4, 30 + 512], BF16)
        hcar = sb("hcar", [128, 4], F32)
        stt = sb("stt", [128, 2, 4, 34], F32)
        xrst = sb("xrst_s", [128, 4, 3], F32)
        dwst = sb("dwst_s", [128, 4, 30], F32)
        tiny = sb("tiny", [128, 8], F32)
        fill_src = sb("fill_src", [128, 512], BF16)
        xb = [sb(f"xb{i}", [128, 8, 512], F32) for i in range(2)]
        hn = sb("hn", [128, 8, 512], BF16)
        hn2 = sb("hn2", [128, 8, 512], BF16)
        rs = sb("rs", [128, 512], F32)
        R1_BYTES = 110592
        r1 = sb("r1", [128, R1_BYTES // 2], BF16)
        cur = [0]

        def carve(shape, dt, reset=None):
            if reset is not None:
                cur[0] = reset
            n = int(np.prod(shape))
            nb = n * (4 if dt == F32 else 2)
            off = cur[0]
            cur[0] += nb
            assert cur[0] <= R1_BYTES, cur[0]
            ap = r1[:, off // 2:(off + nb) // 2]
            if dt == F32:
                ap = ap.bitcast(F32)
            if len(shape) == 2:
                ap = ap.rearrange("p (a b) -> p a b", a=shape[0])
            elif len(shape) == 3:
                ap = ap.rearrange("p (a b c) -> p a b c", a=shape[0], b=shape[1])
            return ap

        def view(off, shape, dt):
            n = int(np.prod(shape))
            nb = n * (4 if dt == F32 else 2)
            assert off + nb <= R1_BYTES, (off, nb)
            ap = r1[:, off // 2:(off + nb) // 2]
            if dt == F32:
                ap = ap.bitcast(F32)
            if len(shape) == 2:
                ap = ap.rearrange("p (a b) -> p a b", a=shape[0])
            return ap

        w_in = view(0, [8, 2048], BF16)
        w_out = view(32768, [8, 1024], BF16)
        TP0 = 49152
        tp = [view(TP0 + 2048 * i, [512], F32) for i in range(20)]
        gg = view(TP0 + 2048 * 12, [4, 512], F32)
        vc = view(TP0 + 2048 * 16, [4, 512], F32)
        xcb = view(90112, [4, 512], BF16)
        ys = view(94208, [8, 512], BF16)
        mix = view(102400, [8, 512], BF16)
        NWU, NWD = 4, 3
        wu = [view(8192 * i, [8, 512], BF16) for i in range(NWU)]
        hT = view(32768, [32, 512], BF16)
        wd = [view(65536 + 8192 * i, [32, 128], BF16) for i in range(NWD)]
        t_relu = view(90112, [3, 512], F32)

        psb = [es.enter_context(nc.psum_tensor(f"ps{i}", [128, 512], F32)) for i in range(8)]

        sem_names = (["pe", "act", "dve", "pool", "sp", "d_c", "d_cast", "d_castb", "d_casto", "d_cast2", "d_x0", "d_x1", "d_s0", "d_s1", "d_o"]
                     + [f"d_wu{i}" for i in range(NWU)] + [f"d_wd{i}" for i in range(NWD)])
        sems = {n: es.enter_context(nc.semaphore(n)) for n in sem_names}
        block = es.enter_context(nc.Block())
        S = Sched(sem_names)
        S.psb = psb
        S.vbs = {}
        S.fill_fn = lambda e: e.matmul(psb[7][:, :], lhsT=onesM[:], rhs=fill_src[:], start=True, stop=True)

        def bank(pool="s"):
            vb = VB(psb[0])
            S.vbs[vb.key] = vb
            return vb, vb.key

        _aliased = set()

        def alias_once(buf, keys):
            if buf in _aliased:
                return []
            _aliased.add(buf)
            return list(keys)

        def cv(c, n=1):
            return cvec[:, c:c + n]

        def drain(gen):
            for _ in gen:
                pass

        def merge(primary, secondary, ratio=2):
            p_alive = s_alive = True
            while p_alive or s_alive:
                for _ in range(ratio):
                    if p_alive:
                        try:
                            next(primary)
                        except StopIteration:
                            p_alive = False
                if s_alive:
                    try:
                        next(secondary)
                    except StopIteration:
                        s_alive = False

        S.op("sp", lambda e: e.dma_start(out=cvec[:], in_=cvec_d), writes=["cvec"], sem="d_c", boost=5000.0)
        S.op("sp", lambda e: e.dma_start(out=ident[:], in_=ident_d), writes=["ident"], sem="d_c", boost=5000.0)
        S.op("sp", lambda e: e.dma_start(out=xrst[:].rearrange("p a b -> p (a b)"), in_=xrst_d), writes=["xrst"], sem="d_c", boost=5000.0)
        S.op("sp", lambda e: e.dma_start(out=dwst[:].rearrange("p a b -> p (a b)"), in_=dwst_d), writes=["dwst"], sem="d_c", boost=5000.0)
        S.settle(["cvec", "ident", "xrst", "dwst"], "d_c")
        S.op("pool", lambda e: e.dma_start(out=Wg[:].rearrange("p a b -> p (a b)"), in_=wg_d, max_dma_last_dim=4096), writes=["Wg"], sem="d_cast")
        S.op("pool", lambda e: e.dma_start(out=w_in[:, :, 0:512], in_=wina_d.rearrange("p (k c) -> p k c", k=8), max_dma_last_dim=4096),
             writes=["w_in_a"], sem="d_cast")
        S.settle(["Wg", "w_in_a"], "d_cast")
        def cast_wout():
            for h in range(4):
                S.op("pool", lambda e, h=h: e.dma_start(out=w_in[:, 2 * h:2 * h + 2, 512:2048],
                                                        in_=winb_d.rearrange("p (k c) -> p k c", k=8)[:, 2 * h:2 * h + 2, :], max_dma_last_dim=8192),
                     writes=[f"w_in_b.{h}"], sem="d_castb", pin=True, c=0.8)
            S.settle(["w_in_b"], "d_castb")
            for h in range(2):
                S.op("pool", lambda e, h=h: e.dma_start(out=w_out[:, 4 * h:4 * h + 4, :].rearrange("p a b -> p (a b)"),
                                                        in_=wout_d[:, h * 4096:(h + 1) * 4096], max_dma_last_dim=4096),
                     writes=[f"w_out.{h}"], sem="d_casto", pin=True, c=0.8)
            S.settle(["w_out"], "d_casto")

        d2d_list = [("u", g) for g in range(8)] + [("d", g) for g in range(8)]

        def cast_d2d(k):
            for _ in range(k):
                if not d2d_list:
                    return
                kind, g = d2d_list.pop(0)
                if kind == "u":
                    S.op("pool", lambda e, g=g: e.dma_start(out=wup_b[g], in_=wup_d[g], max_dma_last_dim=4096), sem="d_cast2", pin=True, c=0.8)
                else:
                    S.op("pool", lambda e, g=g: e.dma_start(out=wdn_b[g], in_=wdn_d[g], max_dma_last_dim=4096), sem="d_cast2", pin=True, c=0.8)

        ddw_next = [0]

        def build_ddw(k):
            return

        S.op("dve", lambda e: e.memset(onesM[:], 1.0 / D), writes=["onesM"])
        S.op("dve", lambda e: e.memset(fill_src[:], 0.5), writes=["fill_src"])
        S.op("dve", lambda e: e.memset(onesC[:], 1.0 / DR), writes=["onesC"])
        S.op("dve", lambda e: e.memset(xrb[:], 0.0), writes=[f"xrb.{j}" for j in range(4)])
        S.op("dve", lambda e: e.memset(vb[:], 0.0), writes=[f"vb.{j}" for j in range(4)])
        S.op("dve", lambda e: e.memset(hcar[:], 0.0), writes=["hcar"])
        S.op("dve", lambda e: e.memset(stt[:], 0.0), writes=["stt"])
        S.op("act", lambda e: e.activation(out=tiny[:, 0:4], in_=cv(CV_LAM, 4), func=AF.Exp, scale=-1.0), reads=["cvec"], writes=["tiny"])
        S.op("act", lambda e: e.activation(out=tiny[:, 4:8], in_=tiny[:, 0:4], func=AF.Ln, bias=1.0), reads=["tiny"], writes=["tiny2"])
        S.op("dve", lambda e: e.tensor_scalar(out=cv(CV_C, 4), in0=tiny[:, 4:8], scalar1=-8.0, scalar2=None, op0=ALU.mult), reads=["tiny2"], writes=["cvc"])
        S.op("dve", lambda e: e.tensor_scalar(out=cv(CV_CH, 4), in0=tiny[:, 4:8], scalar1=-4.0, scalar2=None, op0=ALU.mult), reads=["tiny2"], writes=["cvc"])
        S.op("dve", lambda e: e.tensor_scalar(out=cv(CV_BRH, 8), in0=cv(CV_BR, 8), scalar1=0.5, scalar2=None, op0=ALU.mult), reads=["cvec"], writes=["cvc"])
        S.op("dve", lambda e: e.tensor_tensor(
                 out=Dr[:, :, :],
                 in0=ident[:].unsqueeze(1).broadcast_to([128, 16, 128]),
                 in1=cvec[:, CV_RW:CV_RW + 16].unsqueeze(2).broadcast_to([128, 16, 128]),
                 op=ALU.mult),
             reads=["ident", "cvec"], writes=["Dr"])
        for j in range(4):
            S.op("dve", lambda e, j=j: e.scalar_tensor_tensor(
                     out=Ddw[:, j * 31:(j + 1) * 31, :],
                     in0=ident[:].unsqueeze(1).broadcast_to([128, 31, 128]), scalar=0.5,
                     in1=cvec[:, CV_DW + j * 31:CV_DW + (j + 1) * 31].unsqueeze(2).broadcast_to([128, 31, 128]),
                     op0=ALU.mult, op1=ALU.mult),
                 reads=["ident", "cvec"], writes=[f"Ddw.{j}"])

        def g_load(src_v, c0, n, slot, extra_reads=()):
            S.op("sp", lambda e: e.dma_start(out=xb[slot][:, :, :n], in_=src_v[:, :, c0:c0 + n]),
                 reads=list(extra_reads), writes=[f"xb{slot}"], sem=f"d_x{slot}", boost=PRENORM_BOOST)
            yield

        def g_rstd(ps, pk, n, out_t, out_k, boost=0.0):
            S.op("act", lambda e: e.activation(out=out_t[:, :n], in_=ps[:, :n], func=AF.Ln, bias=cv(CV_EPS)), reads=[pk, "cvec"], writes=[out_k], boost=boost)
            yield
            S.op("act", lambda e: e.activation(out=out_t[:, :n], in_=out_t[:, :n], func=AF.Exp, scale=-0.5), reads=[out_k], writes=[out_k], boost=boost)
            yield

        HBUF = [hn, hn2]
        HKEY = [[f"hn.{k}" for k in range(8)], [f"hn2.{k}" for k in range(8)]]
        HN = HKEY[0]

        def g_prenorm(slot, n, gcol, sqbuf, sqkeys, extra=(), hb=0):
            hn = HBUF[hb]
            if sqbuf is None:
                sqbuf, sqkeys = hn, HKEY[hb]
            x = xb[slot]
            xk = f"xb{slot}"
            S.op("act", lambda e: e.activation(out=sqbuf[:, :, :n], in_=x[:, :, :n], func=AF.Square), reads=[xk] + list(extra), writes=sqkeys, c=0.2 + 0.0008 * 8 * n, boost=PRENORM_BOOST)
            yield
            ps, pk = bank()

            def mm(e):
                for k in range(8):
                    ins = e.matmul(ps[:, :n], lhsT=onesM[:], rhs=sqbuf[:, k, :n], start=(k == 0), stop=(k == 7))
                return ins
            S.op("pe", mm, reads=sqkeys + ["onesM"], writes=[pk], c=pe_c(8, n), boost=PRENORM_BOOST)
            yield
            yield from g_rstd(ps, pk, n, rs, "rs", boost=PRENORM_BOOST)
            for k in range(8):
                S.op("dve", lambda e, k=k: e.scalar_tensor_tensor(out=hn[:, k, :n], in0=x[:, k, :n], scalar=cv(gcol + k), in1=rs[:, :n],
                                                                   op0=ALU.mult, op1=ALU.mult),
                     reads=[xk, "rs", "cvec"], writes=[HKEY[hb][k]], boost=PRENORM_BOOST)
                yield

        def inproj(m, n, pool="s", hb=0):
            hn = HBUF[hb]
            HN = HKEY[hb]
            ps, pk = bank(pool)

            def mm(e):
                for k in range(8):
                    ins = e.matmul(ps[:, :n], lhsT=w_in[:, k, m * 128:(m + 1) * 128], rhs=hn[:, k, :n], start=(k == 0), stop=(k == 7))
                return ins
            S.op("pe", mm, reads=HN + ["w_in_a" if m < 4 else "w_in_b"], writes=[pk], c=pe_c(8, n), boost=INPROJ_BOOST)
            return ps, pk

        def g_xr(n, seg, p_mode=False, hb=0):
            for j in range(4):
                ps, pk = inproj(j, n, hb=hb)
                yield
                if p_mode:
                    S.op("dve", lambda e, j=j, ps=ps: e.tensor_copy(out=xrb[:, j, 3:3 + n], in_=ps[:, :n]), reads=[pk], writes=[f"xrb.{j}"])
                else:
                    S.op("act", lambda e, j=j, ps=ps: e.activation(out=xrb[:, j, 3:3 + n], in_=ps[:, :n], func=AF.Copy), reads=[pk], writes=[f"xrb.{j}"])
                yield
                if seg is not None:
                    S.op("act", lambda e, j=j, ps=ps: e.activation(out=stt[:, seg, j, 0:3], in_=ps[:, n - 3:n], func=AF.Copy), reads=[pk], writes=["stt"])
                    yield

        def g_xr_issue(n, store, hb=0):
            for j in range(4):
                store[j] = inproj(j, n, pool="l", hb=hb)
                yield

        def g_xr_evac(n, store):
            for j in range(4):
                ps, pk = store[j]
                S.op("dve", lambda e, j=j, ps=ps: e.tensor_copy(out=xrb[:, j, 3:3 + n], in_=ps[:, :n]), reads=[pk], writes=[f"xrb.{j}"])
                yield

        def g_glu(n, seg, hb=0):
            for j in range(4):
                pg, pgk = inproj(12 + j, n, hb=hb)
                yield
                tg = tp[10 + j % 2]
                tgk = f"tp{10 + j % 2}"
                S.op("act", lambda e, pg=pg, tg=tg: e.activation(out=tg[:, :n], in_=pg[:, :n], func=AF.Tanh, scale=0.5), reads=[pgk], writes=[tgk])
                yield
                pv, pvk = inproj(8 + j, n, hb=hb)
                yield
                S.op("dve", lambda e, pv=pv, tg=tg, j=j: e.scalar_tensor_tensor(out=vb[:, j, 30:30 + n], in0=tg[:, :n], scalar=1.0, in1=pv[:, :n],
                                                                               op0=ALU.add, op1=ALU.mult),
                     reads=[pvk, tgk], writes=[f"vb.{j}"])
                yield
                if seg is not None:
                    S.op("dve", lambda e, tg=tg: e.tensor_scalar(out=tg[:, n - 30:n], in0=tg[:, n - 30:n], scalar1=0.5, scalar2=0.5, op0=ALU.mult, op1=ALU.add),
                         reads=[tgk], writes=[tgk])
                    S.op("dve", lambda e, pv=pv, tg=tg, j=j: e.tensor_tensor(out=stt[:, seg, j, 4:34], in0=tg[:, n - 30:n], in1=pv[:, n - 30:n], op=ALU.mult),
                         reads=[pvk, tgk], writes=["stt"])
                    yield

        def g_gate(n, hb=0):
            for j in range(4):
                pgt, pgtk = inproj(4 + j, n, hb=hb)
                yield
                S.op("act", lambda e, pgt=pgt, j=j: e.activation(out=gg[:, j, :n], in_=pgt[:, :n], func=AF.Gelu_apprx_tanh), reads=[pgtk, "PA"], writes=[f"tp{12 + j}"])
                yield

        def g_gate_issue(n, store, hb=0):
            for j in range(4):
                store[j] = inproj(4 + j, n, pool="l", hb=hb)
                yield

        def gate_evac(n, store):
            for j in range(4):
                pgt, pgtk = store[j]
                S.op("act", lambda e, pgt=pgt, j=j: e.activation(out=gg[:, j, :n], in_=pgt[:, :n], func=AF.Gelu_apprx_tanh), reads=[pgtk, "PA"], writes=[f"tp{12 + j}"])

        def g_chains(js, sets, n, seg=None, want_y=False, after_gates=None, after_conv=None, p_mode=False):
            J = list(zip(js, sets))
            T = lambda s, i: tp[5 * s + i]
            K = lambda s, i: f"tp{5 * s + i}"
            bk = {}
            for j, s in J:
                ps, pk = bank()
                bk[j] = (ps, pk)

                def mmc(e, j=j, ps=ps):
                    for k in range(4):
                        ins = e.matmul(ps[:, :n], lhsT=Dr[:, j * 4 + k, :], rhs=xrb[:, j, k:k + n], start=(k == 0), stop=(k == 3))
                    return ins
                S.op("pe", mmc, reads=[f"xrb.{j}", "Dr"], writes=[pk], c=pe_c(4, n))
                yield
            if after_conv is not None:
                after_conv()
            for j, s in J:
                ps, pk = bk[j]
                S.op("act", lambda e, j=j, s=s, ps=ps: e.activation(out=T(s, 0)[:, :n], in_=ps[:, :n], func=AF.Identity, bias=cv(CV_CB + j)),
                     reads=[pk, "cvec"], writes=[K(s, 0)])
                yield
                S.op("pool", lambda e, j=j: e.tensor_copy(out=xrb[:, j, 0:3], in_=xrb[:, j, n:n + 3]), reads=[f"xrb.{j}"], writes=[f"xrb.{j}"])
                yield
            for j, s in J:
                S.op("dve", lambda e, s=s: e.tensor_copy(out=xcb[:, s, :n], in_=T(s, 0)[:, :n]), reads=[K(s, 0)], writes=[f"xcb{s}"])
                yield
            gb = {}
            for j, s in J:
                pr, prk = bank()
                S.op("pe", lambda e, j=j, s=s, pr=pr: e.matmul(pr[:, :n], lhsT=Wg[:, j, :], rhs=xcb[:, s, :n], start=True, stop=True), reads=[f"xcb{s}", "Wg"], writes=[prk], c=pe_c(1, n))
                yield
                pi, pik = bank()
                S.op("pe", lambda e, j=j, s=s, pi=pi: e.matmul(pi[:, :n], lhsT=Wg[:, 4 + j, :], rhs=xcb[:, s, :n], start=True, stop=True), reads=[f"xcb{s}", "Wg"], writes=[pik], c=pe_c(1, n))
                yield
                gb[j] = (pr, prk, pi, pik)
                S.op("act", lambda e, j=j, s=s, pr=pr: e.activation(out=T(s, 1)[:, :n], in_=pr[:, :n], func=AF.Tanh, scale=0.5, bias=cv(CV_BRH + j)),
                     reads=[prk, "cvc"], writes=[K(s, 1)])
                yield
                S.op("act", lambda e, j=j, s=s, pi=pi: e.activation(out=T(s, 2)[:, :n], in_=pi[:, :n], func=AF.Tanh, scale=0.5, bias=cv(CV_BIH + j)),
                     reads=[pik, "cvc"], writes=[K(s, 2)])
                yield
            if after_gates is not None:
                after_gates()
            for j, s in J:
                if not p_mode:
                    S.op("act", lambda e, j=j, s=s: e.activation(out=T(s, 3)[:, :n], in_=T(s, 1)[:, :n], func=AF.Exp, scale=cv(CV_C + j), bias=cv(CV_C + j)),
                         reads=[K(s, 1), "cvc"], writes=[K(s, 3)])
                    yield
                S.op("act", lambda e, j=j, s=s: e.activation(out=T(s, 1)[:, :n], in_=T(s, 1)[:, :n], func=AF.Exp, scale=cv(CV_CH + j), bias=cv(CV_CH + j)),
                     reads=[K(s, 1), "cvc"], writes=[K(s, 1)])
                yield
                if p_mode:
                    S.op("dve", lambda e, s=s: e.tensor_tensor(out=T(s, 3)[:, :n], in0=T(s, 1)[:, :n], in1=T(s, 1)[:, :n], op=ALU.mult),
                         reads=[K(s, 1)], writes=[K(s, 3)])
                    yield
            for j, s in J:
                S.op("dve", lambda e, s=s: e.scalar_tensor_tensor(out=T(s, 2)[:, :n], in0=T(s, 2)[:, :n], scalar=1.0, in1=T(s, 0)[:, :n], op0=ALU.add, op1=ALU.mult),
                     reads=[K(s, 2), K(s, 0)], writes=[K(s, 2)])
                yield
            for j, s in J:
                if LNEXP_SQRT:
                    S.op("act", lambda e, s=s: e.activation(out=T(s, 3)[:, :n], in_=T(s, 3)[:, :n], func=AF.Ln, scale=-1.0, bias=1.0), reads=[K(s, 3)], writes=[K(s, 3)])
                    yield
                    S.op("act", lambda e, s=s: e.activation(out=T(s, 3)[:, :n], in_=T(s, 3)[:, :n], func=AF.Exp, scale=0.5), reads=[K(s, 3)], writes=[K(s, 3)])
                else:
                    S.op("act", lambda e, s=s: e.activation(out=T(s, 3)[:, :n], in_=T(s, 3)[:, :n], func=AF.Sqrt, scale=-1.0, bias=1.0), reads=[K(s, 3)], writes=[K(s, 3)])
                yield
            for j, s in J:
                S.op("dve", lambda e, s=s: e.scalar_tensor_tensor(out=T(s, 2)[:, :n], in0=T(s, 2)[:, :n], scalar=0.5, in1=T(s, 3)[:, :n], op0=ALU.mult, op1=ALU.mult),
                     reads=[K(s, 2), K(s, 3)], writes=[K(s, 2)])
                yield
                S.op("dve", lambda e, j=j, s=s: e.tensor_tensor_scan(out=T(s, 4)[:, :n], data0=T(s, 1)[:, :n], data1=T(s, 2)[:, :n], initial=hcar[:, j:j + 1],
                                                                     op0=ALU.mult, op1=ALU.add),
                     reads=[K(s, 1), K(s, 2), "hcar"], writes=[K(s, 4)], c=1.3)
                yield
                S.op("dve", lambda e, j=j, s=s: e.tensor_copy(out=hcar[:, j:j + 1], in_=T(s, 4)[:, n - 1:n]), reads=[K(s, 4)], writes=["hcar"])
                yield
                if seg is not None:
                    S.op("pool", lambda e, j=j, s=s: e.tensor_copy(out=stt[:, seg, j, 3:4], in_=T(s, 4)[:, n - 1:n]), reads=[K(s, 4)], writes=["stt"])
                if want_y:
                    S.op("dve", lambda e, j=j, s=s: e.tensor_tensor(out=gg[:, j, :n], in0=gg[:, j, :n], in1=T(s, 4)[:, :n], op=ALU.mult),
                         reads=[f"tp{12 + j}", K(s, 4)], writes=[f"tp{12 + j}"])
                    yield

        pre_chunks = [(0, 512), (512, 512), (1024, 512), (1536, 512), (2048, 16)]
        xslot = [0]

        def nslot():
            s = xslot[0]
            xslot[0] ^= 1
            return s

        def g_PHa(ci):
            c0, n = pre_chunks[ci]
            slot = nslot()
            yield from g_load(xT_v, C_PRE + c0, n, slot)
            yield from g_prenorm(slot, n, CV_GMIX, None, None, hb=ci % 2)

        xr_store = {}
        drain(g_PHa(0))
        drain(g_xr_issue(pre_chunks[0][1], xr_store, hb=0))
        for ci, (c0, n) in enumerate(pre_chunks):
            drain(g_xr_evac(n, xr_store))
            if ci == 0:
                cast_wout()
            if ci == len(pre_chunks) - 1:
                S.op("dve", lambda e: e.tensor_scalar(out=hcar[:], in0=hcar[:], scalar1=cv(CV_FLAG), scalar2=None, op0=ALU.mult),
                     reads=["hcar", "cvec"], writes=["hcar"])
            ch = g_chains([0, 1, 2, 3], [0, 1, 2, 3], n, p_mode=True)
            if ci + 1 < len(pre_chunks):
                xr_store = {}

                def sec(ci=ci, st=xr_store):
                    yield from g_PHa(ci + 1)
                    yield from g_xr_issue(pre_chunks[ci + 1][1], st, hb=(ci + 1) % 2)
                merge(ch, sec(), ratio=3)
            else:
                drain(ch)
            cast_d2d(2)
            build_ddw(31)

        B_chunks = [(0, 424), (424, 424), (848, 424), (1272, 420), (1692, 420)]
        B_slot = {}
        wu_i = [0]
        wd_i = [0]
        HT = [f"hT.{f}" for f in range(32)]

        def BHB(ci):
            return (ci + len(A_chunks)) % 2

        def g_BH(ci, first=False):
            oc0, n = B_chunks[ci]
            slot = nslot()
            B_slot[ci] = slot
            yield from g_load(x1_v, oc0, n, slot, extra_reads=["x1s"])
            if first:
                yield from g_prenorm(slot, n, CV_GMLP, None, None, hb=BHB(ci))
            else:
                yield from g_prenorm(slot, n, CV_GMLP, mix, ["mixB"], extra=["R1"], hb=BHB(ci))

        def load_wu(g):
            ws = wu_i[0] % NWU
            wu_i[0] += 1
            S.op("sp", lambda e: e.dma_start(out=wu[ws][:].rearrange("p a b -> p (a b)"), in_=wup_b[g]),
                 reads=["wup_b"], writes=[f"wu{ws}"] + alias_once(f"wu{ws}", ["w_in_a", "w_in_b"]), sem=f"d_wu{ws}")
            return ws

        def load_wd(m):
            ws = wd_i[0] % NWD
            wd_i[0] += 1
            S.op("sp", lambda e: e.dma_start(out=wd[ws][:].rearrange("p a b -> p (a b)"), in_=wdn_b[m]),
                 reads=["wdn_b", "R1"], writes=[f"wd{ws}"], sem=f"d_wd{ws}")
            return ws

        def _one_up(ci, g, ws):
            oc0, n = B_chunks[ci]
            for mi in range(4):
                pu, puk = bank("all")

                hnb = HBUF[BHB(ci)]

                def mmu(e, mi=mi, pu=pu, hnb=hnb):
                    for k in range(8):
                        ins = e.matmul(pu[:, :n], lhsT=wu[ws][:, k, mi * 128:(mi + 1) * 128], rhs=hnb[:, k, :n], start=(k == 0), stop=(k == 7))
                    return ins
                S.op("pe", mmu, reads=HKEY[BHB(ci)] + [f"wu{ws}"], writes=[puk], c=pe_c(8, n))
                f = g * 4 + mi
                tb = f % 3
                S.op("act", lambda e, pu=pu, tb=tb: e.activation(out=t_relu[:, tb, :n], in_=pu[:, :n], func=AF.Relu), reads=[puk, "R1"], writes=[f"t_relu{tb}"])
                eng = "dve" if f % 4 != 3 else "pool"
                S.op(eng, lambda e, f=f, tb=tb: e.tensor_tensor(out=hT[:, f, :n], in0=t_relu[:, tb, :n], in1=t_relu[:, tb, :n], op=ALU.mult),
                     reads=[f"t_relu{tb}", "R1"], writes=[f"hT.{f}"])
                yield

        def _one_down(ci, m, ws):
            oc0, n = B_chunks[ci]
            slot = B_slot[ci]
            x = xb[slot]
            xk = f"xb{slot}"
            pd, pdk = bank("all")

            def mmd2(e):
                for k in range(32):
                    ins = e.matmul(pd[:, :n], lhsT=wd[ws][:, k, :], rhs=hT[:, k, :n], start=(k == 0), stop=(k == 31))
                return ins
            S.op("pe", mmd2, reads=HT + [f"wd{ws}"], writes=[pdk], c=pe_c(32, n))
            S.op("dve", lambda e: e.tensor_tensor(out=x[:, m, :n], in0=pd[:, :n], in1=x[:, m, :n], op=ALU.add),
                 reads=[pdk, xk], writes=[xk])
            yield

        def g_Bfinal(ci):
            oc0, n = B_chunks[ci]
            slot = B_slot[ci]
            x = xb[slot]
            xk = f"xb{slot}"
            S.op("act", lambda e: e.activation(out=mix[:, :, :n], in_=x[:, :, :n], func=AF.Square), reads=[xk, "R1"], writes=["mixB"], c=0.2 + 0.0008 * 8 * n)
            yield
            ps, pk = bank("all")

            def mmf(e):
                for k in range(8):
                    ins = e.matmul(ps[:, :n], lhsT=onesM[:], rhs=mix[:, k, :n], start=(k == 0), stop=(k == 7))
                return ins
            S.op("pe", mmf, reads=["mixB", "onesM"], writes=[pk], c=pe_c(8, n))
            yield
            yield from g_rstd(ps, pk, n, rs, "rs")
            for k in range(8):
                S.op("dve", lambda e, k=k: e.scalar_tensor_tensor(out=x[:, k, :n], in0=x[:, k, :n], scalar=cv(CV_GFIN + k), in1=rs[:, :n],
                                                                   op0=ALU.mult, op1=ALU.mult),
                     reads=[xk, "rs", "cvec"], writes=[xk])
                yield
            S.op("sp", lambda e: e.dma_start(out=yT_v[:, :, oc0:oc0 + n], in_=x[:, :, :n]), reads=[xk], sem=f"d_s{slot}")
            yield

        nB = len(B_chunks)
        wu_uses = [(c, g) for c in range(nB) for g in range(8)]
        wd_uses = [(c, m) for c in range(nB) for m in range(8)]
        wu_slot, wd_slot = {}, {}
        wu_ptr, wd_ptr = [0], [0]

        def ensure_wu(idx):
            while wu_ptr[0] <= min(idx, len(wu_uses) - 1):
                c, g = wu_uses[wu_ptr[0]]
                wu_slot[(c, g)] = load_wu(g)
                wu_ptr[0] += 1

        def ensure_wd(idx):
            while wd_ptr[0] <= min(idx, len(wd_uses) - 1):
                c, m = wd_uses[wd_ptr[0]]
                wd_slot[(c, m)] = load_wd(m)
                wd_ptr[0] += 1

        def step(gen, k):
            for _ in range(k):
                try:
                    next(gen)
                except StopIteration:
                    return

        A_chunks = [(C_SMP, SMP, "sample", 1, MAIN), (C_HALO, HALO, "halo", None, None)] \
            + [(C_MAIN + 512 * c, 512, "main", 0 if c == 3 else None, 512 * c) for c in range(4)]
        A_slot = {}

        def g_AH1a(ci):
            c0, n, mode, seg, oc0 = A_chunks[ci]
            if ci == 2:
                slot = A_slot[1]
            else:
                slot = nslot()
            A_slot[ci] = slot
            yield from g_load(xT_v, c0, n, slot)
            yield from g_prenorm(slot, n, CV_GMIX, None, None, hb=ci % 2)

        def g_AH1b(ci):
            c0, n, mode, seg, oc0 = A_chunks[ci]
            if mode == "sample":
                S.op("dve", lambda e: e.tensor_copy(out=tiny[:, 0:4], in_=hcar[:]), reads=["hcar"], writes=["tiny"])
                S.op("dve", lambda e: e.tensor_copy(out=xrb[:, :, 0:3], in_=xrst[:]), reads=["xrst"], writes=[f"xrb.{j}" for j in range(4)])
                S.op("dve", lambda e: e.tensor_scalar(out=vb[:, :, 0:30], in0=dwst[:], scalar1=2.0, scalar2=None, op0=ALU.mult), reads=["dwst"], writes=[f"vb.{j}" for j in range(4)])
                S.op("dve", lambda e: e.tensor_copy(out=hcar[:], in_=cv(CV_H0, 4)), reads=["cvec"], writes=["hcar"])
            yield from g_xr(n, seg, hb=ci % 2)
            yield from g_glu(n, seg, hb=ci % 2)

        def g_c31(j, n, store):
            pc, pck = bank("l")
            store[j] = (pc, pck)

            def mmd(e):
                for k in range(31):
                    ins = e.matmul(pc[:, :n], lhsT=Ddw[:, j * 31 + k, :], rhs=vb[:, j, k:k + n], start=(k == 0), stop=(k == 30))
                return ins
            S.op("pe", mmd, reads=[f"vb.{j}", f"Ddw.{j}"], writes=[pck], c=pe_c(31, n))
            S.op("pool", lambda e: e.tensor_copy(out=vb[:, j, 0:30], in_=vb[:, j, n:n + 30]), reads=[f"vb.{j}"], writes=[f"vb.{j}"])

        def A_M(ci, c31):
            c0, n, mode, seg, oc0 = A_chunks[ci]
            drain(g_chains([0, 1], [0, 1], n, seg=seg, want_y=True, after_conv=lambda: g_c31(0, n, c31), after_gates=lambda: g_c31(1, n, c31)))
            drain(g_chains([2, 3], [0, 1], n, seg=seg, want_y=True, after_conv=lambda: g_c31(2, n, c31), after_gates=lambda: g_c31(3, n, c31)))

        YS = [f"ys.{k}" for k in range(8)]

        def g_AT1(ci, c31):
            c0, n, mode, seg, oc0 = A_chunks[ci]
            mean_t, m2_t, rc_t, rr_t, rk_t = tp[0], tp[1], tp[2], tp[3], tp[4]
            for j in range(4):
                pc, pck = c31[j]
                S.op("act", lambda e, j=j, pc=pc: e.activation(out=vc[:, j, :n], in_=pc[:, :n], func=AF.Identity, bias=cv(CV_DWB + j)),
                     reads=[pck, "cvec", "PA"], writes=[f"tp{16 + j}"])
                yield
                S.op("act", lambda e, j=j, pc=pc: e.activation(out=ys[:, 4 + j, :n], in_=pc[:, :n], func=AF.Square, bias=cv(CV_DWB + j)),
                     reads=[pck, "cvec", "ys"], writes=[f"ys.{4 + j}"])
                yield
                S.op("dve", lambda e, j=j: e.tensor_copy(out=ys[:, j, :n], in_=vc[:, j, :n]), reads=[f"tp{16 + j}", "ys"], writes=[f"ys.{j}"])
                yield
            pm, pmk = bank()

            def mm_mean(e):
                for j in range(4):
                    ins = e.matmul(pm[:, :n], lhsT=onesC[:], rhs=ys[:, j, :n], start=(j == 0), stop=(j == 3))
                return ins
            S.op("pe", mm_mean, reads=YS[0:4] + ["onesC"], writes=[pmk], c=pe_c(4, n))
            yield
            pq, pqk = bank()

            def mm_msq(e):
                for j in range(4):
                    ins = e.matmul(pq[:, :n], lhsT=onesC[:], rhs=ys[:, 4 + j, :n], start=(j == 0), stop=(j == 3))
                return ins
            S.op("pe", mm_msq, reads=YS[4:8] + ["onesC"], writes=[pqk], c=pe_c(4, n))
            yield
            S.op("act", lambda e: e.activation(out=mean_t[:, :n], in_=pm[:, :n], func=AF.Copy), reads=[pmk], writes=["tp0"])
            yield
            S.op("pool", lambda e: e.tensor_tensor(out=m2_t[:, :n], in0=mean_t[:, :n], in1=mean_t[:, :n], op=ALU.mult), reads=["tp0"], writes=["tp1"])
            yield
            S.op("dve", lambda e: e.tensor_tensor(out=m2_t[:, :n], in0=pq[:, :n], in1=m2_t[:, :n], op=ALU.subtract), reads=[pqk, "tp1"], writes=["tp1"])
            yield
            S.op("dve", lambda e: e.tensor_scalar(out=m2_t[:, :n], in0=m2_t[:, :n], scalar1=0.0, scalar2=None, op0=ALU.max), reads=["tp1"], writes=["tp1"])
            yield
            S.op("act", lambda e: e.activation(out=rc_t[:, :n], in_=m2_t[:, :n], func=AF.Ln, bias=cv(CV_EPS)), reads=["tp1", "cvec"], writes=["tp2"])
            yield
            S.op("act", lambda e: e.activation(out=rc_t[:, :n], in_=rc_t[:, :n], func=AF.Exp, scale=-0.5), reads=["tp2"], writes=["tp2"])
            yield
            for j in range(4):
                S.op("pool", lambda e, j=j: e.tensor_tensor(out=vc[:, j, :n], in0=vc[:, j, :n], in1=mean_t[:, :n], op=ALU.subtract),
                     reads=[f"tp{16 + j}", "tp0"], writes=[f"tp{16 + j}"])
                yield
                S.op("dve", lambda e, j=j: e.tensor_tensor(out=vc[:, j, :n], in0=vc[:, j, :n], in1=rc_t[:, :n], op=ALU.mult),
                     reads=[f"tp{16 + j}", "tp2"], writes=[f"tp{16 + j}"])
                yield
            for j in range(4):
                S.op("act", lambda e, j=j: e.activation(out=vc[:, j, :n], in_=vc[:, j, :n], func=AF.Silu, scale=cv(CV_LNG + j), bias=cv(CV_LNB + j)),
                     reads=[f"tp{16 + j}", "cvec"], writes=[f"tp{16 + j}"])
                yield
            S.op("act", lambda e: e.activation(out=ys[:, 0:4, :n], in_=gg[:, :, :n], func=AF.Square), reads=[f"tp{12 + j}" for j in range(4)] + ["ys"], writes=YS[0:4], c=0.2 + 0.0008 * 4 * n)
            yield
            S.op("act", lambda e: e.activation(out=ys[:, 4:8, :n], in_=vc[:, :, :n], func=AF.Square), reads=[f"tp{16 + j}" for j in range(4)] + ["ys"], writes=YS[4:8], c=0.2 + 0.0008 * 4 * n)
            yield
            pr_, prk_ = bank()

            def mm_sr(e):
                for j in range(4):
                    ins = e.matmul(pr_[:, :n], lhsT=onesC[:], rhs=ys[:, j, :n], start=(j == 0), stop=(j == 3))
                return ins
            S.op("pe", mm_sr, reads=YS[0:4] + ["onesC"], writes=[prk_], c=pe_c(4, n))
            yield
            pc_, pck_ = bank()

            def mm_sc(e):
                for j in range(4):
                    ins = e.matmul(pc_[:, :n], lhsT=onesC[:], rhs=ys[:, 4 + j, :n], start=(j == 0), stop=(j == 3))
                return ins
            S.op("pe", mm_sc, reads=YS[4:8] + ["onesC"], writes=[pck_], c=pe_c(4, n))
            yield
            S.op("act", lambda e: e.activation(out=rr_t[:, :n], in_=pr_[:, :n], func=AF.Ln, bias=cv(CV_EPS)), reads=[prk_, "cvec"], writes=["tp3"])
            yield
            S.op("act", lambda e: e.activation(out=rk_t[:, :n], in_=pc_[:, :n], func=AF.Ln, bias=cv(CV_EPS)), reads=[pck_, "cvec"], writes=["tp4"])
            yield
            S.op("act", lambda e: e.activation(out=rr_t[:, :n], in_=rr_t[:, :n], func=AF.Exp, scale=-0.5), reads=["tp3"], writes=["tp3"])
            yield
            S.op("act", lambda e: e.activation(out=rk_t[:, :n], in_=rk_t[:, :n], func=AF.Exp, scale=-0.5), reads=["tp4"], writes=["tp4"])
            yield
            for j in range(4):
                S.op("dve", lambda e, j=j: e.scalar_tensor_tensor(out=mix[:, j, :n], in0=gg[:, j, :n], scalar=cv(CV_GR + j), in1=rr_t[:, :n], op0=ALU.mult, op1=ALU.mult),
                     reads=[f"tp{12 + j}", "tp3", "cvec"], writes=["mix"])
                yield
            for j in range(4):
                S.op("dve", lambda e, j=j: e.scalar_tensor_tensor(out=mix[:, 4 + j, :n], in0=vc[:, j, :n], scalar=cv(CV_GC + j), in1=rk_t[:, :n], op0=ALU.mult, op1=ALU.mult),
                     reads=[f"tp{16 + j}", "tp4", "cvec"], writes=["mix"])
                yield

        def g_AT2(ci):
            c0, n, mode, seg, oc0 = A_chunks[ci]
            slot = A_slot[ci]
            x = xb[slot]
            xk = f"xb{slot}"
            for m in range(8):
                po, pok = bank()

                def mmo(e, m=m, po=po):
                    for k in range(8):
                        ins = e.matmul(po[:, :n], lhsT=w_out[:, k, m * 128:(m + 1) * 128], rhs=mix[:, k, :n], start=(k == 0), stop=(k == 7))
                    return ins
                S.op("pe", mmo, reads=["mix", "w_out"], writes=[pok], c=pe_c(8, n))
                yield
                S.op("dve", lambda e, m=m, po=po: e.tensor_tensor(out=x[:, m, :n], in0=po[:, :n], in1=x[:, m, :n], op=ALU.add),
                     reads=[pok, xk], writes=[xk])
                yield
            S.op("sp", lambda e: e.dma_start(out=x1_v[:, :, oc0:oc0 + n], in_=x[:, :, :n]), reads=[xk], writes=["x1s"], sem=f"d_s{slot}")
            yield

        def halo_tails(n):
            for j in range(4):
                S.op("pool", lambda e, j=j: e.tensor_copy(out=xrb[:, j, 0:3], in_=xrb[:, j, n:n + 3]), reads=[f"xrb.{j}"], writes=[f"xrb.{j}"])
                S.op("pool", lambda e, j=j: e.tensor_copy(out=vb[:, j, 0:30], in_=vb[:, j, n:n + 30]), reads=[f"vb.{j}"], writes=[f"vb.{j}"])

        nA = len(A_chunks)
        drain(g_AH1a(0))
        drain(g_AH1b(0))
        drain(g_gate(A_chunks[0][1], hb=0))
        for ci in [0, 2, 3, 4, 5]:
            n = A_chunks[ci][1]
            c31 = {}
            A_M(ci, c31)
            if ci == 0:
                S.op("dve", lambda e: e.tensor_copy(out=hcar[:], in_=tiny[:, 0:4]), reads=["tiny", "hcar"], writes=["hcar"])
            cast_d2d(2)
            t1 = g_AT1(ci, c31)
            nxt = 2 if ci == 0 else ci + 1
            if nxt < nA:
                gstore = {}

                def head(ci=ci, nxt=nxt, gstore=gstore):
                    if ci == 0:
                        yield from g_AH1a(1)
                        yield from g_AH1b(1)
                        halo_tails(HALO)
                        yield
                    yield from g_AH1a(nxt)
                    yield from g_AH1b(nxt)
                    yield from g_gate_issue(A_chunks[nxt][1], gstore, hb=nxt % 2)
                merge(t1, head(), ratio=1)
                gate_evac(A_chunks[nxt][1], gstore)
                if nxt == nA - 1:
                    cast_d2d(16)
                    S.settle(["wup_b", "wdn_b"], "d_cast2")
                    ensure_wu(NWU - 1)
                drain(g_AT2(ci))
            else:
                b0 = g_BH(0, first=True)
                merge(t1, b0, ratio=2)
                drain(g_AT2(ci))
        S.op("sp", lambda e: e.dma_start(out=st_d, in_=stt[:].rearrange("p a b c -> p (a b c)")), reads=["stt"], sem="d_o")
        cast_d2d(16)
        S.settle(["wup_b", "wdn_b"], "d_cast2")

        S.fence("R1")
        for ci in range(nB):
            oc0, n = B_chunks[ci]
            ensure_wd(8 * ci + NWD - 1)
            nxt = g_BH(ci + 1) if ci + 1 < nB else iter(())
            for g in range(8):
                ensure_wu(8 * ci + g + NWU - 1)
                drain(_one_up(ci, g, wu_slot[(ci, g)]))
                if g == 1:
                    step(nxt, 2)
                if g == 4:
                    step(nxt, 3)
            drain(nxt)
            for m in range(8):
                ensure_wd(8 * ci + m + NWD - 1)
                if m == 2:
                    ensure_wu(8 * (ci + 1) + NWU - 1)
                drain(_one_down(ci, m, wd_slot[(ci, m)]))
            drain(g_Bfinal(ci))

        S.emit(block, sems, final_waits=["d_s0", "d_s1", "d_o"])
    return nc


_NC_CACHE = {}


def _host_inputs(inp):
    f32 = np.float32
    g = lambda k: np.asarray(inp[k], dtype=f32)
    x_prompt, x_sample, meta = g("x_prompt"), g("x_sample"), g("meta_tokens")
    fm8 = lambda v: np.ascontiguousarray(v.reshape(8, 128).T)
    fm4 = lambda v: np.ascontiguousarray(v.reshape(4, 128).T)
    cv = np.zeros((128, NV), f32)
    cv[:, CV_GMIX:CV_GMIX + 8] = fm8(g("norm_mix")[0])
    cv[:, CV_GMLP:CV_GMLP + 8] = fm8(g("norm_mlp")[0])
    cv[:, CV_GFIN:CV_GFIN + 8] = fm8(g("norm_final"))
    for col, key in ((CV_CB, "rnn_conv_b"), (CV_BR, "b_gate_r"), (CV_BI, "b_gate_i"), (CV_LAM, "rglru_lambda"), (CV_DWB, "dw_b"),
                     (CV_LNG, "ln_conv_g"), (CV_LNB, "ln_conv_b"), (CV_GR, "out_norm_rnn"), (CV_GC, "out_norm_conv")):
        cv[:, col:col + 4] = fm4(g(key)[0])
    rw = g("rnn_conv_w")[0]
    dw = g("dw_w")[0]
    for j in range(4):
        cv[:, CV_RW + 4 * j:CV_RW + 4 * j + 4] = rw[:, j * 128:(j + 1) * 128].T
        cv[:, CV_DW + 31 * j:CV_DW + 31 * j + 31] = dw[:, j * 128:(j + 1) * 128].T
    cv[:, CV_EPS] = EPS
    cv[:, CV_ONEP] = np.float32(1.0) + np.float32(1.1920929e-07)
    ident = np.eye(128, dtype=f32)
    wg = np.zeros((128, 8, 128), f32)
    for gi, key in enumerate(("w_gate_r", "w_gate_i")):
        w = g(key)[0]
        for j in range(4):
            for a in range(2):
                wg[64 * a:64 * a + 64, gi * 4 + j, 64 * a:64 * a + 64] = w[2 * j + a]
    wg = wg.reshape(128, 8 * 128)
    blk = lambda w, K, N: np.ascontiguousarray(w.reshape(K // 128, 128, N).transpose(1, 0, 2))
    w_in3 = blk(g("w_in")[0], D, 2048)
    w_in_a = np.ascontiguousarray(w_in3[:, :, 0:512]).reshape(128, -1)
    w_in_b = np.ascontiguousarray(w_in3[:, :, 512:2048]).reshape(128, -1)
    w_out = blk(g("w_out")[0], D, D).reshape(128, -1)
    wu = blk(g("w_up")[0], D, DFF)
    w_up = np.ascontiguousarray(wu.reshape(128, 8, 8, 512).transpose(2, 0, 1, 3)).reshape(8, 128, 8 * 512)
    wd = blk(g("w_down")[0], DFF, D)
    w_dn = np.ascontiguousarray(wd.reshape(128, 32, 8, 128).transpose(2, 0, 1, 3)).reshape(8, 128, 32 * 128)
    common = dict(ident=ident, wg=wg, w_in_a=w_in_a, w_in_b=w_in_b, w_out=w_out, w_up=w_up, w_dn=w_dn)
    st_conv, st_h, st_dw = g("state_rglru_conv")[0], g("state_rglru_h")[0], g("state_dwconv")[0]
    maps = []
    for c in range(8):
        j, p = divmod(c, 2)
        seq = np.concatenate([meta, x_prompt[j]], axis=0)
        xs = np.zeros((NTOK, D), f32)
        if p == 0:
            xs[C_PRE + 2048:C_PRE + 2064] = seq[0:16]
            xs[C_HALO + 32:C_HALO + 48] = seq[0:16]
            xs[C_MAIN:C_MAIN + MAIN] = seq[16:16 + MAIN]
        else:
            xs[C_PRE:C_PRE + PRE] = seq[0:PRE]
            xs[C_HALO:C_HALO + HALO] = seq[PRE - HALO:PRE]
            xs[C_MAIN:C_MAIN + MAIN] = seq[PRE:PRE + MAIN]
        xs[C_SMP:C_SMP + SMP] = x_sample[c]
        cvc = cv.copy()
        cvc[:, CV_FLAG] = float(p)
        cvc[:, CV_H0:CV_H0 + 4] = fm4(st_h[c])
        m = dict(common)
        m["xT"] = np.ascontiguousarray(xs.T)
        m["cvec"] = cvc
        m["xrst"] = np.ascontiguousarray(st_conv[c].T.reshape(4, 128, 3).transpose(1, 0, 2)).reshape(128, 12)
        m["dwst"] = np.ascontiguousarray(st_dw[c].T.reshape(4, 128, 30).transpose(1, 0, 2)).reshape(128, 120)
        maps.append(m)
    return maps


def kernel(**inputs):
    if "nc" not in _NC_CACHE:
        _NC_CACHE["nc"] = build_program()
    nc = _NC_CACHE["nc"]
    maps = _host_inputs(inputs)
    res = run_bass_kernel_spmd(nc, maps, core_ids=list(range(8)))
    outs = res.results
    f32 = np.float32
    y_prompt = np.zeros((4, SEQ, D), f32)
    y_sample = np.zeros((8, SMP, D), f32)
    conv_p = np.zeros((1, 4, 3, DR), f32)
    h_p = np.zeros((1, 4, DR), f32)
    dw_p = np.zeros((1, 4, 30, DR), f32)
    conv_s = np.zeros((1, 8, 3, DR), f32)
    h_s = np.zeros((1, 8, DR), f32)
    dw_s = np.zeros((1, 8, 30, DR), f32)
    for c in range(8):
        j, p = divmod(c, 2)
        yT = np.asarray(outs[c]["yT"], dtype=f32)
        y_prompt[j, p * MAIN:(p + 1) * MAIN] = yT[:, :MAIN].T
        y_sample[c] = yT[:, MAIN:].T
        st = np.asarray(outs[c]["st"], dtype=f32).reshape(128, 2, 4, 34)
        fm = lambda a: a.transpose(2, 1, 0).reshape(a.shape[2], 512)
        if p == 1:
            conv_p[0, j] = fm(st[:, 0, :, 0:3])
            h_p[0, j] = fm(st[:, 0, :, 3:4])[0]
            dw_p[0, j] = fm(st[:, 0, :, 4:34])
        conv_s[0, c] = fm(st[:, 1, :, 0:3])
        h_s[0, c] = fm(st[:, 1, :, 3:4])[0]
        dw_s[0, c] = fm(st[:, 1, :, 4:34])
    return (y_prompt, y_sample, conv_p, h_p, dw_p, conv_s, h_s, dw_s)
```

```python
import contextlib
import numpy as np
import concourse.bass as bass
import concourse.mybir as mybir
from concourse.bass_utils import run_bass_kernel_spmd

F32 = mybir.dt.float32
BF16 = mybir.dt.bfloat16
AF = mybir.ActivationFunctionType
ALU = mybir.AluOpType

D = 1024
DR = 512
DFF = 4096
NMETA = 16
SEQ = 4096
PRE = 2064
HALO = 48
MAIN = 2048
SMP = 64
C_PRE, C_HALO, C_MAIN, C_SMP = 0, PRE, PRE + HALO, PRE + HALO + MAIN
NTOK = PRE + HALO + MAIN + SMP
NOUT = MAIN + SMP
EPS = 1e-6

CV_GMIX, CV_GMLP, CV_GFIN = 0, 8, 16
CV_CB, CV_BR, CV_BI, CV_LAM, CV_DWB, CV_LNG, CV_LNB, CV_GR, CV_GC = 24, 28, 32, 36, 40, 44, 48, 52, 56
CV_RW, CV_DW = 60, 76
CV_FLAG, CV_H0, CV_EPS = 200, 201, 205
CV_C, CV_CH, CV_BRH, CV_BIH = 206, 210, 214, 218
CV_ONEP = 222
NV = 224


_SIM = {}
DEF_COST = {"pe": 2.0, "act": 0.6, "dve": 0.65, "pool": 0.9, "sp": 0.3}
HOP = 0.45
LIST_SCHED = True
PE_COLD = 1.0
FILL_WARM = True
FILL_LAST_SEG = 2
LNEXP_SQRT = True
PRENORM_BOOST = 1000.0
INPROJ_BOOST = 0.0
FILL_DUR = 0.25


def pe_c(nmm, n):
    return nmm * (0.045 + 0.205 * n / 512.0)


class _Rec:
    def __init__(self):
        self.calls = []

    def __getattr__(self, name):
        def f(*a, **k):
            self.calls.append((name, a, k))
            return self
        return f

    def then_inc(self, *a, **k):
        return self


def _fd(ap):
    try:
        return int(ap.free_size())
    except Exception:
        return 512


def _is_psum(ap):
    try:
        return "psum" in str(ap.space).lower() or "PSUM" in str(ap.space)
    except Exception:
        return False


_ASETS = {"Exp": frozenset([0, 6]), "Tanh": frozenset([0, 11, 18]), "Ln": frozenset([6]), "Sqrt": frozenset([3]),
          "Gelu_apprx_tanh": frozenset([11]), "Silu": frozenset([18])}
TBL_LOAD = 1.3
TBL_CHOICE = 1.0


def estimate_cost(eng, fn):
    r = _Rec()
    try:
        fn(r)
    except Exception:
        return None
    occ = 0.0
    lat_extra = 0.0
    aset = None
    for name, a, k in r.calls:
        out = k.get("out", a[0] if a else None)
        fd = _fd(out) if out is not None else 512
        if name == "matmul":
            rhs = k.get("rhs", a[2] if len(a) > 2 else None)
            nn = _fd(rhs) if rhs is not None else 512
            occ += max(0.055, 0.012 + 0.215 * nn / 512.0)
        elif name == "dma_start":
            try:
                nbytes = out.size() * (4 if out.dtype == F32 else 2)
            except Exception:
                nbytes = 1 << 20
            occ += 0.8 if eng == "pool" else 0.5
            lat_extra = 2.0 + nbytes / 200e3
        elif name == "activation":
            fname = str(k.get("func", "")).split(".")[-1]
            aset = _ASETS.get(fname)
            occ += 0.13 + 0.00082 * fd + (0.08 if not isinstance(k.get("scale", 1.0), (int, float)) else 0.0)
        elif eng == "dve":
            if name == "tensor_tensor_scan":
                occ += 0.2 + 0.0021 * fd
            elif name == "scalar_tensor_tensor":
                occ += 0.15 + 0.00105 * fd
            elif name == "tensor_tensor":
                occ += 0.1 + 0.001 * fd
            elif name == "tensor_scalar":
                occ += 0.12 + 0.00055 * fd
            elif name == "tensor_copy":
                src = k.get("in_", a[1] if len(a) > 1 else None)
                occ += 0.1 + (0.001 if (src is not None and _is_psum(src)) else 0.00065) * fd
            elif name == "reciprocal":
                occ += 0.1 + 0.0052 * fd
            else:
                occ += 0.06 + 0.0005 * fd
        elif eng == "pool":
            if name == "tensor_tensor":
                occ += 0.15 + 0.0017 * fd
            elif name == "tensor_scalar":
                occ += 0.45 + 0.0004 * fd
            elif name == "tensor_copy":
                occ += 0.12 + 0.0027 * fd
            else:
                occ += 0.06 + 0.0005 * fd
        else:
            occ += 0.3
    if occ == 0.0:
        return None
    return occ, (lat_extra if lat_extra else occ), aset


class VB:
    _n = 0

    def __init__(self, default_t):
        VB._n += 1
        self.id = VB._n
        self.key = f"VB#{self.id}"
        self.t = default_t

    def __getitem__(self, idx):
        return self.t[idx]

    def __getattr__(self, name):
        return getattr(self.t, name)


class Sched:
    def __init__(self, sem_names):
        self.sem_names = list(sem_names)
        self.ops = []
        self.events = []
        self.cnt = {k: 0 for k in sem_names}
        self.prog = {e: [] for e in ("pe", "act", "dve", "pool", "sp")}
        self.nbank = 0
        self.cur_boost = 0.0

    def op(self, eng, fn, reads=(), writes=(), sem=None, inc=None, c=None, lat=None, pin=False, boost=None):
        if sem is None:
            sem = eng
        if inc is None:
            inc = 16 if sem.startswith("d_") else 1
        if c is None:
            c = DEF_COST[eng]
        if lat is None:
            lat = 6.0 if sem.startswith("d_") else c
        self.ops.append(dict(eng=eng, fn=fn, reads=list(reads), writes=list(writes), sem=sem, inc=inc, c=c, lat=lat, pin=pin, boost=(self.cur_boost if boost is None else boost)))
        self.events.append(("op", len(self.ops) - 1))

    def fence(self, key):
        self.events.append(("fence", key))

    def settle(self, keys, sem):
        self.events.append(("settle", list(keys), sem))

    def finalize(self):
        ops = self.ops
        n = len(ops)
        for o in ops:
            est = estimate_cost(o["eng"], o["fn"])
            o["aset"] = None
            if est is not None:
                o["c"], o["lat"], o["aset"] = est
                if o["sem"].startswith("d_"):
                    o["lat"] = max(o["lat"], 2.0)
        if PE_COLD != 1.0:
            for ev in self.events:
                if ev[0] == "fence":
                    break
                if ev[0] == "op" and ops[ev[1]]["eng"] == "pe":
                    ops[ev[1]]["c"] *= PE_COLD
                    ops[ev[1]]["lat"] *= PE_COLD
        deps = [set() for _ in range(n)]
        last_w, readers, sem_ops = {}, {}, {}
        last_rec = {}
        pin_only = set()
        order = {e: [] for e in self.prog}
        finish = [0.0] * n
        eng_free = {e: 0.0 for e in self.prog}
        seg = []
        vb_alloc, vb_users = {}, {}
        for i, o in enumerate(ops):
            for k in set(o["reads"] + o["writes"]):
                if k.startswith("VB#"):
                    if k not in vb_alloc:
                        assert k in o["writes"], k
                        vb_alloc[k] = i
                        vb_users[k] = 0
                    else:
                        vb_users[k] += 1
        op_alloc = {i: k for k, i in vb_alloc.items()}
        phys = [dict(vb=None, ops=[], left=0) for _ in range(7)]
        vb_phys = {}
        seg_idx = [0]
        nfill = [0]
        self.fill_dep = None
        for i_, o_ in enumerate(ops):
            if "fill_src" in o_["writes"]:
                self.fill_dep = i_
        tbl = [None]
        nload = [0]

        def tbl_pen(o):
            a = o.get("aset")
            if a is None or tbl[0] is None:
                return 0.0 if a is None else TBL_LOAD
            return 0.0 if (a & tbl[0]) else TBL_LOAD

        def tbl_upd(o):
            a = o.get("aset")
            if a is None:
                return
            if tbl[0] is not None and (a & tbl[0]):
                tbl[0] = a & tbl[0]
            else:
                tbl[0] = a
                nload[0] += 1

        def sched_segment(seg):
            if not seg:
                return
            if not LIST_SCHED:
                for i in seg:
                    o = ops[i]
                    dr = max([finish[d] + HOP for d in deps[i]] + [0.0])
                    pen = tbl_pen(o) if o["eng"] == "act" else 0.0
                    st = max(eng_free[o["eng"]], dr) + pen
                    if o["eng"] == "act":
                        tbl_upd(o)
                    eng_free[o["eng"]] = st + o["c"]
                    finish[i] = st + o["lat"]
                    order[o["eng"]].append(i)
                return
            inseg = set(seg)
            succ = {i: [] for i in seg}
            indeg = {}
            for i in seg:
                k = 0
                for d in deps[i]:
                    if d in inseg:
                        succ[d].append(i)
                        k += 1
                indeg[i] = k
            prio = {}
            for i in reversed(seg):
                prio[i] = ops[i]["lat"] + max([prio[s_] + HOP for s_ in succ[i]] + [0.0])
            _SIM.setdefault("cp", []).append(round(max(prio.values()), 1))
            for i in seg:
                prio[i] += ops[i].get("boost", 0.0)
            dep_ready = {}
            ready = {e: [] for e in self.prog}

            def make_ready(i):
                dr = 0.0
                for d in deps[i]:
                    dr = max(dr, finish[d] + (HOP if ops[d]["eng"] != ops[i]["eng"] else 0.1))
                dep_ready[i] = dr
                ready[ops[i]["eng"]].append(i)
            for i in seg:
                if indeg[i] == 0:
                    make_ready(i)
            left = len(seg)
            while left:
                best = None
                for e, lst in ready.items():
                    if not lst:
                        continue
                    ef = eng_free[e]
                    for i in lst:
                        est = dep_ready[i] if dep_ready[i] > ef else ef
                        tp_ = 0.0
                        if e == "act":
                            tp_ = tbl_pen(ops[i])
                            est += tp_ * TBL_CHOICE
                        bsel = None
                        if i in op_alloc:
                            bt = None
                            for b, ph in enumerate(phys):
                                if ph["vb"] is None or ph["left"] == 0:
                                    t_ = max([finish[d] + HOP for d in ph["ops"]] + [0.0])
                                    if bt is None or t_ < bt:
                                        bt, bsel = t_, b
                            if bsel is None:
                                continue
                            if bt > est:
                                est = bt
                        key = (est, -prio[i], i)
                        if best is None or key < best[0]:
                            best = (key, e, i, bsel, est - tp_ * (TBL_CHOICE - 1.0))
                if best is None:
                    raise RuntimeError("scheduler: no PSUM bank available (deadlock)")
                key, e, i, bsel, st_real = best
                if bsel is not None:
                    ph = phys[bsel]
                    for d in ph["ops"]:
                        deps[i].add(d)
                    k_ = op_alloc[i]
                    self.vbs[k_].t = self.psb[bsel]
                    vb_phys[k_] = bsel
                    ph["vb"], ph["ops"], ph["left"] = k_, [i], vb_users[k_]
                for k_ in set(ops[i]["reads"] + ops[i]["writes"]):
                    if k_.startswith("VB#") and vb_alloc[k_] != i:
                        ph = phys[vb_phys[k_]]
                        ph["ops"].append(i)
                        ph["left"] -= 1
                ready[e].remove(i)
                o = ops[i]
                st = st_real
                if e == "pe" and FILL_WARM and seg_idx[0] <= FILL_LAST_SEG and self.fill_dep is not None and st - eng_free["pe"] > 2 * FILL_DUR and eng_free["pe"] > 0:
                    k_ = int((st - eng_free["pe"]) / FILL_DUR) - 1
                    t_ = eng_free["pe"]
                    for _ in range(k_):
                        ops.append(dict(eng="pe", fn=self.fill_fn, reads=[], writes=[], sem="pe", inc=1, c=FILL_DUR, lat=FILL_DUR, pin=False, aset=None))
                        deps.append(set([self.fill_dep]))
                        finish.append(t_ + FILL_DUR)
                        order["pe"].append(len(ops) - 1)
                        t_ += FILL_DUR
                        nfill[0] += 1
                if e == "act":
                    tbl_upd(o)
                eng_free[e] = st + o["c"]
                finish[i] = st + o["lat"]
                order[e].append(i)
                left -= 1
                for s_ in succ[i]:
                    indeg[s_] -= 1
                    if indeg[s_] == 0:
                        make_ready(s_)

        for ev in self.events:
            if ev[0] == "op":
                i = ev[1]
                o = ops[i]
                for k in o["reads"] + o["writes"]:
                    for d in last_w.get(k, ()):
                        deps[i].add(d)
                for k in o["writes"]:
                    for d in readers.get(k, ()):
                        deps[i].add(d)
                if o["pin"] and last_rec.get(o["eng"]) is not None:
                    if last_rec[o["eng"]] not in deps[i]:
                        pin_only.add((i, last_rec[o["eng"]]))
                    deps[i].add(last_rec[o["eng"]])
                last_rec[o["eng"]] = i
                deps[i].discard(i)
                for k in o["writes"]:
                    last_w[k] = [i]
                    readers[k] = []
                for k in o["reads"]:
                    readers.setdefault(k, []).append(i)
                sem_ops.setdefault(o["sem"], []).append(i)
                seg.append(i)
            elif ev[0] == "settle":
                for k in ev[1]:
                    last_w[k] = list(sem_ops.get(ev[2], []))
            elif ev[0] == "fence":
                sched_segment(seg)
                seg_idx[0] += 1
                _SIM.setdefault("segs", []).append((ev[1], max([finish[i] for i in seg] + [0.0]), {e: round(sum(ops[i]["c"] for i in seg if ops[i]["eng"] == e), 1) for e in self.prog}))
                seg = []
                last_w[ev[1]] = [order[e][-1] for e in ("pe", "act", "dve", "pool") if order[e]]
                readers[ev[1]] = []
        sched_segment(seg)
        self.sim_time = max(finish) if finish else 0.0
        _SIM["nload"] = nload[0]
        self.sim_finish = finish
        self.sim_order = order
        self.sim_deps = deps
        _SIM["S"] = self
        n = len(ops)
        tok = [None] * n
        cnt = {k: 0 for k in self.sem_names}
        _SIM["nfill"] = nfill[0]
        for e in self.prog:
            for i in order[e]:
                o = ops[i]
                cnt[o["sem"]] += o["inc"]
                tok[i] = (o["sem"], cnt[o["sem"]])
        self.cnt = cnt
        for e in self.prog:
            waited = {}
            for i in order[e]:
                o = ops[i]
                w = {}
                for d in deps[i]:
                    if (i, d) in pin_only:
                        continue
                    s_, v = tok[d]
                    if waited.get(s_, 0) < v:
                        w[s_] = max(w.get(s_, 0), v)
                for s_, v in w.items():
                    waited[s_] = v
                self.prog[e].append((list(w.items()), o["fn"], o["sem"], o["inc"]))

    def emit(self, block, sems, final_waits=()):
        self.finalize()
        _SIM["t"] = self.sim_time

        def run(engname, e):
            for (waits, fn, sem, inc) in self.prog[engname]:
                for (s, v) in waits:
                    e.wait_ge(sems[s], v)
                fn(e).then_inc(sems[sem], inc)

        @block.tensor
        def _(e):
            run("pe", e)

        @block.scalar
        def _(e):
            run("act", e)

        @block.vector
        def _(e):
            run("dve", e)

        @block.gpsimd
        def _(e):
            run("pool", e)

        @block.sync
        def _(e):
            run("sp", e)
            for s in final_waits:
                if self.cnt[s] > 0:
                    e.wait_ge(sems[s], self.cnt[s])


def build_program():
    nc = bass.Bass("TRN2", target_bir_lowering=False)
    dram = lambda name, shape, dt, kind="Internal": nc.dram_tensor(name, shape, dt, kind=kind).ap()
    xT = dram("xT", [D, NTOK], F32, "ExternalInput")
    cvec_d = dram("cvec", [128, NV], F32, "ExternalInput")
    ident_d = dram("ident", [128, 128], F32, "ExternalInput")
    wg_d = dram("wg", [128, 8 * 128], F32, "ExternalInput")
    xrst_d = dram("xrst", [128, 12], F32, "ExternalInput")
    dwst_d = dram("dwst", [128, 120], F32, "ExternalInput")
    wina_d = dram("w_in_a", [128, 8 * 512], F32, "ExternalInput")
    winb_d = dram("w_in_b", [128, 8 * 1536], F32, "ExternalInput")
    wout_d = dram("w_out", [128, 8 * 1024], F32, "ExternalInput")
    wup_d = dram("w_up", [8, 128, 8 * 512], F32, "ExternalInput")
    wdn_d = dram("w_dn", [8, 128, 32 * 128], F32, "ExternalInput")
    yT = dram("yT", [D, NOUT], F32, "ExternalOutput")
    st_d = dram("st", [128, 2 * 4 * 34], F32, "ExternalOutput")
    x1s = dram("x1s", [D, NOUT], F32)
    wup_b = dram("wup_b", [8, 128, 8 * 512], BF16)
    wdn_b = dram("wdn_b", [8, 128, 32 * 128], BF16)
    xT_v = xT.rearrange("(k p) t -> p k t", p=128)
    yT_v = yT.rearrange("(k p) t -> p k t", p=128)
    x1_v = x1s.rearrange("(k p) t -> p k t", p=128)

    with contextlib.ExitStack() as es:
        def sb(name, shape, dt):
            return es.enter_context(nc.sbuf_tensor(name, shape, dt))

        cvec = sb("cvec_s", [128, NV], F32)
        ident = sb("ident_s", [128, 128], F32)
        onesM = sb("onesM", [128, 128], BF16)
        onesC = sb("onesC", [128, 128], BF16)
        Dr = sb("Dr", [128, 16, 128], BF16)
        Ddw = sb("Ddw", [128, 124, 128], BF16)
        Wg = sb("Wg", [128, 8, 128], BF16)
        xrb = sb("xrb", [128, 4, 3 + 512], BF16)
        vb = sb("vb", [128, 4, 30 + 512], BF16)
        hcar = sb("hcar", [128, 4], F32)
        stt = sb("stt", [128, 2, 4, 34], F32)
        xrst = sb("xrst_s", [128, 4, 3], F32)
        dwst = sb("dwst_s", [128, 4, 30], F32)
        tiny = sb("tiny", [128, 8], F32)
        fill_src = sb("fill_src", [128, 512], BF16)
        xb = [sb(f"xb{i}", [128, 8, 512], F32) for i in range(2)]
        hn = sb("hn", [128, 8, 512], BF16)
        hn2 = sb("hn2", [128, 8, 512], BF16)
        rs = sb("rs", [128, 512], F32)
        R1_BYTES = 110592
        r1 = sb("r1", [128, R1_BYTES // 2], BF16)
        cur = [0]

        def carve(shape, dt, reset=None):
            if reset is not None:
                cur[0] = reset
            n = int(np.prod(shape))
            nb = n * (4 if dt == F32 else 2)
            off = cur[0]
            cur[0] += nb
            assert cur[0] <= R1_BYTES, cur[0]
            ap = r1[:, off // 2:(off + nb) // 2]
            if dt == F32:
                ap = ap.bitcast(F32)
            if len(shape) == 2:
                ap = ap.rearrange("p (a b) -> p a b", a=shape[0])
            elif len(shape) == 3:
                ap = ap.rearrange("p (a b c) -> p a b c", a=shape[0], b=shape[1])
            return ap

        def view(off, shape, dt):
            n = int(np.prod(shape))
            nb = n * (4 if dt == F32 else 2)
            assert off + nb <= R1_BYTES, (off, nb)
            ap = r1[:, off // 2:(off + nb) // 2]
            if dt == F32:
                ap = ap.bitcast(F32)
            if len(shape) == 2:
                ap = ap.rearrange("p (a b) -> p a b", a=shape[0])
            return ap

        w_in = view(0, [8, 2048], BF16)
        w_out = view(32768, [8, 1024], BF16)
        TP0 = 49152
        tp = [view(TP0 + 2048 * i, [512], F32) for i in range(20)]
        gg = view(TP0 + 2048 * 12, [4, 512], F32)
        vc = view(TP0 + 2048 * 16, [4, 512], F32)
        xcb = view(90112, [4, 512], BF16)
        ys = view(94208, [8, 512], BF16)
        mix = view(102400, [8, 512], BF16)
        NWU, NWD = 4, 3
        wu = [view(8192 * i, [8, 512], BF16) for i in range(NWU)]
        hT = view(32768, [32, 512], BF16)
        wd = [view(65536 + 8192 * i, [32, 128], BF16) for i in range(NWD)]
        t_relu = view(90112, [3, 512], F32)

        psb = [es.enter_context(nc.psum_tensor(f"ps{i}", [128, 512], F32)) for i in range(8)]

        sem_names = (["pe", "act", "dve", "pool", "sp", "d_c", "d_cast", "d_castb", "d_casto", "d_cast2", "d_x0", "d_x1", "d_s0", "d_s1", "d_o"]
                     + [f"d_wu{i}" for i in range(NWU)] + [f"d_wd{i}" for i in range(NWD)])
        sems = {n: es.enter_context(nc.semaphore(n)) for n in sem_names}
        block = es.enter_context(nc.Block())
        S = Sched(sem_names)
        S.psb = psb
        S.vbs = {}
        S.fill_fn = lambda e: e.matmul(psb[7][:, :], lhsT=onesM[:], rhs=fill_src[:], start=True, stop=True)

        def bank(pool="s"):
            vb = VB(psb[0])
            S.vbs[vb.key] = vb
            return vb, vb.key

        _aliased = set()

        def alias_once(buf, keys):
            if buf in _aliased:
                return []
            _aliased.add(buf)
            return list(keys)

        def cv(c, n=1):
            return cvec[:, c:c + n]

        def drain(gen):
            for _ in gen:
                pass

        def merge(primary, secondary, ratio=2):
            p_alive = s_alive = True
            while p_alive or s_alive:
                for _ in range(ratio):
                    if p_alive:
                        try:
                            next(primary)
                        except StopIteration:
                            p_alive = False
                if s_alive:
                    try:
                        next(secondary)
                    except StopIteration:
                        s_alive = False

        S.op("sp", lambda e: e.dma_start(out=cvec[:], in_=cvec_d), writes=["cvec"], sem="d_c", boost=5000.0)
        S.op("sp", lambda e: e.dma_start(out=ident[:], in_=ident_d), writes=["ident"], sem="d_c", boost=5000.0)
        S.op("sp", lambda e: e.dma_start(out=xrst[:].rearrange("p a b -> p (a b)"), in_=xrst_d), writes=["xrst"], sem="d_c", boost=5000.0)
        S.op("sp", lambda e: e.dma_start(out=dwst[:].rearrange("p a b -> p (a b)"), in_=dwst_d), writes=["dwst"], sem="d_c", boost=5000.0)
        S.settle(["cvec", "ident", "xrst", "dwst"], "d_c")
        S.op("pool", lambda e: e.dma_start(out=Wg[:].rearrange("p a b -> p (a b)"), in_=wg_d, max_dma_last_dim=4096), writes=["Wg"], sem="d_cast")
        S.op("pool", lambda e: e.dma_start(out=w_in[:, :, 0:512], in_=wina_d.rearrange("p (k c) -> p k c", k=8), max_dma_last_dim=4096),
             writes=["w_in_a"], sem="d_cast")
        S.settle(["Wg", "w_in_a"], "d_cast")
        def cast_wout():
            for h in range(4):
                S.op("pool", lambda e, h=h: e.dma_start(out=w_in[:, 2 * h:2 * h + 2, 512:2048],
                                                        in_=winb_d.rearrange("p (k c) -> p k c", k=8)[:, 2 * h:2 * h + 2, :], max_dma_last_dim=8192),
                     writes=[f"w_in_b.{h}"], sem="d_castb", pin=True, c=0.8)
            S.settle(["w_in_b"], "d_castb")
            for h in range(2):
                S.op("pool", lambda e, h=h: e.dma_start(out=w_out[:, 4 * h:4 * h + 4, :].rearrange("p a b -> p (a b)"),
                                                        in_=wout_d[:, h * 4096:(h + 1) * 4096], max_dma_last_dim=4096),
                     writes=[f"w_out.{h}"], sem="d_casto", pin=True, c=0.8)
            S.settle(["w_out"], "d_casto")

        d2d_list = [("u", g) for g in range(8)] + [("d", g) for g in range(8)]

        def cast_d2d(k):
            for _ in range(k):
                if not d2d_list:
                    return
                kind, g = d2d_list.pop(0)
                if kind == "u":
                    S.op("pool", lambda e, g=g: e.dma_start(out=wup_b[g], in_=wup_d[g], max_dma_last_dim=4096), sem="d_cast2", pin=True, c=0.8)
                else:
                    S.op("pool", lambda e, g=g: e.dma_start(out=wdn_b[g], in_=wdn_d[g], max_dma_last_dim=4096), sem="d_cast2", pin=True, c=0.8)

        ddw_next = [0]

        def build_ddw(k):
            return

        S.op("dve", lambda e: e.memset(onesM[:], 1.0 / D), writes=["onesM"])
        S.op("dve", lambda e: e.memset(fill_src[:], 0.5), writes=["fill_src"])
        S.op("dve", lambda e: e.memset(onesC[:], 1.0 / DR), writes=["onesC"])
        S.op("dve", lambda e: e.memset(xrb[:], 0.0), writes=[f"xrb.{j}" for j in range(4)])
        S.op("dve", lambda e: e.memset(vb[:], 0.0), writes=[f"vb.{j}" for j in range(4)])
        S.op("dve", lambda e: e.memset(hcar[:], 0.0), writes=["hcar"])
        S.op("dve", lambda e: e.memset(stt[:], 0.0), writes=["stt"])
        S.op("act", lambda e: e.activation(out=tiny[:, 0:4], in_=cv(CV_LAM, 4), func=AF.Exp, scale=-1.0), reads=["cvec"], writes=["tiny"])
        S.op("act", lambda e: e.activation(out=tiny[:, 4:8], in_=tiny[:, 0:4], func=AF.Ln, bias=1.0), reads=["tiny"], writes=["tiny2"])
        S.op("dve", lambda e: e.tensor_scalar(out=cv(CV_C, 4), in0=tiny[:, 4:8], scalar1=-8.0, scalar2=None, op0=ALU.mult), reads=["tiny2"], writes=["cvc"])
        S.op("dve", lambda e: e.tensor_scalar(out=cv(CV_CH, 4), in0=tiny[:, 4:8], scalar1=-4.0, scalar2=None, op0=ALU.mult), reads=["tiny2"], writes=["cvc"])
        S.op("dve", lambda e: e.tensor_scalar(out=cv(CV_BRH, 8), in0=cv(CV_BR, 8), scalar1=0.5, scalar2=None, op0=ALU.mult), reads=["cvec"], writes=["cvc"])
        S.op("dve", lambda e: e.tensor_tensor(
                 out=Dr[:, :, :],
                 in0=ident[:].unsqueeze(1).broadcast_to([128, 16, 128]),
                 in1=cvec[:, CV_RW:CV_RW + 16].unsqueeze(2).broadcast_to([128, 16, 128]),
                 op=ALU.mult),
             reads=["ident", "cvec"], writes=["Dr"])
        for j in range(4):
            S.op("dve", lambda e, j=j: e.scalar_tensor_tensor(
                     out=Ddw[:, j * 31:(j + 1) * 31, :],
                     in0=ident[:].unsqueeze(1).broadcast_to([128, 31, 128]), scalar=0.5,
                     in1=cvec[:, CV_DW + j * 31:CV_DW + (j + 1) * 31].unsqueeze(2).broadcast_to([128, 31, 128]),
                     op0=ALU.mult, op1=ALU.mult),
                 reads=["ident", "cvec"], writes=[f"Ddw.{j}"])

        def g_load(src_v, c0, n, slot, extra_reads=()):
            S.op("sp", lambda e: e.dma_start(out=xb[slot][:, :, :n], in_=src_v[:, :, c0:c0 + n]),
                 reads=list(extra_reads), writes=[f"xb{slot}"], sem=f"d_x{slot}", boost=PRENORM_BOOST)
            yield

        def g_rstd(ps, pk, n, out_t, out_k, boost=0.0):
            S.op("act", lambda e: e.activation(out=out_t[:, :n], in_=ps[:, :n], func=AF.Ln, bias=cv(CV_EPS)), reads=[pk, "cvec"], writes=[out_k], boost=boost)
            yield
            S.op("act", lambda e: e.activation(out=out_t[:, :n], in_=out_t[:, :n], func=AF.Exp, scale=-0.5), reads=[out_k], writes=[out_k], boost=boost)
            yield

        HBUF = [hn, hn2]
        HKEY = [[f"hn.{k}" for k in range(8)], [f"hn2.{k}" for k in range(8)]]
        HN = HKEY[0]

        def g_prenorm(slot, n, gcol, sqbuf, sqkeys, extra=(), hb=0):
            hn = HBUF[hb]
            if sqbuf is None:
                sqbuf, sqkeys = hn, HKEY[hb]
            x = xb[slot]
            xk = f"xb{slot}"
            S.op("act", lambda e: e.activation(out=sqbuf[:, :, :n], in_=x[:, :, :n], func=AF.Square), reads=[xk] + list(extra), writes=sqkeys, c=0.2 + 0.0008 * 8 * n, boost=PRENORM_BOOST)
            yield
            ps, pk = bank()

            def mm(e):
                for k in range(8):
                    ins = e.matmul(ps[:, :n], lhsT=onesM[:], rhs=sqbuf[:, k, :n], start=(k == 0), stop=(k == 7))
                return ins
            S.op("pe", mm, reads=sqkeys + ["onesM"], writes=[pk], c=pe_c(8, n), boost=PRENORM_BOOST)
            yield
            yield from g_rstd(ps, pk, n, rs, "rs", boost=PRENORM_BOOST)
            for k in range(8):
                S.op("dve", lambda e, k=k: e.scalar_tensor_tensor(out=hn[:, k, :n], in0=x[:, k, :n], scalar=cv(gcol + k), in1=rs[:, :n],
                                                                   op0=ALU.mult, op1=ALU.mult),
                     reads=[xk, "rs", "cvec"], writes=[HKEY[hb][k]], boost=PRENORM_BOOST)
                yield

        def inproj(m, n, pool="s", hb=0):
            hn = HBUF[hb]
            HN = HKEY[hb]
            ps, pk = bank(pool)

            def mm(e):
                for k in range(8):
                    ins = e.matmul(ps[:, :n], lhsT=w_in[:, k, m * 128:(m + 1) * 128], rhs=hn[:, k, :n], start=(k == 0), stop=(k == 7))
                return ins
            S.op("pe", mm, reads=HN + ["w_in_a" if m < 4 else "w_in_b"], writes=[pk], c=pe_c(8, n), boost=INPROJ_BOOST)
            return ps, pk

        def g_xr(n, seg, p_mode=False, hb=0):
            for j in range(4):
                ps, pk = inproj(j, n, hb=hb)
                yield
                if p_mode:
                    S.op("dve", lambda e, j=j, ps=ps: e.tensor_copy(out=xrb[:, j, 3:3 + n], in_=ps[:, :n]), reads=[pk], writes=[f"xrb.{j}"])
                else:
                    S.op("act", lambda e, j=j, ps=ps: e.activation(out=xrb[:, j, 3:3 + n], in_=ps[:, :n], func=AF.Copy), reads=[pk], writes=[f"xrb.{j}"])
                yield
                if seg is not None:
                    S.op("act", lambda e, j=j, ps=ps: e.activation(out=stt[:, seg, j, 0:3], in_=ps[:, n - 3:n], func=AF.Copy), reads=[pk], writes=["stt"])
                    yield

        def g_xr_issue(n, store, hb=0):
            for j in range(4):
                store[j] = inproj(j, n, pool="l", hb=hb)
                yield

        def g_xr_evac(n, store):
            for j in range(4):
                ps, pk = store[j]
                S.op("dve", lambda e, j=j, ps=ps: e.tensor_copy(out=xrb[:, j, 3:3 + n], in_=ps[:, :n]), reads=[pk], writes=[f"xrb.{j}"])
                yield

        def g_glu(n, seg, hb=0):
            for j in range(4):
                pg, pgk = inproj(12 + j, n, hb=hb)
                yield
                tg = tp[10 + j % 2]
                tgk = f"tp{10 + j % 2}"
                S.op("act", lambda e, pg=pg, tg=tg: e.activation(out=tg[:, :n], in_=pg[:, :n], func=AF.Tanh, scale=0.5), reads=[pgk], writes=[tgk])
                yield
                pv, pvk = inproj(8 + j, n, hb=hb)
                yield
                S.op("dve", lambda e, pv=pv, tg=tg, j=j: e.scalar_tensor_tensor(out=vb[:, j, 30:30 + n], in0=tg[:, :n], scalar=1.0, in1=pv[:, :n],
                                                                               op0=ALU.add, op1=ALU.mult),
                     reads=[pvk, tgk], writes=[f"vb.{j}"])
                yield
                if seg is not None:
                    S.op("dve", lambda e, tg=tg: e.tensor_scalar(out=tg[:, n - 30:n], in0=tg[:, n - 30:n], scalar1=0.5, scalar2=0.5, op0=ALU.mult, op1=ALU.add),
                         reads=[tgk], writes=[tgk])
                    S.op("dve", lambda e, pv=pv, tg=tg, j=j: e.tensor_tensor(out=stt[:, seg, j, 4:34], in0=tg[:, n - 30:n], in1=pv[:, n - 30:n], op=ALU.mult),
                         reads=[pvk, tgk], writes=["stt"])
                    yield

        def g_gate(n, hb=0):
            for j in range(4):
                pgt, pgtk = inproj(4 + j, n, hb=hb)
                yield
                S.op("act", lambda e, pgt=pgt, j=j: e.activation(out=gg[:, j, :n], in_=pgt[:, :n], func=AF.Gelu_apprx_tanh), reads=[pgtk, "PA"], writes=[f"tp{12 + j}"])
                yield

        def g_gate_issue(n, store, hb=0):
            for j in range(4):
                store[j] = inproj(4 + j, n, pool="l", hb=hb)
                yield

        def gate_evac(n, store):
            for j in range(4):
                pgt, pgtk = store[j]
                S.op("act", lambda e, pgt=pgt, j=j: e.activation(out=gg[:, j, :n], in_=pgt[:, :n], func=AF.Gelu_apprx_tanh), reads=[pgtk, "PA"], writes=[f"tp{12 + j}"])

        def g_chains(js, sets, n, seg=None, want_y=False, after_gates=None, after_conv=None, p_mode=False):
            J = list(zip(js, sets))
            T = lambda s, i: tp[5 * s + i]
            K = lambda s, i: f"tp{5 * s + i}"
            bk = {}
            for j, s in J:
                ps, pk = bank()
                bk[j] = (ps, pk)

                def mmc(e, j=j, ps=ps):
                    for k in range(4):
                        ins = e.matmul(ps[:, :n], lhsT=Dr[:, j * 4 + k, :], rhs=xrb[:, j, k:k + n], start=(k == 0), stop=(k == 3))
                    return ins
                S.op("pe", mmc, reads=[f"xrb.{j}", "Dr"], writes=[pk], c=pe_c(4, n))
                yield
            if after_conv is not None:
                after_conv()
            for j, s in J:
                ps, pk = bk[j]
                S.op("act", lambda e, j=j, s=s, ps=ps: e.activation(out=T(s, 0)[:, :n], in_=ps[:, :n], func=AF.Identity, bias=cv(CV_CB + j)),
                     reads=[pk, "cvec"], writes=[K(s, 0)])
                yield
                S.op("pool", lambda e, j=j: e.tensor_copy(out=xrb[:, j, 0:3], in_=xrb[:, j, n:n + 3]), reads=[f"xrb.{j}"], writes=[f"xrb.{j}"])
                yield
            for j, s in J:
                S.op("dve", lambda e, s=s: e.tensor_copy(out=xcb[:, s, :n], in_=T(s, 0)[:, :n]), reads=[K(s, 0)], writes=[f"xcb{s}"])
                yield
            gb = {}
            for j, s in J:
                pr, prk = bank()
                S.op("pe", lambda e, j=j, s=s, pr=pr: e.matmul(pr[:, :n], lhsT=Wg[:, j, :], rhs=xcb[:, s, :n], start=True, stop=True), reads=[f"xcb{s}", "Wg"], writes=[prk], c=pe_c(1, n))
                yield
                pi, pik = bank()
                S.op("pe", lambda e, j=j, s=s, pi=pi: e.matmul(pi[:, :n], lhsT=Wg[:, 4 + j, :], rhs=xcb[:, s, :n], start=True, stop=True), reads=[f"xcb{s}", "Wg"], writes=[pik], c=pe_c(1, n))
                yield
                gb[j] = (pr, prk, pi, pik)
                S.op("act", lambda e, j=j, s=s, pr=pr: e.activation(out=T(s, 1)[:, :n], in_=pr[:, :n], func=AF.Tanh, scale=0.5, bias=cv(CV_BRH + j)),
                     reads=[prk, "cvc"], writes=[K(s, 1)])
                yield
                S.op("act", lambda e, j=j, s=s, pi=pi: e.activation(out=T(s, 2)[:, :n], in_=pi[:, :n], func=AF.Tanh, scale=0.5, bias=cv(CV_BIH + j)),
                     reads=[pik, "cvc"], writes=[K(s, 2)])
                yield
            if after_gates is not None:
                after_gates()
            for j, s in J:
                if not p_mode:
                    S.op("act", lambda e, j=j, s=s: e.activation(out=T(s, 3)[:, :n], in_=T(s, 1)[:, :n], func=AF.Exp, scale=cv(CV_C + j), bias=cv(CV_C + j)),
                         reads=[K(s, 1), "cvc"], writes=[K(s, 3)])
                    yield
                S.op("act", lambda e, j=j, s=s: e.activation(out=T(s, 1)[:, :n], in_=T(s, 1)[:, :n], func=AF.Exp, scale=cv(CV_CH + j), bias=cv(CV_CH + j)),
                     reads=[K(s, 1), "cvc"], writes=[K(s, 1)])
                yield
                if p_mode:
                    S.op("dve", lambda e, s=s: e.tensor_tensor(out=T(s, 3)[:, :n], in0=T(s, 1)[:, :n], in1=T(s, 1)[:, :n], op=ALU.mult),
                         reads=[K(s, 1)], writes=[K(s, 3)])
                    yield
            for j, s in J:
                S.op("dve", lambda e, s=s: e.scalar_tensor_tensor(out=T(s, 2)[:, :n], in0=T(s, 2)[:, :n], scalar=1.0, in1=T(s, 0)[:, :n], op0=ALU.add, op1=ALU.mult),
                     reads=[K(s, 2), K(s, 0)], writes=[K(s, 2)])
                yield
            for j, s in J:
                if LNEXP_SQRT:
                    S.op("act", lambda e, s=s: e.activation(out=T(s, 3)[:, :n], in_=T(s, 3)[:, :n], func=AF.Ln, scale=-1.0, bias=1.0), reads=[K(s, 3)], writes=[K(s, 3)])
                    yield
                    S.op("act", lambda e, s=s: e.activation(out=T(s, 3)[:, :n], in_=T(s, 3)[:, :n], func=AF.Exp, scale=0.5), reads=[K(s, 3)], writes=[K(s, 3)])
                else:
                    S.op("act", lambda e, s=s: e.activation(out=T(s, 3)[:, :n], in_=T(s, 3)[:, :n], func=AF.Sqrt, scale=-1.0, bias=1.0), reads=[K(s, 3)], writes=[K(s, 3)])
                yield
            for j, s in J:
                S.op("dve", lambda e, s=s: e.scalar_tensor_tensor(out=T(s, 2)[:, :n], in0=T(s, 2)[:, :n], scalar=0.5, in1=T(s, 3)[:, :n], op0=ALU.mult, op1=ALU.mult),
                     reads=[K(s, 2), K(s, 3)], writes=[K(s, 2)])
                yield
                S.op("dve", lambda e, j=j, s=s: e.tensor_tensor_scan(out=T(s, 4)[:, :n], data0=T(s, 1)[:, :n], data1=T(s, 2)[:, :n], initial=hcar[:, j:j + 1],
                                                                     op0=ALU.mult, op1=ALU.add),
                     reads=[K(s, 1), K(s, 2), "hcar"], writes=[K(s, 4)], c=1.3)
                yield
                S.op("dve", lambda e, j=j, s=s: e.tensor_copy(out=hcar[:, j:j + 1], in_=T(s, 4)[:, n - 1:n]), reads=[K(s, 4)], writes=["hcar"])
                yield
                if seg is not None:
                    S.op("pool", lambda e, j=j, s=s: e.tensor_copy(out=stt[:, seg, j, 3:4], in_=T(s, 4)[:, n - 1:n]), reads=[K(s, 4)], writes=["stt"])
                if want_y:
                    S.op("dve", lambda e, j=j, s=s: e.tensor_tensor(out=gg[:, j, :n], in0=gg[:, j, :n], in1=T(s, 4)[:, :n], op=ALU.mult),
                         reads=[f"tp{12 + j}", K(s, 4)], writes=[f"tp{12 + j}"])
                    yield

        pre_chunks = [(0, 512), (512, 512), (1024, 512), (1536, 512), (2048, 16)]
        xslot = [0]

        def nslot():
            s = xslot[0]
            xslot[0] ^= 1
            return s

        def g_PHa(ci):
            c0, n = pre_chunks[ci]
            slot = nslot()
            yield from g_load(xT_v, C_PRE + c0, n, slot)
            yield from g_prenorm(slot, n, CV_GMIX, None, None, hb=ci % 2)

        xr_store = {}
        drain(g_PHa(0))
        drain(g_xr_issue(pre_chunks[0][1], xr_store, hb=0))
        for ci, (c0, n) in enumerate(pre_chunks):
            drain(g_xr_evac(n, xr_store))
            if ci == 0:
                cast_wout()
            if ci == len(pre_chunks) - 1:
                S.op("dve", lambda e: e.tensor_scalar(out=hcar[:], in0=hcar[:], scalar1=cv(CV_FLAG), scalar2=None, op0=ALU.mult),
                     reads=["hcar", "cvec"], writes=["hcar"])
            ch = g_chains([0, 1, 2, 3], [0, 1, 2, 3], n, p_mode=True)
            if ci + 1 < len(pre_chunks):
                xr_store = {}

                def sec(ci=ci, st=xr_store):
                    yield from g_PHa(ci + 1)
                    yield from g_xr_issue(pre_chunks[ci + 1][1], st, hb=(ci + 1) % 2)
                merge(ch, sec(), ratio=3)
            else:
                drain(ch)
            cast_d2d(2)
            build_ddw(31)

        B_chunks = [(0, 424), (424, 424), (848, 424), (1272, 420), (1692, 420)]
        B_slot = {}
        wu_i = [0]
        wd_i = [0]
        HT = [f"hT.{f}" for f in range(32)]

        def BHB(ci):
            return (ci + len(A_chunks)) % 2

        def g_BH(ci, first=False):
            oc0, n = B_chunks[ci]
            slot = nslot()
            B_slot[ci] = slot
            yield from g_load(x1_v, oc0, n, slot, extra_reads=["x1s"])
            if first:
                yield from g_prenorm(slot, n, CV_GMLP, None, None, hb=BHB(ci))
            else:
                yield from g_prenorm(slot, n, CV_GMLP, mix, ["mixB"], extra=["R1"], hb=BHB(ci))

        def load_wu(g):
            ws = wu_i[0] % NWU
            wu_i[0] += 1
            S.op("sp", lambda e: e.dma_start(out=wu[ws][:].rearrange("p a b -> p (a b)"), in_=wup_b[g]),
                 reads=["wup_b"], writes=[f"wu{ws}"] + alias_once(f"wu{ws}", ["w_in_a", "w_in_b"]), sem=f"d_wu{ws}")
            return ws

        def load_wd(m):
            ws = wd_i[0] % NWD
            wd_i[0] += 1
            S.op("sp", lambda e: e.dma_start(out=wd[ws][:].rearrange("p a b -> p (a b)"), in_=wdn_b[m]),
                 reads=["wdn_b", "R1"], writes=[f"wd{ws}"], sem=f"d_wd{ws}")
            return ws

        def _one_up(ci, g, ws):
            oc0, n = B_chunks[ci]
            for mi in range(4):
                pu, puk = bank("all")

                hnb = HBUF[BHB(ci)]

                def mmu(e, mi=mi, pu=pu, hnb=hnb):
                    for k in range(8):
                        ins = e.matmul(pu[:, :n], lhsT=wu[ws][:, k, mi * 128:(mi + 1) * 128], rhs=hnb[:, k, :n], start=(k == 0), stop=(k == 7))
                    return ins
                S.op("pe", mmu, reads=HKEY[BHB(ci)] + [f"wu{ws}"], writes=[puk], c=pe_c(8, n))
                f = g * 4 + mi
                tb = f % 3
                S.op("act", lambda e, pu=pu, tb=tb: e.activation(out=t_relu[:, tb, :n], in_=pu[:, :n], func=AF.Relu), reads=[puk, "R1"], writes=[f"t_relu{tb}"])
                eng = "dve" if f % 4 != 3 else "pool"
                S.op(eng, lambda e, f=f, tb=tb: e.tensor_tensor(out=hT[:, f, :n], in0=t_relu[:, tb, :n], in1=t_relu[:, tb, :n], op=ALU.mult),
                     reads=[f"t_relu{tb}", "R1"], writes=[f"hT.{f}"])
                yield

        def _one_down(ci, m, ws):
            oc0, n = B_chunks[ci]
            slot = B_slot[ci]
            x = xb[slot]
            xk = f"xb{slot}"
            pd, pdk = bank("all")

            def mmd2(e):
                for k in range(32):
                    ins = e.matmul(pd[:, :n], lhsT=wd[ws][:, k, :], rhs=hT[:, k, :n], start=(k == 0), stop=(k == 31))
                return ins
            S.op("pe", mmd2, reads=HT + [f"wd{ws}"], writes=[pdk], c=pe_c(32, n))
            S.op("dve", lambda e: e.tensor_tensor(out=x[:, m, :n], in0=pd[:, :n], in1=x[:, m, :n], op=ALU.add),
                 reads=[pdk, xk], writes=[xk])
            yield

        def g_Bfinal(ci):
            oc0, n = B_chunks[ci]
            slot = B_slot[ci]
            x = xb[slot]
            xk = f"xb{slot}"
            S.op("act", lambda e: e.activation(out=mix[:, :, :n], in_=x[:, :, :n], func=AF.Square), reads=[xk, "R1"], writes=["mixB"], c=0.2 + 0.0008 * 8 * n)
            yield
            ps, pk = bank("all")

            def mmf(e):
                for k in range(8):
                    ins = e.matmul(ps[:, :n], lhsT=onesM[:], rhs=mix[:, k, :n], start=(k == 0), stop=(k == 7))
                return ins
            S.op("pe", mmf, reads=["mixB", "onesM"], writes=[pk], c=pe_c(8, n))
            yield
            yield from g_rstd(ps, pk, n, rs, "rs")
            for k in range(8):
                S.op("dve", lambda e, k=k: e.scalar_tensor_tensor(out=x[:, k, :n], in0=x[:, k, :n], scalar=cv(CV_GFIN + k), in1=rs[:, :n],
                                                                   op0=ALU.mult, op1=ALU.mult),
                     reads=[xk, "rs", "cvec"], writes=[f"{xk}.y{k // 4}"])
                yield
                if k % 4 == 3:
                    h = k // 4
                    S.op("sp", lambda e, h=h: e.dma_start(out=yT_v[:, 4 * h:4 * h + 4, oc0:oc0 + n], in_=x[:, 4 * h:4 * h + 4, :n]),
                         reads=[xk, f"{xk}.y{h}"], sem=f"d_s{slot}")
                    yield

        nB = len(B_chunks)
        wu_uses = [(c, g) for c in range(nB) for g in range(8)]
        wd_uses = [(c, m) for c in range(nB) for m in range(8)]
        wu_slot, wd_slot = {}, {}
        wu_ptr, wd_ptr = [0], [0]

        def ensure_wu(idx):
            while wu_ptr[0] <= min(idx, len(wu_uses) - 1):
                c, g = wu_uses[wu_ptr[0]]
                wu_slot[(c, g)] = load_wu(g)
                wu_ptr[0] += 1

        def ensure_wd(idx):
            while wd_ptr[0] <= min(idx, len(wd_uses) - 1):
                c, m = wd_uses[wd_ptr[0]]
                wd_slot[(c, m)] = load_wd(m)
                wd_ptr[0] += 1

        def step(gen, k):
            for _ in range(k):
                try:
                    next(gen)
                except StopIteration:
                    return

        A_chunks = [(C_SMP, SMP, "sample", 1, MAIN), (C_HALO, HALO, "halo", None, None)] \
            + [(C_MAIN + 512 * c, 512, "main", 0 if c == 3 else None, 512 * c) for c in range(4)]
        A_slot = {}

        def g_AH1a(ci):
            c0, n, mode, seg, oc0 = A_chunks[ci]
            if ci == 2:
                slot = A_slot[1]
            else:
                slot = nslot()
            A_slot[ci] = slot
            yield from g_load(xT_v, c0, n, slot)
            yield from g_prenorm(slot, n, CV_GMIX, None, None, hb=ci % 2)

        def g_AH1b(ci):
            c0, n, mode, seg, oc0 = A_chunks[ci]
            if mode == "sample":
                S.op("dve", lambda e: e.tensor_copy(out=tiny[:, 0:4], in_=hcar[:]), reads=["hcar"], writes=["tiny"])
                S.op("dve", lambda e: e.tensor_copy(out=xrb[:, :, 0:3], in_=xrst[:]), reads=["xrst"], writes=[f"xrb.{j}" for j in range(4)])
                S.op("dve", lambda e: e.tensor_scalar(out=vb[:, :, 0:30], in0=dwst[:], scalar1=2.0, scalar2=None, op0=ALU.mult), reads=["dwst"], writes=[f"vb.{j}" for j in range(4)])
                S.op("dve", lambda e: e.tensor_copy(out=hcar[:], in_=cv(CV_H0, 4)), reads=["cvec"], writes=["hcar"])
            yield from g_xr(n, seg, hb=ci % 2)
            yield from g_glu(n, seg, hb=ci % 2)

        def g_c31(j, n, store):
            pc, pck = bank("l")
            store[j] = (pc, pck)

            def mmd(e):
                for k in range(31):
                    ins = e.matmul(pc[:, :n], lhsT=Ddw[:, j * 31 + k, :], rhs=vb[:, j, k:k + n], start=(k == 0), stop=(k == 30))
                return ins
            S.op("pe", mmd, reads=[f"vb.{j}", f"Ddw.{j}"], writes=[pck], c=pe_c(31, n))
            S.op("pool", lambda e: e.tensor_copy(out=vb[:, j, 0:30], in_=vb[:, j, n:n + 30]), reads=[f"vb.{j}"], writes=[f"vb.{j}"])

        def A_M(ci, c31):
            c0, n, mode, seg, oc0 = A_chunks[ci]
            drain(g_chains([0, 1], [0, 1], n, seg=seg, want_y=True, after_conv=lambda: g_c31(0, n, c31), after_gates=lambda: g_c31(1, n, c31)))
            drain(g_chains([2, 3], [0, 1], n, seg=seg, want_y=True, after_conv=lambda: g_c31(2, n, c31), after_gates=lambda: g_c31(3, n, c31)))

        YS = [f"ys.{k}" for k in range(8)]

        def g_AT1(ci, c31):
            c0, n, mode, seg, oc0 = A_chunks[ci]
            mean_t, m2_t, rc_t, rr_t, rk_t = tp[0], tp[1], tp[2], tp[3], tp[4]
            for j in range(4):
                pc, pck = c31[j]
                S.op("act", lambda e, j=j, pc=pc: e.activation(out=vc[:, j, :n], in_=pc[:, :n], func=AF.Identity, bias=cv(CV_DWB + j)),
                     reads=[pck, "cvec", "PA"], writes=[f"tp{16 + j}"])
                yield
                S.op("act", lambda e, j=j, pc=pc: e.activation(out=ys[:, 4 + j, :n], in_=pc[:, :n], func=AF.Square, bias=cv(CV_DWB + j)),
                     reads=[pck, "cvec", "ys"], writes=[f"ys.{4 + j}"])
                yield
                S.op("dve", lambda e, j=j: e.tensor_copy(out=ys[:, j, :n], in_=vc[:, j, :n]), reads=[f"tp{16 + j}", "ys"], writes=[f"ys.{j}"])
                yield
            pm, pmk = bank()

            def mm_mean(e):
                for j in range(4):
                    ins = e.matmul(pm[:, :n], lhsT=onesC[:], rhs=ys[:, j, :n], start=(j == 0), stop=(j == 3))
                return ins
            S.op("pe", mm_mean, reads=YS[0:4] + ["onesC"], writes=[pmk], c=pe_c(4, n))
            yield
            pq, pqk = bank()

            def mm_msq(e):
                for j in range(4):
                    ins = e.matmul(pq[:, :n], lhsT=onesC[:], rhs=ys[:, 4 + j, :n], start=(j == 0), stop=(j == 3))
                return ins
            S.op("pe", mm_msq, reads=YS[4:8] + ["onesC"], writes=[pqk], c=pe_c(4, n))
            yield
            S.op("act", lambda e: e.activation(out=mean_t[:, :n], in_=pm[:, :n], func=AF.Copy), reads=[pmk], writes=["tp0"])
            yield
            S.op("pool", lambda e: e.tensor_tensor(out=m2_t[:, :n], in0=mean_t[:, :n], in1=mean_t[:, :n], op=ALU.mult), reads=["tp0"], writes=["tp1"])
            yield
            S.op("dve", lambda e: e.tensor_tensor(out=m2_t[:, :n], in0=pq[:, :n], in1=m2_t[:, :n], op=ALU.subtract), reads=[pqk, "tp1"], writes=["tp1"])
            yield
            S.op("dve", lambda e: e.tensor_scalar(out=m2_t[:, :n], in0=m2_t[:, :n], scalar1=0.0, scalar2=None, op0=ALU.max), reads=["tp1"], writes=["tp1"])
            yield
            S.op("act", lambda e: e.activation(out=rc_t[:, :n], in_=m2_t[:, :n], func=AF.Ln, bias=cv(CV_EPS)), reads=["tp1", "cvec"], writes=["tp2"])
            yield
            S.op("act", lambda e: e.activation(out=rc_t[:, :n], in_=rc_t[:, :n], func=AF.Exp, scale=-0.5), reads=["tp2"], writes=["tp2"])
            yield
            for j in range(4):
                S.op("pool", lambda e, j=j: e.tensor_tensor(out=vc[:, j, :n], in0=vc[:, j, :n], in1=mean_t[:, :n], op=ALU.subtract),
                     reads=[f"tp{16 + j}", "tp0"], writes=[f"tp{16 + j}"])
                yield
                S.op("dve", lambda e, j=j: e.tensor_tensor(out=vc[:, j, :n], in0=vc[:, j, :n], in1=rc_t[:, :n], op=ALU.mult),
                     reads=[f"tp{16 + j}", "tp2"], writes=[f"tp{16 + j}"])
                yield
            for j in range(4):
                S.op("act", lambda e, j=j: e.activation(out=vc[:, j, :n], in_=vc[:, j, :n], func=AF.Silu, scale=cv(CV_LNG + j), bias=cv(CV_LNB + j)),
                     reads=[f"tp{16 + j}", "cvec"], writes=[f"tp{16 + j}"])
                yield
            S.op("act", lambda e: e.activation(out=ys[:, 0:4, :n], in_=gg[:, :, :n], func=AF.Square), reads=[f"tp{12 + j}" for j in range(4)] + ["ys"], writes=YS[0:4], c=0.2 + 0.0008 * 4 * n)
            yield
            S.op("act", lambda e: e.activation(out=ys[:, 4:8, :n], in_=vc[:, :, :n], func=AF.Square), reads=[f"tp{16 + j}" for j in range(4)] + ["ys"], writes=YS[4:8], c=0.2 + 0.0008 * 4 * n)
            yield
            pr_, prk_ = bank()

            def mm_sr(e):
                for j in range(4):
                    ins = e.matmul(pr_[:, :n], lhsT=onesC[:], rhs=ys[:, j, :n], start=(j == 0), stop=(j == 3))
                return ins
            S.op("pe", mm_sr, reads=YS[0:4] + ["onesC"], writes=[prk_], c=pe_c(4, n))
            yield
            pc_, pck_ = bank()

            def mm_sc(e):
                for j in range(4):
                    ins = e.matmul(pc_[:, :n], lhsT=onesC[:], rhs=ys[:, 4 + j, :n], start=(j == 0), stop=(j == 3))
                return ins
            S.op("pe", mm_sc, reads=YS[4:8] + ["onesC"], writes=[pck_], c=pe_c(4, n))
            yield
            S.op("act", lambda e: e.activation(out=rr_t[:, :n], in_=pr_[:, :n], func=AF.Ln, bias=cv(CV_EPS)), reads=[prk_, "cvec"], writes=["tp3"])
            yield
            S.op("act", lambda e: e.activation(out=rk_t[:, :n], in_=pc_[:, :n], func=AF.Ln, bias=cv(CV_EPS)), reads=[pck_, "cvec"], writes=["tp4"])
            yield
            S.op("act", lambda e: e.activation(out=rr_t[:, :n], in_=rr_t[:, :n], func=AF.Exp, scale=-0.5), reads=["tp3"], writes=["tp3"])
            yield
            S.op("act", lambda e: e.activation(out=rk_t[:, :n], in_=rk_t[:, :n], func=AF.Exp, scale=-0.5), reads=["tp4"], writes=["tp4"])
            yield
            for j in range(4):
                S.op("dve", lambda e, j=j: e.scalar_tensor_tensor(out=mix[:, j, :n], in0=gg[:, j, :n], scalar=cv(CV_GR + j), in1=rr_t[:, :n], op0=ALU.mult, op1=ALU.mult),
                     reads=[f"tp{12 + j}", "tp3", "cvec"], writes=["mix"])
                yield
            for j in range(4):
                S.op("dve", lambda e, j=j: e.scalar_tensor_tensor(out=mix[:, 4 + j, :n], in0=vc[:, j, :n], scalar=cv(CV_GC + j), in1=rk_t[:, :n], op0=ALU.mult, op1=ALU.mult),
                     reads=[f"tp{16 + j}", "tp4", "cvec"], writes=["mix"])
                yield

        def g_AT2(ci):
            c0, n, mode, seg, oc0 = A_chunks[ci]
            slot = A_slot[ci]
            x = xb[slot]
            xk = f"xb{slot}"
            for m in range(8):
                po, pok = bank()

                def mmo(e, m=m, po=po):
                    for k in range(8):
                        ins = e.matmul(po[:, :n], lhsT=w_out[:, k, m * 128:(m + 1) * 128], rhs=mix[:, k, :n], start=(k == 0), stop=(k == 7))
                    return ins
                S.op("pe", mmo, reads=["mix", "w_out"], writes=[pok], c=pe_c(8, n))
                yield
                S.op("dve", lambda e, m=m, po=po: e.tensor_tensor(out=x[:, m, :n], in0=po[:, :n], in1=x[:, m, :n], op=ALU.add),
                     reads=[pok, xk], writes=[xk])
                yield
            S.op("sp", lambda e: e.dma_start(out=x1_v[:, :, oc0:oc0 + n], in_=x[:, :, :n]), reads=[xk], writes=["x1s"], sem=f"d_s{slot}")
            yield

        def halo_tails(n):
            for j in range(4):
                S.op("pool", lambda e, j=j: e.tensor_copy(out=xrb[:, j, 0:3], in_=xrb[:, j, n:n + 3]), reads=[f"xrb.{j}"], writes=[f"xrb.{j}"])
                S.op("pool", lambda e, j=j: e.tensor_copy(out=vb[:, j, 0:30], in_=vb[:, j, n:n + 30]), reads=[f"vb.{j}"], writes=[f"vb.{j}"])

        nA = len(A_chunks)
        drain(g_AH1a(0))
        drain(g_AH1b(0))
        drain(g_gate(A_chunks[0][1], hb=0))
        for ci in [0, 2, 3, 4, 5]:
            n = A_chunks[ci][1]
            c31 = {}
            A_M(ci, c31)
            if ci == 0:
                S.op("dve", lambda e: e.tensor_copy(out=hcar[:], in_=tiny[:, 0:4]), reads=["tiny", "hcar"], writes=["hcar"])
            cast_d2d(2)
            t1 = g_AT1(ci, c31)
            nxt = 2 if ci == 0 else ci + 1
            if nxt < nA:
                gstore = {}

                def head(ci=ci, nxt=nxt, gstore=gstore):
                    if ci == 0:
                        yield from g_AH1a(1)
                        yield from g_AH1b(1)
                        halo_tails(HALO)
                        yield
                    yield from g_AH1a(nxt)
                    yield from g_AH1b(nxt)
                    yield from g_gate_issue(A_chunks[nxt][1], gstore, hb=nxt % 2)
                merge(t1, head(), ratio=1)
                gate_evac(A_chunks[nxt][1], gstore)
                if nxt == nA - 1:
                    cast_d2d(16)
                    S.settle(["wup_b", "wdn_b"], "d_cast2")
                    ensure_wu(NWU - 1)
                drain(g_AT2(ci))
            else:
                b0 = g_BH(0, first=True)
                merge(t1, b0, ratio=2)
                drain(g_AT2(ci))
        S.op("sp", lambda e: e.dma_start(out=st_d, in_=stt[:].rearrange("p a b c -> p (a b c)")), reads=["stt"], sem="d_o")
        cast_d2d(16)
        S.settle(["wup_b", "wdn_b"], "d_cast2")

        S.fence("R1")
        for ci in range(nB):
            oc0, n = B_chunks[ci]
            ensure_wd(8 * ci + NWD - 1)
            nxt = g_BH(ci + 1) if ci + 1 < nB else iter(())
            for g in range(8):
                ensure_wu(8 * ci + g + NWU - 1)
                drain(_one_up(ci, g, wu_slot[(ci, g)]))
                if g == 1:
                    step(nxt, 2)
                if g == 4:
                    step(nxt, 3)
            drain(nxt)
            for m in range(8):
                ensure_wd(8 * ci + m + NWD - 1)
                if m == 2:
                    ensure_wu(8 * (ci + 1) + NWU - 1)
                drain(_one_down(ci, m, wd_slot[(ci, m)]))
            drain(g_Bfinal(ci))

        S.emit(block, sems, final_waits=["d_s0", "d_s1", "d_o"])
    return nc


_NC_CACHE = {}


def _host_inputs(inp):
    f32 = np.float32
    g = lambda k: np.asarray(inp[k], dtype=f32)
    x_prompt, x_sample, meta = g("x_prompt"), g("x_sample"), g("meta_tokens")
    fm8 = lambda v: np.ascontiguousarray(v.reshape(8, 128).T)
    fm4 = lambda v: np.ascontiguousarray(v.reshape(4, 128).T)
    cv = np.zeros((128, NV), f32)
    cv[:, CV_GMIX:CV_GMIX + 8] = fm8(g("norm_mix")[0])
    cv[:, CV_GMLP:CV_GMLP + 8] = fm8(g("norm_mlp")[0])
    cv[:, CV_GFIN:CV_GFIN + 8] = fm8(g("norm_final"))
    for col, key in ((CV_CB, "rnn_conv_b"), (CV_BR, "b_gate_r"), (CV_BI, "b_gate_i"), (CV_LAM, "rglru_lambda"), (CV_DWB, "dw_b"),
                     (CV_LNG, "ln_conv_g"), (CV_LNB, "ln_conv_b"), (CV_GR, "out_norm_rnn"), (CV_GC, "out_norm_conv")):
        cv[:, col:col + 4] = fm4(g(key)[0])
    rw = g("rnn_conv_w")[0]
    dw = g("dw_w")[0]
    for j in range(4):
        cv[:, CV_RW + 4 * j:CV_RW + 4 * j + 4] = rw[:, j * 128:(j + 1) * 128].T
        cv[:, CV_DW + 31 * j:CV_DW + 31 * j + 31] = dw[:, j * 128:(j + 1) * 128].T
    cv[:, CV_EPS] = EPS
    cv[:, CV_ONEP] = np.float32(1.0) + np.float32(1.1920929e-07)
    ident = np.eye(128, dtype=f32)
    wg = np.zeros((128, 8, 128), f32)
    for gi, key in enumerate(("w_gate_r", "w_gate_i")):
        w = g(key)[0]
        for j in range(4):
            for a in range(2):
                wg[64 * a:64 * a + 64, gi * 4 + j, 64 * a:64 * a + 64] = w[2 * j + a]
    wg = wg.reshape(128, 8 * 128)
    blk = lambda w, K, N: np.ascontiguousarray(w.reshape(K // 128, 128, N).transpose(1, 0, 2))
    w_in3 = blk(g("w_in")[0], D, 2048)
    w_in_a = np.ascontiguousarray(w_in3[:, :, 0:512]).reshape(128, -1)
    w_in_b = np.ascontiguousarray(w_in3[:, :, 512:2048]).reshape(128, -1)
    w_out = blk(g("w_out")[0], D, D).reshape(128, -1)
    wu = blk(g("w_up")[0], D, DFF)
    w_up = np.ascontiguousarray(wu.reshape(128, 8, 8, 512).transpose(2, 0, 1, 3)).reshape(8, 128, 8 * 512)
    wd = blk(g("w_down")[0], DFF, D)
    w_dn = np.ascontiguousarray(wd.reshape(128, 32, 8, 128).transpose(2, 0, 1, 3)).reshape(8, 128, 32 * 128)
    common = dict(ident=ident, wg=wg, w_in_a=w_in_a, w_in_b=w_in_b, w_out=w_out, w_up=w_up, w_dn=w_dn)
    st_conv, st_h, st_dw = g("state_rglru_conv")[0], g("state_rglru_h")[0], g("state_dwconv")[0]
    maps = []
    for c in range(8):
        j, p = divmod(c, 2)
        seq = np.concatenate([meta, x_prompt[j]], axis=0)
        xs = np.zeros((NTOK, D), f32)
        if p == 0:
            xs[C_PRE + 2048:C_PRE + 2064] = seq[0:16]
            xs[C_HALO + 32:C_HALO + 48] = seq[0:16]
            xs[C_MAIN:C_MAIN + MAIN] = seq[16:16 + MAIN]
        else:
            xs[C_PRE:C_PRE + PRE] = seq[0:PRE]
            xs[C_HALO:C_HALO + HALO] = seq[PRE - HALO:PRE]
            xs[C_MAIN:C_MAIN + MAIN] = seq[PRE:PRE + MAIN]
        xs[C_SMP:C_SMP + SMP] = x_sample[c]
        cvc = cv.copy()
        cvc[:, CV_FLAG] = float(p)
        cvc[:, CV_H0:CV_H0 + 4] = fm4(st_h[c])
        m = dict(common)
        m["xT"] = np.ascontiguousarray(xs.T)
        m["cvec"] = cvc
        m["xrst"] = np.ascontiguousarray(st_conv[c].T.reshape(4, 128, 3).transpose(1, 0, 2)).reshape(128, 12)
        m["dwst"] = np.ascontiguousarray(st_dw[c].T.reshape(4, 128, 30).transpose(1, 0, 2)).reshape(128, 120)
        maps.append(m)
    return maps


def kernel(**inputs):
    if "nc" not in _NC_CACHE:
        _NC_CACHE["nc"] = build_program()
    nc = _NC_CACHE["nc"]
    maps = _host_inputs(inputs)
    res = run_bass_kernel_spmd(nc, maps, core_ids=list(range(8)))
    outs = res.results
    f32 = np.float32
    y_prompt = np.zeros((4, SEQ, D), f32)
    y_sample = np.zeros((8, SMP, D), f32)
    conv_p = np.zeros((1, 4, 3, DR), f32)
    h_p = np.zeros((1, 4, DR), f32)
    dw_p = np.zeros((1, 4, 30, DR), f32)
    conv_s = np.zeros((1, 8, 3, DR), f32)
    h_s = np.zeros((1, 8, DR), f32)
    dw_s = np.zeros((1, 8, 30, DR), f32)
    for c in range(8):
        j, p = divmod(c, 2)
        yT = np.asarray(outs[c]["yT"], dtype=f32)
        y_prompt[j, p * MAIN:(p + 1) * MAIN] = yT[:, :MAIN].T
        y_sample[c] = yT[:, MAIN:].T
        st = np.asarray(outs[c]["st"], dtype=f32).reshape(128, 2, 4, 34)
        fm = lambda a: a.transpose(2, 1, 0).reshape(a.shape[2], 512)
        if p == 1:
            conv_p[0, j] = fm(st[:, 0, :, 0:3])
            h_p[0, j] = fm(st[:, 0, :, 3:4])[0]
            dw_p[0, j] = fm(st[:, 0, :, 4:34])
        conv_s[0, c] = fm(st[:, 1, :, 0:3])
        h_s[0, c] = fm(st[:, 1, :, 3:4])[0]
        dw_s[0, c] = fm(st[:, 1, :, 4:34])
    return (y_prompt, y_sample, conv_p, h_p, dw_p, conv_s, h_s, dw_s)
```
